# Optimizing a Trainium2 kernel written in Bass

```python
import jax, jax.numpy as jnp
from jax import lax
import numpy as np

D_MODEL = 1024
BATCH = 4
SEQ = 8192
DEPTH = 1

FOURIER_GROUPS = 4
FOURIER_WIDTH = D_MODEL // 2
FOURIER_GROUP_DIM = FOURIER_WIDTH // FOURIER_GROUPS
MLA_HEADS = 8
QK_NOPE_DIM = 64
QK_ROPE_DIM = 32
V_HEAD_DIM = (D_MODEL // 2) // MLA_HEADS
Q_LORA_RANK = 384
KV_LORA_RANK = 256
Q_BLOCK = 128
ROPE_THETA = 10000.0
D_FF = 2816
NORM_EPS = 1e-6
N_ADA = 9
W_IN_COLS = FOURIER_WIDTH + Q_LORA_RANK + KV_LORA_RANK + QK_ROPE_DIM + 2 * D_MODEL
SPLIT_1 = FOURIER_WIDTH
SPLIT_2 = SPLIT_1 + Q_LORA_RANK
SPLIT_3 = SPLIT_2 + KV_LORA_RANK
SPLIT_4 = SPLIT_3 + QK_ROPE_DIM

kernel_name = "hybrid_fourier_mla_macaron_adaln_encoder"


def rms_norm(x, g):
    x32 = x.astype(jnp.float32)
    y = x32 * lax.rsqrt(jnp.mean(x32 * x32, axis=-1, keepdims=True) + NORM_EPS)
    return (y * g.astype(jnp.float32)).astype(x.dtype)


def modulate(h, shift, scale):
    return h * (1 + scale[:, None, :]) + shift[:, None, :]


def swiglu(h, w_gate, w_up, w_down):
    return (jax.nn.silu(h @ w_gate) * (h @ w_up)) @ w_down


def rope_tables(positions):
    half = QK_ROPE_DIM // 2
    inv_freq = 1.0 / (ROPE_THETA ** (jnp.arange(half, dtype=jnp.float32) * 2.0 / QK_ROPE_DIM))
    ang = positions.astype(jnp.float32)[..., None] * inv_freq
    return jnp.cos(ang), jnp.sin(ang)


def apply_rope(x, cos, sin):
    x32 = x.astype(jnp.float32)
    x1, x2 = jnp.split(x32, 2, axis=-1)
    out = jnp.concatenate([x1 * cos - x2 * sin, x2 * cos + x1 * sin], axis=-1)
    return out.astype(x.dtype)


def fourier_mix(u):
    b, s, _ = u.shape
    ug = u.reshape(b, s, FOURIER_GROUPS, FOURIER_GROUP_DIM).astype(jnp.float32)
    f = jnp.fft.fft2(ug, axes=(1, 3), norm="ortho").real
    return f.reshape(b, s, FOURIER_WIDTH).astype(u.dtype)


def mla_attention(q_lat, kv_lat, k_rope, positions, q_norm, w_q_up, kv_norm, w_kv_up):
    b, s, _ = q_lat.shape
    q = (rms_norm(q_lat, q_norm) @ w_q_up).reshape(b, s, MLA_HEADS, QK_NOPE_DIM + QK_ROPE_DIM)
    kv = (rms_norm(kv_lat, kv_norm) @ w_kv_up).reshape(b, s, MLA_HEADS, QK_NOPE_DIM + V_HEAD_DIM)
    q_nope, q_rot = q[..., :QK_NOPE_DIM], q[..., QK_NOPE_DIM:]
    k_nope, v = kv[..., :QK_NOPE_DIM], kv[..., QK_NOPE_DIM:]
    cos, sin = rope_tables(positions)
    q_rot = apply_rope(q_rot, cos[:, :, None, :], sin[:, :, None, :])
    k_rot = apply_rope(k_rope, cos, sin)
    sm_scale = (QK_NOPE_DIM + QK_ROPE_DIM) ** -0.5
    q_nope = q_nope * sm_scale
    q_rot = q_rot * sm_scale
    n_blk = s // Q_BLOCK
    qn_b = q_nope.reshape(b, n_blk, Q_BLOCK, MLA_HEADS, QK_NOPE_DIM).transpose(1, 0, 2, 3, 4)
    qr_b = q_rot.reshape(b, n_blk, Q_BLOCK, MLA_HEADS, QK_ROPE_DIM).transpose(1, 0, 2, 3, 4)

    def attend(blk):
        qn_i, qr_i = blk
        scores = (jnp.einsum('bqhd,bkhd->bhqk', qn_i, k_nope)
                  + jnp.einsum('bqhr,bkr->bhqk', qr_i, k_rot))
        p = jax.nn.softmax(scores.astype(jnp.float32), axis=-1).astype(v.dtype)
        return jnp.einsum('bhqk,bkhd->bqhd', p, v)

    o = lax.map(attend, (qn_b, qr_b))
    return o.transpose(1, 0, 2, 3, 4).reshape(b, s, MLA_HEADS * V_HEAD_DIM)


def setup_inputs(seed: int = 0) -> dict:
    key = jax.random.key(seed)
    ks = jax.random.split(key, 24)
    f32 = jnp.float32

    def w(k, shape, fan_in, gain=1.0):
        return (jax.random.normal(k, shape, f32) * (gain * fan_in ** -0.5)).astype(f32)

    def gain(k, shape):
        return 1.0 + 0.02 * jax.random.normal(k, shape, f32)

    L, D = DEPTH, D_MODEL
    x = jax.random.normal(ks[0], (BATCH, SEQ, D), f32)
    c = jax.random.normal(ks[1], (BATCH, D), f32)
    offs = jax.random.randint(ks[2], (BATCH, 1), 0, 1024, dtype=jnp.int32)
    positions = (offs + jnp.arange(SEQ, dtype=jnp.int32)[None, :]).astype(jnp.int32)
    return {
        "x": x,
        "c": c,
        "positions": positions,
        "ada_w": w(ks[3], (L, D, N_ADA * D), D, 0.2),
        "ada_b": 0.02 * jax.random.normal(ks[4], (L, N_ADA * D), f32),
        "ffn1_norm": gain(ks[5], (L, D)),
        "ffn1_w_gate": w(ks[6], (L, D, D_FF), D),
        "ffn1_w_up": w(ks[7], (L, D, D_FF), D),
        "ffn1_w_down": w(ks[8], (L, D_FF, D), D_FF),
        "mix_norm": gain(ks[9], (L, D)),
        "w_in": w(ks[10], (L, D, W_IN_COLS), D),
        "q_norm": gain(ks[11], (L, Q_LORA_RANK)),
        "w_q_up": w(ks[12], (L, Q_LORA_RANK, MLA_HEADS * (QK_NOPE_DIM + QK_ROPE_DIM)), Q_LORA_RANK),
        "kv_norm": gain(ks[13], (L, KV_LORA_RANK)),
        "w_kv_up": w(ks[14], (L, KV_LORA_RANK, MLA_HEADS * (QK_NOPE_DIM + V_HEAD_DIM)), KV_LORA_RANK),
        "w_fourier_out": w(ks[15], (L, FOURIER_WIDTH, D), FOURIER_WIDTH),
        "w_mla_out": w(ks[16], (L, MLA_HEADS * V_HEAD_DIM, D), MLA_HEADS * V_HEAD_DIM),
        "w_out": w(ks[17], (L, D, D), D),
        "ffn2_norm": gain(ks[18], (L, D)),
        "ffn2_w_gate": w(ks[19], (L, D, D_FF), D),
        "ffn2_w_up": w(ks[20], (L, D, D_FF), D),
        "ffn2_w_down": w(ks[21], (L, D_FF, D), D_FF),
        "final_norm": gain(ks[22], (D,)),
    }


def reference(x, c, positions, ada_w, ada_b, ffn1_norm, ffn1_w_gate, ffn1_w_up, ffn1_w_down,
              mix_norm, w_in, q_norm, w_q_up, kv_norm, w_kv_up, w_fourier_out, w_mla_out, w_out,
              ffn2_norm, ffn2_w_gate, ffn2_w_up, ffn2_w_down, final_norm):
    c_act = jax.nn.silu(c)
    for l in range(DEPTH):
        mod = c_act @ ada_w[l] + ada_b[l]
        (sh1, sc1, g1, sh2, sc2, g2, sh3, sc3, g3) = jnp.split(mod, N_ADA, axis=-1)

        h = modulate(rms_norm(x, ffn1_norm[l]), sh1, sc1)
        x = x + 0.5 * g1[:, None, :] * swiglu(h, ffn1_w_gate[l], ffn1_w_up[l], ffn1_w_down[l])

        h = modulate(rms_norm(x, mix_norm[l]), sh2, sc2)
        z = h @ w_in[l]
        u_f, q_lat, kv_lat, k_rope, gate_logits = jnp.split(
            z, [SPLIT_1, SPLIT_2, SPLIT_3, SPLIT_4], axis=-1)
        y_a = fourier_mix(u_f) @ w_fourier_out[l]
        y_b = mla_attention(q_lat, kv_lat, k_rope, positions, q_norm[l], w_q_up[l],
                            kv_norm[l], w_kv_up[l]) @ w_mla_out[l]
        gate_a, gate_b = jnp.split(jax.nn.sigmoid(gate_logits), 2, axis=-1)
        y = (gate_a * y_a + gate_b * y_b) @ w_out[l]
        x = x + g2[:, None, :] * y

        h = modulate(rms_norm(x, ffn2_norm[l]), sh3, sc3)
        x = x + 0.5 * g3[:, None, :] * swiglu(h, ffn2_w_gate[l], ffn2_w_up[l], ffn2_w_down[l])
    return rms_norm(x, final_norm)
```

```python
import ml_dtypes
from concourse.bass_utils import run_bass_kernel_spmd
import numpy as np
import concourse.bass as bass
import concourse.mybir as mybir

F32 = mybir.dt.float32
BF16 = mybir.dt.bfloat16
I32 = mybir.dt.int32
U8 = mybir.dt.uint8
AF = mybir.ActivationFunctionType
ALU = mybir.AluOpType
AX = mybir.AxisListType
DTSIZE = {F32: 4, BF16: 2, I32: 4, U8: 1}


class Buf:
    __slots__ = ("name", "w", "rs", "rd")

    def __init__(self, name):
        self.name = name
        self.w = None
        self.rs = {}
        self.rd = []


class Op:
    __slots__ = ("eng", "fn", "seq", "signal", "is_dma", "dsem", "dval", "waits",
                 "cnt", "phase", "edeps", "ddeps")


class Prog:
    ENGS = ("pe", "act", "dve", "pool", "sp")
    KDMA = 12

    def __init__(self, nc):
        self.nc = nc
        self.q = {e: [] for e in self.ENGS}
        self.phase = 0
        self.esem = {}
        for e in ("pe", "act", "dve", "pool"):
            self.esem[e] = nc.alloc_semaphore("s_" + e)
        self.bar_sem = nc.alloc_semaphore("s_bar")
        self.bar_cnt = 0
        self.dsems = {}
        self.dcount = {}
        self.dma_ops = {}
        for e in ("sp", "act", "pool"):
            self.dsems[e] = [nc.alloc_semaphore("d_%s_%d" % (e, i)) for i in range(self.KDMA)]
            self.dcount[e] = 0
            self.dma_ops[e] = []
        self.waited = {}
        self.dwaited = {}
        self.bar_wait_pending = {e: 0 for e in self.ENGS}
        self.arena = None
        self.sb_off = 0
        self.sb_mark = 0
        self.sb_cap = 0
        self.bufs = {}

    def init_mem(self, sbuf_bytes=206 * 1024):
        nc = self.nc
        self.arena = nc.alloc_sbuf_tensor("arena", [128, sbuf_bytes], U8)
        self.sb_cap = sbuf_bytes
        self.psum = nc.alloc_psum_tensor("psum", [128, 4096], F32)

    def sb(self, shape, dtype, name=None):
        n = 1
        for s in shape[1:]:
            n *= s
        nbytes = n * DTSIZE[dtype]
        off = (self.sb_off + 63) // 64 * 64
        assert off + nbytes <= self.sb_cap, "SBUF overflow: %s need %d at %d" % (name, nbytes, off)
        self.sb_off = off + nbytes
        ap = self.arena[0:shape[0], off:off + nbytes].bitcast(dtype)
        if len(shape) > 2:
            names = " ".join("d%d" % i for i in range(1, len(shape)))
            kw = {"d%d" % i: shape[i] for i in range(1, len(shape))}
            ap = ap.rearrange("p (%s) -> p %s" % (names, names), **kw)
        return ap

    def mark(self):
        self.sb_mark = self.sb_off

    def release(self):
        self.sb_off = self.sb_mark

    def bank(self, b, dtype=F32, nb=1):
        ap = self.psum[:, b * 512:(b + nb) * 512]
        if dtype != F32:
            ap = ap.bitcast(dtype)
        return ap

    def buf(self, name):
        b = self.bufs.get(name)
        if b is None:
            b = Buf(name)
            self.bufs[name] = b
        return b

    def _mk(self, eng, fn, dma):
        op = Op()
        op.eng = eng
        op.fn = fn
        op.seq = len(self.q[eng])
        op.signal = False
        op.is_dma = dma
        op.dsem = None
        op.dval = 0
        op.waits = []
        op.cnt = 0
        op.phase = self.phase
        op.edeps = {}
        op.ddeps = []
        return op

    def op(self, eng, fn, reads=(), writes=(), dma=False):
        op = self._mk(eng, fn, dma)
        deps = []
        for b in reads:
            if isinstance(b, str):
                b = self.buf(b)
            if b.w is not None:
                deps.append((b.w, True))
        for b in writes:
            if isinstance(b, str):
                b = self.buf(b)
            if b.w is not None:
                deps.append((b.w, True))
            for r in b.rs.values():
                deps.append((r, False))
            for r in b.rd:
                deps.append((r, False))
        best = {}
        for d, strong in deps:
            if d is op or d.phase != self.phase:
                continue
            if d.is_dma:
                key = (eng, id(d))
                if key in self.dwaited:
                    continue
                self.dwaited[key] = True
                op.ddeps.append(d)
            else:
                if d.eng == eng and eng == "pe":
                    continue
                b = best.get(d.eng)
                if b is None or b.seq < d.seq:
                    best[d.eng] = d
        for te, d in best.items():
            key = (eng, te)
            if self.waited.get(key, -1) >= d.seq:
                continue
            self.waited[key] = d.seq
            d.signal = True
            op.edeps[te] = d
        if dma:
            i = self.dcount[eng]
            self.dcount[eng] = i + 1
            K = self.KDMA
            op.dsem = self.dsems[eng][i % K]
            op.dval = 16 * (i // K + 1)
            if i >= K:
                old = self.dma_ops[eng][i - K]
                key = (eng, id(old))
                if key not in self.dwaited:
                    self.dwaited[key] = True
                    op.ddeps.append(old)
            self.dma_ops[eng].append(op)
        if self.bar_wait_pending[eng]:
            op.waits.append((self.bar_sem, self.bar_wait_pending[eng]))
            self.bar_wait_pending[eng] = 0
        for b in writes:
            if isinstance(b, str):
                b = self.buf(b)
            b.w = op
            b.rs = {}
            b.rd = []
        for b in reads:
            if isinstance(b, str):
                b = self.buf(b)
            if b.w is not op:
                if dma:
                    b.rd.append(op)
                else:
                    b.rs[eng] = op
        self.q[eng].append(op)
        return op

    def barrier(self):
        sp_op = self._mk("sp", None, False)
        for e in ("pe", "act", "dve", "pool"):
            if self.q[e]:
                last = self.q[e][-1]
                if last.is_dma:
                    for o in reversed(self.q[e]):
                        if not o.is_dma:
                            last = o
                            break
                if not last.is_dma:
                    last.signal = True
                    sp_op.edeps[e] = last
        for e in ("sp", "act", "pool"):
            n = self.dcount[e]
            for o in self.dma_ops[e][max(0, n - self.KDMA):]:
                sp_op.ddeps.append(o)
        self.bar_cnt += 1
        bc = self.bar_cnt
        bs = self.bar_sem
        sp_op.fn = lambda eng: eng.sem_inc(bs, 1)
        if self.bar_wait_pending["sp"]:
            sp_op.waits.append((self.bar_sem, self.bar_wait_pending["sp"]))
            self.bar_wait_pending["sp"] = 0
        self.q["sp"].append(sp_op)
        for e in ("pe", "act", "dve", "pool"):
            self.bar_wait_pending[e] = bc
        self.phase += 1
        for b in self.bufs.values():
            b.w = None
            b.rs = {}
            b.rd = []

    def emit(self):
        nc = self.nc
        for e in ("pe", "act", "dve", "pool"):
            c = 0
            for o in self.q[e]:
                if o.signal:
                    c += 1
                    o.cnt = c
        esem = self.esem

        def run(eng_name, eng):
            for o in self.q[eng_name]:
                for (s, v) in o.waits:
                    eng.wait_ge(s, v)
                for te, d in o.edeps.items():
                    eng.wait_ge(esem[te], d.cnt)
                for d in o.ddeps:
                    eng.wait_ge(d.dsem, d.dval)
                inst = o.fn(eng)
                if o.signal:
                    inst.then_inc(esem[eng_name], 1)
                if o.is_dma:
                    inst.then_inc(o.dsem, 16)

        with nc.Block() as block:
            @block.tensor
            def _(e):
                run("pe", e)

            @block.scalar
            def _(e):
                run("act", e)

            @block.vector
            def _(e):
                run("dve", e)

            @block.gpsimd
            def _(e):
                run("pool", e)

            @block.sync
            def _(e):
                run("sp", e)

D = 1024
DFF = 2816
NJ = DFF // 128
S = 8192
NOWN = 4096
TB = 512
EPS = 1e-6
NH = 8
SM_SCALE = 96.0 ** -0.5


def build(stop_after=99, dbg=()):
    nc = bass.Bass("TRN2", target_bir_lowering=False)
    P = Prog(nc)
    P.init_mem()

    def din(name, shape, dt=F32):
        return nc.dram_tensor(name, list(shape), dt, kind="ExternalInput").ap()

    def dscr(name, shape, dt):
        kind = "ExternalOutput" if name in dbg else "Internal"
        return nc.dram_tensor(name, list(shape), dt, kind=kind).ap()

    x_d = din("x", [S, D])
    pos_d = din("pos", [1, S], I32)
    cpp_d = din("c_pp", [128, 8])
    adaw_d = din("ada_w", [D, 9 * D])
    adab_d = din("ada_b_pp", [128, 72])
    norms_d = din("norms_pp", [128, 24])
    fnorm_d = din("final_norm", [1, D])
    f1g_d = din("f1_wg", [D, DFF])
    f1u_d = din("f1_wu", [D, DFF])
    f1d_d = din("f1_wd", [DFF, D])
    f2g_d = din("f2_wg", [D, DFF])
    f2u_d = din("f2_wu", [D, DFF])
    f2d_d = din("f2_wd", [DFF, D])
    winA_d = din("w_inA", [D, 1280])
    winG_d = din("w_inG", [D, 2048])
    qkn_d = din("qkn_pp", [128, 5])
    wq_d = din("w_q", [384, 1024])
    wkn_d = din("w_kn", [256, 512])
    wv_d = din("w_v", [256, 512])
    wfo_d = din("w_fo", [512, D])
    wmo_d = din("w_mo", [512, D])
    wout_d = din("w_out", [D, D])
    fc_d = din("t_fc", [128, 256])
    f128r_d = din("t_f128r", [128, 128])
    f128i_d = din("t_f128i", [128, 128])
    tcos_d = din("t_cos", [64, 4096])
    tsin_d = din("t_sin", [64, 4096])
    ropec_d = din("t_rope", [32, 4])

    out_d = nc.dram_tensor("out", [NOWN, D], F32, kind="ExternalOutput").ap()
    mods_d = dscr("mods", [1, 9 * D], F32)
    x1_d = dscr("x1s", [S, D], F32)
    zu_d = dscr("zu", [512, S], BF16)
    zl_d = dscr("zl", [768, S], F32)
    ft_d = dscr("fts", [512, NOWN], BF16)
    ot_d = dscr("ots", [512, NOWN], BF16)
    x2_d = dscr("x2s", [NOWN, D], F32)

    ident_f = P.sb([128, 128], F32)
    ident_b = P.sb([128, 128], BF16)
    ones_b = P.sb([128, 128], BF16)
    epsb = P.sb([128, 1], F32)
    modpp = P.sb([128, 72], F32)
    gv = P.sb([128, 24], F32)
    normspp = P.sb([128, 24], F32)
    P.op("pool", lambda e: e.memset(ident_f, 0.0), writes=["ident_f"])
    P.op("pool", lambda e: e.affine_select(out=ident_f, in_=ident_f, pattern=[[-1, 128]],
                                          compare_op=ALU.not_equal, fill=1.0, base=0,
                                          channel_multiplier=1),
         reads=["ident_f"], writes=["ident_f"])
    P.op("dve", lambda e: e.tensor_copy(out=ident_b, in_=ident_f), reads=["ident_f"], writes=["ident_b"])
    P.op("dve", lambda e: e.memset(ones_b, 1.0), writes=["ones_b"])
    P.op("dve", lambda e: e.memset(epsb, EPS), writes=["epsb"])
    P.mark()

    PSB = ["psb%d" % i for i in range(8)]

    def phase_adaln():
        cpp = P.sb([128, 8], F32)
        cact = P.sb([128, 8], BF16)
        adab = P.sb([128, 72], F32)
        modT = P.sb([128, 128], F32)
        wblk = [P.sb([128, 8, 1024], BF16) for _ in range(2)]
        P.op("sp", lambda e: e.dma_start(out=cpp, in_=cpp_d), writes=["cpp"], dma=True)
        P.op("sp", lambda e: e.dma_start(out=adab, in_=adab_d), writes=["adab"], dma=True)
        P.op("sp", lambda e: e.dma_start(out=normspp, in_=norms_d), writes=["normspp"], dma=True)
        P.op("act", lambda e: e.activation(out=cact, in_=cpp, func=AF.Silu), reads=["cpp"], writes=["cact"])
        ps0 = P.bank(0)
        for blk in range(9):
            wb = wblk[blk % 2]
            wn = "wblk%d" % (blk % 2)
            src = adaw_d[:, blk * 1024:(blk + 1) * 1024].rearrange("(k p) n -> p k n", p=128)
            P.op("pool", lambda e, wb=wb, src=src: e.dma_start(out=wb, in_=src), writes=[wn], dma=True)
            for jj in range(8):
                j = blk * 8 + jj
                for k in range(8):
                    P.op("pe", lambda e, wb=wb, j=j, jj=jj, k=k: e.matmul(
                        ps0[:, j:j + 1], lhsT=wb[:, k, jj * 128:(jj + 1) * 128], rhs=cact[:, k:k + 1],
                        start=(k == 0), stop=(k == 7)),
                        reads=[wn, "cact"], writes=[PSB[0]])
        P.op("dve", lambda e: e.tensor_tensor(out=modpp, in0=ps0[:, 0:72], in1=adab, op=ALU.add),
             reads=[PSB[0], "adab"], writes=["modpp"])
        for i in range(3):
            sc = modpp[:, i * 24 + 8:i * 24 + 16]
            P.op("dve", lambda e, i=i, sc=sc: e.scalar_tensor_tensor(
                out=gv[:, i * 8:(i + 1) * 8], in0=sc, scalar=1.0, in1=normspp[:, i * 8:(i + 1) * 8],
                op0=ALU.add, op1=ALU.mult),
                reads=["modpp", "normspp"], writes=["gv"])
        ps1 = P.bank(1)
        P.op("pe", lambda e: e.transpose(ps1[0:72, 0:128], modpp[:, 0:72], ident_f),
             reads=["modpp", "ident_f"], writes=[PSB[1]])
        P.op("dve", lambda e: e.tensor_copy(out=modT[0:72, :], in_=ps1[0:72, 0:128]),
             reads=[PSB[1]], writes=["modT"])
        P.op("sp", lambda e: e.dma_start(
            out=mods_d.rearrange("o (j f) -> (o j) f", f=128), in_=modT[0:72, :]),
            reads=["modT"], dma=True)

    def load_bcast(dst, name, off):
        P.op("sp", lambda e: e.dma_start(out=dst, in_=mods_d[0:1, off:off + D].broadcast_to([128, D])),
             writes=[name], dma=True)

    def norm_to_hT(xt, xnames, ni, hT, hname, scr):
        ss, rstd, xs, junk = scr["ss"], scr["rstd"], scr["xs"], scr["junk"]
        SSN = ["ss0", "ss1", "ss2", "ss3"]
        P.op("dve", lambda e: e.memset(ss, 0.0), writes=SSN)
        for t in range(4):
            P.op("act", lambda e, t=t: e.activation(out=junk, in_=xt[t], func=AF.Square,
                                                    accum_out=ss[:, t:t + 1]),
                 reads=[xnames[t]], writes=[SSN[t], "junk"])
        P.op("act", lambda e: e.activation(out=rstd, in_=ss, func=AF.Sqrt, scale=1.0 / D, bias=epsb),
             reads=SSN + ["epsb"], writes=["rstd"])
        P.op("dve", lambda e: e.reciprocal(out=rstd, in_=rstd), reads=["rstd"], writes=["rstd"])
        for t in range(4):
            P.op("act", lambda e, t=t: e.activation(out=xs[t % 2], in_=xt[t], func=AF.Copy,
                                                    scale=rstd[:, t:t + 1]),
                 reads=[xnames[t], "rstd"], writes=["xs%d" % (t % 2)])
            for c in range(8):
                pb = P.bank(4 + c // 2, BF16)
                P.op("pe", lambda e, t=t, c=c, pb=pb: e.transpose(
                    pb[:, (c % 2) * 512 + t * 128:(c % 2) * 512 + (t + 1) * 128],
                    xs[t % 2][:, c * 128:(c + 1) * 128], ident_b),
                    reads=["xs%d" % (t % 2), "ident_b"], writes=[PSB[4 + c // 2]])
        for c in range(8):
            pb = P.bank(4 + c // 2, BF16)
            P.op("dve", lambda e, c=c, pb=pb: e.tensor_scalar(
                out=hT[:, c, :], in0=pb[:, (c % 2) * 512:(c % 2 + 1) * 512],
                scalar1=gv[:, ni * 8 + c:ni * 8 + c + 1],
                scalar2=modpp[:, ni * 24 + c:ni * 24 + c + 1],
                op0=ALU.mult, op1=ALU.add),
                reads=[PSB[4 + c // 2], "gv", "modpp"], writes=[hname])

    def norm_scratch():
        return {"ss": P.sb([128, 4], F32), "rstd": P.sb([128, 4], F32),
                "xs": [P.sb([128, D], BF16) for _ in range(2)],
                "junk": P.sb([128, D], BF16)}

    def ffn_phase(src_d, dst_d, nblk, wg_d, wu_d, wd_d, ni, goff, final):
        wg = P.sb([128, 8, DFF], BF16)
        wu = P.sb([128, 8, DFF], BF16)
        wd = P.sb([128, NJ, D], BF16)
        xt = [P.sb([128, D], F32) for _ in range(4)]
        xn = ["xt%d" % t for t in range(4)]
        scr = norm_scratch()
        hT = P.sb([128, 8, TB], BF16)
        aT = P.sb([128, NJ, TB], BF16)
        sg = [P.sb([128, TB], BF16) for _ in range(2)]
        tmp = P.sb([128, D], F32)
        ghb = P.sb([128, D], F32)
        if final:
            fnb = P.sb([128, D], F32)
            ss2 = P.sb([128, 1], F32)
            P.op("sp", lambda e: e.dma_start(out=fnb, in_=fnorm_d.broadcast_to([128, D])),
                 writes=["fnb"], dma=True)
        load_bcast(ghb, "ghb", goff)
        P.op("dve", lambda e: e.tensor_scalar(out=ghb, in0=ghb, scalar1=0.5, scalar2=None, op0=ALU.mult),
             reads=["ghb"], writes=["ghb"])
        for i in range(NJ // 2):
            cs = slice(i * 256, (i + 1) * 256)
            P.op("pool", lambda e, cs=cs: e.dma_start(
                out=wg[:, :, cs], in_=wg_d[:, cs].rearrange("(k p) n -> p k n", p=128)),
                writes=["wg%d" % i], dma=True)
            P.op("pool", lambda e, cs=cs: e.dma_start(
                out=wu[:, :, cs], in_=wu_d[:, cs].rearrange("(k p) n -> p k n", p=128)),
                writes=["wu%d" % i], dma=True)
        for i in range(NJ // 2):
            P.op("pool", lambda e, i=i: e.dma_start(
                out=wd[:, 2 * i:2 * i + 2, :],
                in_=wd_d[i * 256:(i + 1) * 256, :].rearrange("(j p) n -> p j n", p=128)),
                writes=["wd%d" % i], dma=True)
        for blk in range(nblk):
            r0 = blk * TB
            for t in range(4):
                P.op("sp", lambda e, t=t, r0=r0: e.dma_start(out=xt[t], in_=src_d[r0 + t * 128:r0 + (t + 1) * 128, :]),
                     writes=[xn[t]], dma=True)
            norm_to_hT(xt, xn, ni, hT, "hT", scr)
            for j in range(NJ):
                pg = P.bank(j % 2)
                pu = P.bank(2 + j % 2)
                for k in range(8):
                    P.op("pe", lambda e, j=j, k=k, pg=pg: e.matmul(
                        pg, lhsT=wg[:, k, j * 128:(j + 1) * 128], rhs=hT[:, k, :], start=(k == 0), stop=(k == 7)),
                        reads=["wg%d" % (j // 2), "hT"], writes=[PSB[j % 2]])
                for k in range(8):
                    P.op("pe", lambda e, j=j, k=k, pu=pu: e.matmul(
                        pu, lhsT=wu[:, k, j * 128:(j + 1) * 128], rhs=hT[:, k, :], start=(k == 0), stop=(k == 7)),
                        reads=["wu%d" % (j // 2), "hT"], writes=[PSB[2 + j % 2]])
                P.op("act", lambda e, j=j, pg=pg: e.activation(out=sg[j % 2], in_=pg, func=AF.Silu),
                     reads=[PSB[j % 2]], writes=["sg%d" % (j % 2)])
                P.op("dve", lambda e, j=j, pu=pu: e.tensor_tensor(out=aT[:, j, :], in0=pu, in1=sg[j % 2], op=ALU.mult),
                     reads=[PSB[2 + j % 2], "sg%d" % (j % 2)], writes=["aT"])
            for t in range(4):
                b0 = 4 + 2 * (t % 2)
                pd = P.bank(b0, F32, 2)
                for j in range(NJ):
                    for hf in range(2):
                        P.op("pe", lambda e, t=t, j=j, hf=hf, pd=pd: e.matmul(
                            pd[:, hf * 512:(hf + 1) * 512], lhsT=aT[:, j, t * 128:(t + 1) * 128],
                            rhs=wd[:, j, hf * 512:(hf + 1) * 512], start=(j == 0), stop=(j == NJ - 1)),
                            reads=["aT", "wd%d" % (j // 2)], writes=[PSB[b0 + hf]])
                P.op("dve", lambda e, pd=pd: e.tensor_tensor(out=tmp, in0=pd, in1=ghb, op=ALU.mult),
                     reads=[PSB[b0], PSB[b0 + 1], "ghb"], writes=["tmp"])
                P.op("pool", lambda e, t=t: e.tensor_tensor(out=xt[t], in0=xt[t], in1=tmp, op=ALU.add),
                     reads=[xn[t], "tmp"], writes=[xn[t]])
                if final:
                    P.op("dve", lambda e: e.memset(ss2, 0.0), writes=["ss2"])
                    P.op("act", lambda e, t=t: e.activation(out=tmp, in_=xt[t], func=AF.Square, accum_out=ss2),
                         reads=[xn[t], "ss2"], writes=["tmp", "ss2"])
                    P.op("act", lambda e: e.activation(out=ss2, in_=ss2, func=AF.Sqrt, scale=1.0 / D, bias=epsb),
                         reads=["ss2", "epsb"], writes=["ss2"])
                    P.op("dve", lambda e: e.reciprocal(out=ss2, in_=ss2), reads=["ss2"], writes=["ss2"])
                    P.op("dve", lambda e, t=t: e.scalar_tensor_tensor(
                        out=xt[t], in0=xt[t], scalar=ss2, in1=fnb, op0=ALU.mult, op1=ALU.mult),
                        reads=[xn[t], "ss2", "fnb"], writes=[xn[t]])
                P.op("sp", lambda e, t=t, r0=r0: e.dma_start(out=dst_d[r0 + t * 128:r0 + (t + 1) * 128, :], in_=xt[t]),
                     reads=[xn[t]], dma=True)

    phases = []
    phases.append(phase_adaln)
    phases.append(lambda: ffn_phase(x_d, x1_d, S // TB, f1g_d, f1u_d, f1d_d, 0, 16 * 128, False))
    def phase_a2():
        wA = P.sb([128, 8, 1280], BF16)
        xt = [P.sb([128, D], F32) for _ in range(4)]
        xn = ["xt%d" % t for t in range(4)]
        scr = norm_scratch()
        hT = P.sb([128, 8, TB], BF16)
        zu_sb = P.sb([128, 4, TB], BF16)
        zl_sb = P.sb([128, 6, TB], F32)
        for k in range(8):
            P.op("pool", lambda e, k=k: e.dma_start(out=wA[:, k, :], in_=winA_d[k * 128:(k + 1) * 128, :]),
                 writes=["wA"], dma=True)
        for blk in range(S // TB):
            r0 = blk * TB
            for t in range(4):
                P.op("sp", lambda e, t=t, r0=r0: e.dma_start(out=xt[t], in_=x1_d[r0 + t * 128:r0 + (t + 1) * 128, :]),
                     writes=[xn[t]], dma=True)
            norm_to_hT(xt, xn, 1, hT, "hT", scr)
            for cc in range(10):
                pb = P.bank(cc % 4)
                for k in range(8):
                    P.op("pe", lambda e, cc=cc, k=k, pb=pb: e.matmul(
                        pb, lhsT=wA[:, k, cc * 128:(cc + 1) * 128], rhs=hT[:, k, :], start=(k == 0), stop=(k == 7)),
                        reads=["wA", "hT"], writes=[PSB[cc % 4]])
                if cc < 4:
                    P.op("act", lambda e, cc=cc, pb=pb: e.activation(out=zu_sb[:, cc, :], in_=pb, func=AF.Copy),
                         reads=[PSB[cc % 4]], writes=["zu_sb"])
                else:
                    P.op("dve", lambda e, cc=cc, pb=pb: e.tensor_copy(out=zl_sb[:, cc - 4, :], in_=pb),
                         reads=[PSB[cc % 4]], writes=["zl_sb"])
            P.op("sp", lambda e, r0=r0: e.dma_start(
                out=zu_d[:, r0:r0 + TB].rearrange("(c p) n -> p c n", p=128), in_=zu_sb),
                reads=["zu_sb"], dma=True)
            P.op("sp", lambda e, r0=r0: e.dma_start(
                out=zl_d[:, r0:r0 + TB].rearrange("(c p) n -> p c n", p=128), in_=zl_sb),
                reads=["zl_sb"], dma=True)

    def phase_attn():
        ropec = P.sb([32, 4], F32)
        qkn = P.sb([128, 5], F32)
        cosT = P.sb([32, S], BF16)
        ssT = P.sb([32, S], BF16)
        kvnT = P.sb([128, 2, S], BF16)
        kropeT = P.sb([32, S], BF16)
        qnT = P.sb([128, 3, NOWN], BF16)
        wkn = P.sb([128, 2, 512], BF16)
        wv = P.sb([128, 2, 512], BF16)
        wq = P.sb([128, 3, 1024], BF16)
        KhT = P.sb([128, S], BF16)
        QhT = P.sb([128, NOWN], BF16)
        Vaug = P.sb([128, 64, 128], BF16)
        PT = [P.sb([128, 1024], BF16) for _ in range(3)]
        posi = P.sb([32, 1024], I32)
        ang = P.sb([32, 1024], F32)
        ang2 = P.sb([32, 1024], F32)
        lat = P.sb([128, 3, TB], F32)
        kra = P.sb([32, TB], F32)
        krb = P.sb([32, TB], F32)
        sq = P.sb([128, 3, TB], BF16)
        rstdk = P.sb([128, TB], F32)
        t1 = P.sb([32, TB], F32)
        t2 = P.sb([32, TB], F32)
        rden = P.sb([64, TB], F32)
        oT = [P.sb([64, TB], BF16) for _ in range(2)]
        P.op("sp", lambda e: e.dma_start(out=ropec, in_=ropec_d), writes=["ropec"], dma=True)
        P.op("sp", lambda e: e.dma_start(out=qkn, in_=qkn_d), writes=["qkn"], dma=True)
        P.op("pool", lambda e: e.dma_start(out=wkn, in_=wkn_d.rearrange("(k p) n -> p k n", p=128)), writes=["wkn"], dma=True)
        P.op("pool", lambda e: e.dma_start(out=wv, in_=wv_d.rearrange("(k p) n -> p k n", p=128)), writes=["wv"], dma=True)
        P.op("pool", lambda e: e.dma_start(out=wq, in_=wq_d.rearrange("(k p) n -> p k n", p=128)), writes=["wq"], dma=True)
        PI = float(np.pi)
        for ch in range(S // 1024):
            cs = slice(ch * 1024, (ch + 1) * 1024)
            P.op("sp", lambda e, cs=cs: e.dma_start(out=posi, in_=pos_d[0:1, cs].broadcast_to([32, 1024])),
                 writes=["posi"], dma=True)
            P.op("dve", lambda e: e.tensor_copy(out=ang, in_=posi), reads=["posi"], writes=["ang"])
            P.op("dve", lambda e: e.tensor_scalar(out=ang, in0=ang, scalar1=ropec[:, 0:1], scalar2=None, op0=ALU.mult),
                 reads=["ang", "ropec"], writes=["ang"])
            P.op("dve", lambda e: e.tensor_scalar(out=ang, in0=ang, scalar1=1.0 / (2 * PI), scalar2=None, op0=ALU.mult),
                 reads=["ang"], writes=["ang"])
            for which in range(2):
                if which == 1:
                    P.op("dve", lambda e: e.tensor_scalar(out=ang, in0=ang, scalar1=0.25, scalar2=None, op0=ALU.add),
                         reads=["ang"], writes=["ang"])
                P.op("dve", lambda e: e.tensor_copy(out=posi, in_=ang), reads=["ang"], writes=["posi"])
                P.op("dve", lambda e: e.tensor_copy(out=ang2, in_=posi), reads=["posi"], writes=["ang2"])
                P.op("dve", lambda e: e.tensor_tensor(out=ang2, in0=ang, in1=ang2, op=ALU.subtract),
                     reads=["ang", "ang2"], writes=["ang2"])
                if which == 0:
                    P.op("act", lambda e, cs=cs: e.activation(out=ssT[:, cs], in_=ang2, func=AF.Sin, scale=ropec[:, 1:2]),
                         reads=["ang2", "ropec"], writes=["ssT"])
                else:
                    P.op("act", lambda e, cs=cs: e.activation(out=cosT[:, cs], in_=ang2, func=AF.Sin, scale=2 * PI),
                         reads=["ang2"], writes=["cosT"])

        def latent_norm(row0, nch, nfeat, gcol, dstT, dname, blk):
            r0 = blk * TB
            P.op("sp", lambda e: e.dma_start(
                out=lat[:, 0:nch, :], in_=zl_d[row0:row0 + nch * 128, r0:r0 + TB].rearrange("(c p) n -> p c n", p=128)),
                writes=["lat"], dma=True)
            P.op("dve", lambda e: e.tensor_tensor(out=sq[:, 0:nch, :], in0=lat[:, 0:nch, :], in1=lat[:, 0:nch, :], op=ALU.mult),
                 reads=["lat"], writes=["sq"])
            pb = P.bank(0)
            for c in range(nch):
                P.op("pe", lambda e, c=c: e.matmul(pb, lhsT=ones_b, rhs=sq[:, c, :], start=(c == 0), stop=(c == nch - 1)),
                     reads=["ones_b", "sq"], writes=[PSB[0]])
            P.op("act", lambda e: e.activation(out=rstdk, in_=pb, func=AF.Sqrt, scale=1.0 / nfeat, bias=epsb),
                 reads=[PSB[0], "epsb"], writes=["rstdk"])
            P.op("dve", lambda e: e.reciprocal(out=rstdk, in_=rstdk), reads=["rstdk"], writes=["rstdk"])
            for c in range(nch):
                P.op("dve", lambda e, c=c: e.scalar_tensor_tensor(
                    out=dstT[:, c, r0:r0 + TB], in0=lat[:, c, :], scalar=qkn[:, gcol + c:gcol + c + 1], in1=rstdk,
                    op0=ALU.mult, op1=ALU.mult),
                    reads=["lat", "qkn", "rstdk"], writes=[dname])

        for blk in range(S // TB):
            r0 = blk * TB
            latent_norm(384, 2, 256, 3, kvnT, "kvnT", blk)
            P.op("sp", lambda e, r0=r0: e.dma_start(out=kra, in_=zl_d[640:672, r0:r0 + TB]), writes=["kra"], dma=True)
            P.op("sp", lambda e, r0=r0: e.dma_start(out=krb, in_=zl_d[672:704, r0:r0 + TB]), writes=["krb"], dma=True)
            P.op("dve", lambda e, r0=r0: e.tensor_tensor(out=t1, in0=kra, in1=cosT[:, r0:r0 + TB], op=ALU.mult),
                 reads=["kra", "cosT"], writes=["t1"])
            P.op("dve", lambda e, r0=r0: e.tensor_tensor(out=t2, in0=krb, in1=ssT[:, r0:r0 + TB], op=ALU.mult),
                 reads=["krb", "ssT"], writes=["t2"])
            P.op("dve", lambda e, r0=r0: e.tensor_tensor(out=kropeT[:, r0:r0 + TB], in0=t1, in1=t2, op=ALU.add),
                 reads=["t1", "t2"], writes=["kropeT"])
        for blk in range(NOWN // TB):
            latent_norm(0, 3, 384, 0, qnT, "qnT", blk)
        P.op("dve", lambda e: e.memset(Vaug[:, :, 64:128], 1.0), writes=["Vaug"])
        P.op("pool", lambda e: e.memset(KhT[96:128, :], 0.0), writes=["KhT"])
        P.op("pool", lambda e: e.memset(QhT[96:128, :], 0.0), writes=["QhT"])

        def do_head(h):
            for blk in range(S // TB):
                r0 = blk * TB
                pb = P.bank(blk % 2)
                for c in range(2):
                    P.op("pe", lambda e, c=c, r0=r0, pb=pb: e.matmul(
                        pb[0:64, :], lhsT=wkn[:, c, h * 64:(h + 1) * 64], rhs=kvnT[:, c, r0:r0 + TB],
                        start=(c == 0), stop=(c == 1)),
                        reads=["wkn", "kvnT"], writes=[PSB[blk % 2]])
                P.op("act", lambda e, r0=r0, pb=pb: e.activation(out=KhT[0:64, r0:r0 + TB], in_=pb[0:64, :], func=AF.Copy),
                     reads=[PSB[blk % 2]], writes=["KhT"])
            P.op("dve", lambda e: e.tensor_copy(out=KhT[64:96, :], in_=kropeT), reads=["kropeT"], writes=["KhT"])
            for g8 in range(8):
                pb = P.bank(2 + g8 % 2)
                for i in range(8):
                    tt = g8 * 8 + i
                    for c in range(2):
                        P.op("pe", lambda e, c=c, tt=tt, i=i, pb=pb: e.matmul(
                            pb[:, i * 64:(i + 1) * 64], lhsT=kvnT[:, c, tt * 128:(tt + 1) * 128],
                            rhs=wv[:, c, h * 64:(h + 1) * 64], start=(c == 0), stop=(c == 1)),
                            reads=["kvnT", "wv"], writes=[PSB[2 + g8 % 2]])
                P.op("dve", lambda e, g8=g8, pb=pb: e.tensor_copy(
                    out=Vaug[:, g8 * 8:(g8 + 1) * 8, 0:64], in_=pb.rearrange("p (a d) -> p a d", a=8)),
                    reads=[PSB[2 + g8 % 2]], writes=["Vaug"])
            for blk in range(NOWN // TB):
                r0 = blk * TB
                pq = P.bank(4)
                pr = P.bank(5)
                pw = P.bank(6)
                for c in range(3):
                    P.op("pe", lambda e, c=c, r0=r0: e.matmul(
                        pq[0:64, :], lhsT=wq[:, c, h * 128:h * 128 + 64], rhs=qnT[:, c, r0:r0 + TB],
                        start=(c == 0), stop=(c == 2)),
                        reads=["wq", "qnT"], writes=[PSB[4]])
                for c in range(3):
                    P.op("pe", lambda e, c=c, r0=r0: e.matmul(
                        pr[0:32, :], lhsT=wq[:, c, h * 128 + 64:h * 128 + 96], rhs=qnT[:, c, r0:r0 + TB],
                        start=(c == 0), stop=(c == 2)),
                        reads=["wq", "qnT"], writes=[PSB[5]])
                for c in range(3):
                    P.op("pe", lambda e, c=c, r0=r0: e.matmul(
                        pw[0:32, :], lhsT=wq[:, c, h * 128 + 96:h * 128 + 128], rhs=qnT[:, c, r0:r0 + TB],
                        start=(c == 0), stop=(c == 2)),
                        reads=["wq", "qnT"], writes=[PSB[6]])
                P.op("act", lambda e, r0=r0: e.activation(out=QhT[0:64, r0:r0 + TB], in_=pq[0:64, :], func=AF.Copy, scale=SM_SCALE),
                     reads=[PSB[4]], writes=["QhT"])
                P.op("dve", lambda e, r0=r0: e.scalar_tensor_tensor(
                    out=t1, in0=pr[0:32, :], scalar=SM_SCALE, in1=cosT[:, r0:r0 + TB], op0=ALU.mult, op1=ALU.mult),
                    reads=[PSB[5], "cosT"], writes=["t1"])
                P.op("dve", lambda e, r0=r0: e.scalar_tensor_tensor(
                    out=t2, in0=pw[0:32, :], scalar=SM_SCALE, in1=ssT[:, r0:r0 + TB], op0=ALU.mult, op1=ALU.mult),
                    reads=[PSB[6], "ssT"], writes=["t2"])
                P.op("dve", lambda e, r0=r0: e.tensor_tensor(out=QhT[64:96, r0:r0 + TB], in0=t1, in1=t2, op=ALU.add),
                     reads=["t1", "t2"], writes=["QhT"])
            pair = 0
            for qb in range(NOWN // TB):
                q0 = qb * TB
                po = P.bank(6 + qb % 2)
                for kp in range(32):
                    b0 = 2 * (pair % 3)
                    ps = P.bank(b0, F32, 2)
                    ptb = PT[pair % 3]
                    ptn = "PT%d" % (pair % 3)
                    for i in range(2):
                        kt = kp * 2 + i
                        P.op("pe", lambda e, kt=kt, i=i, ps=ps, q0=q0: e.matmul(
                            ps[:, i * 512:(i + 1) * 512], lhsT=KhT[:, kt * 128:(kt + 1) * 128],
                            rhs=QhT[:, q0:q0 + TB], start=True, stop=True),
                            reads=["KhT", "QhT"], writes=[PSB[b0 + i]])
                    P.op("act", lambda e, ps=ps, ptb=ptb: e.activation(out=ptb, in_=ps, func=AF.Exp),
                         reads=[PSB[b0], PSB[b0 + 1]], writes=[ptn])
                    for i in range(2):
                        kt = kp * 2 + i
                        P.op("pe", lambda e, kt=kt, i=i, ptb=ptb, po=po: e.matmul(
                            po, lhsT=Vaug[:, kt, :], rhs=ptb[:, i * 512:(i + 1) * 512],
                            start=(kt == 0), stop=(kt == 63)),
                            reads=["Vaug", ptn], writes=[PSB[6 + qb % 2]])
                    pair += 1
                P.op("dve", lambda e, po=po: e.reciprocal(out=rden, in_=po[64:128, :]),
                     reads=[PSB[6 + qb % 2]], writes=["rden"])
                ob = oT[qb % 2]
                on = "oT%d" % (qb % 2)
                P.op("dve", lambda e, po=po, ob=ob: e.tensor_tensor(out=ob, in0=po[0:64, :], in1=rden, op=ALU.mult),
                     reads=[PSB[6 + qb % 2], "rden"], writes=[on])
                P.op("sp", lambda e, ob=ob, q0=q0: e.dma_start(out=ot_d[h * 64:(h + 1) * 64, q0:q0 + TB], in_=ob),
                     reads=[on], dma=True)

        for h in range(NH):
            do_head(h)

    def phase_fourier():
        fc = P.sb([128, 256], BF16)
        f128r = P.sb([128, 128], BF16)
        f128i = P.sb([128, 128], BF16)
        tcos = P.sb([64, 4096], BF16)
        tsin = P.sb([64, 4096], BF16)
        uT = P.sb([128, S], BF16)
        Z = P.sb([128, 64, 256], BF16)
        W = P.sb([64, 64, 2, 128], BF16)
        FT = P.sb([128, NOWN], BF16)
        for dst, src, nm in ((fc, fc_d, "fc"), (f128r, f128r_d, "f128r"), (f128i, f128i_d, "f128i"),
                             (tcos, tcos_d, "tcos"), (tsin, tsin_d, "tsin")):
            P.op("pool", lambda e, dst=dst, src=src: e.dma_start(out=dst, in_=src), writes=[nm], dma=True)
        uTv = uT.rearrange("p (j s) -> p s j", s=64)
        tcv = tcos.rearrange("p (k2 k1) -> p k1 k2", k1=64)
        tsv = tsin.rearrange("p (k2 k1) -> p k1 k2", k1=64)
        FTv = FT.rearrange("p (k2 k1) -> p k1 k2", k1=64)
        Wv = W.rearrange("p k r c -> p c r k")
        for g in range(4):
            P.op("sp", lambda e, g=g: e.dma_start(out=uT, in_=zu_d[g * 128:(g + 1) * 128, :]), writes=["uT"], dma=True)
            for sp_ in range(32):
                pb = P.bank(sp_ % 2)
                for i in range(2):
                    s2 = sp_ * 2 + i
                    P.op("pe", lambda e, s2=s2, i=i, pb=pb: e.matmul(
                        pb[:, i * 256:(i + 1) * 256], lhsT=uTv[:, s2, :], rhs=fc, start=True, stop=True),
                        reads=["uT", "fc"], writes=[PSB[sp_ % 2]])
                eng = "act" if sp_ % 2 == 0 else "dve"
                if eng == "act":
                    P.op("act", lambda e, sp_=sp_, pb=pb: e.activation(
                        out=Z[:, 2 * sp_:2 * sp_ + 2, :], in_=pb.rearrange("p (a c) -> p a c", a=2), func=AF.Copy),
                        reads=[PSB[sp_ % 2]], writes=["Z"])
                else:
                    P.op("dve", lambda e, sp_=sp_, pb=pb: e.tensor_copy(
                        out=Z[:, 2 * sp_:2 * sp_ + 2, :], in_=pb.rearrange("p (a c) -> p a c", a=2)),
                        reads=[PSB[sp_ % 2]], writes=["Z"])
            for c4 in range(32):
                pb = P.bank(2 + c4 % 2)
                for i in range(4):
                    cp = c4 * 4 + i
                    P.op("pe", lambda e, cp=cp, i=i, pb=pb: e.matmul(
                        pb[0:64, i * 128:(i + 1) * 128], lhsT=Z[:, :, cp], rhs=f128r, start=True, stop=False),
                        reads=["Z", "f128r"], writes=[PSB[2 + c4 % 2]])
                    P.op("pe", lambda e, cp=cp, i=i, pb=pb: e.matmul(
                        pb[0:64, i * 128:(i + 1) * 128], lhsT=Z[:, :, 128 + cp], rhs=f128i, start=False, stop=True),
                        reads=["Z", "f128i"], writes=[PSB[2 + c4 % 2]])
                src = pb[0:64, :].rearrange("p (c r k) -> p c r k", c=4, r=2)
                dstv = Wv[:, c4 * 4:(c4 + 1) * 4, :, :]
                if c4 % 2 == 0:
                    P.op("act", lambda e, src=src, dstv=dstv: e.activation(out=dstv, in_=src, func=AF.Copy),
                         reads=[PSB[2 + c4 % 2]], writes=["W"])
                else:
                    P.op("dve", lambda e, src=src, dstv=dstv: e.tensor_copy(out=dstv, in_=src),
                         reads=[PSB[2 + c4 % 2]], writes=["W"])
            for k8 in range(8):
                pb = P.bank(4 + k8 % 2)
                for i in range(8):
                    k1 = k8 * 8 + i
                    P.op("pe", lambda e, k1=k1, i=i, pb=pb: e.matmul(
                        pb[:, i * 64:(i + 1) * 64], lhsT=W[:, k1, 0, :], rhs=tcv[:, k1, :], start=True, stop=False),
                        reads=["W", "tcos"], writes=[PSB[4 + k8 % 2]])
                    P.op("pe", lambda e, k1=k1, i=i, pb=pb: e.matmul(
                        pb[:, i * 64:(i + 1) * 64], lhsT=W[:, k1, 1, :], rhs=tsv[:, k1, :], start=False, stop=True),
                        reads=["W", "tsin"], writes=[PSB[4 + k8 % 2]])
                P.op("dve", lambda e, k8=k8, pb=pb: e.tensor_copy(
                    out=FTv[:, k8 * 8:(k8 + 1) * 8, :], in_=pb.rearrange("p (a k) -> p a k", a=8)),
                    reads=[PSB[4 + k8 % 2]], writes=["FT"])
            P.op("sp", lambda e, g=g: e.dma_start(out=ft_d[g * 128:(g + 1) * 128, :], in_=FT), reads=["FT"], dma=True)

    def phase_c1():
        wG = P.sb([128, 8, 2048], BF16)
        wfo = P.sb([128, 4, D], BF16)
        wmo = P.sb([128, 4, D], BF16)
        wo = P.sb([128, 8, D], BF16)
        xt = [P.sb([128, D], F32) for _ in range(4)]
        xn = ["xt%d" % t for t in range(4)]
        scr = norm_scratch()
        hT = P.sb([128, 8, TB], BF16)
        FTb = P.sb([128, 4, TB], BF16)
        OTb = P.sb([128, 4, TB], BF16)
        mT = P.sb([128, 8, TB], BF16)
        sga = [P.sb([128, TB], F32) for _ in range(2)]
        sgb = [P.sb([128, TB], F32) for _ in range(2)]
        u1 = P.sb([128, TB], F32)
        u2 = P.sb([128, TB], F32)
        tmp = P.sb([128, D], F32)
        g2b = P.sb([128, D], F32)
        load_bcast(g2b, "g2b", 40 * 128)
        for k in range(8):
            P.op("pool", lambda e, k=k: e.dma_start(out=wG[:, k, :], in_=winG_d[k * 128:(k + 1) * 128, :]), writes=["wG"], dma=True)
        P.op("pool", lambda e: e.dma_start(out=wfo, in_=wfo_d.rearrange("(k p) n -> p k n", p=128)), writes=["wfo"], dma=True)
        P.op("pool", lambda e: e.dma_start(out=wmo, in_=wmo_d.rearrange("(k p) n -> p k n", p=128)), writes=["wmo"], dma=True)
        P.op("pool", lambda e: e.dma_start(out=wo, in_=wout_d.rearrange("(k p) n -> p k n", p=128)), writes=["wo"], dma=True)
        for blk in range(NOWN // TB):
            r0 = blk * TB
            for t in range(4):
                P.op("sp", lambda e, t=t, r0=r0: e.dma_start(out=xt[t], in_=x1_d[r0 + t * 128:r0 + (t + 1) * 128, :]),
                     writes=[xn[t]], dma=True)
            P.op("sp", lambda e, r0=r0: e.dma_start(out=FTb, in_=ft_d[:, r0:r0 + TB].rearrange("(c p) n -> p c n", p=128)),
                 writes=["FTb"], dma=True)
            P.op("sp", lambda e, r0=r0: e.dma_start(out=OTb, in_=ot_d[:, r0:r0 + TB].rearrange("(c p) n -> p c n", p=128)),
                 writes=["OTb"], dma=True)
            norm_to_hT(xt, xn, 1, hT, "hT", scr)
            for c in range(8):
                par = c % 2
                pga = P.bank(par * 2)
                pgb = P.bank(par * 2 + 1)
                pya = P.bank(4 + par * 2)
                pyb = P.bank(5 + par * 2)
                for k in range(8):
                    P.op("pe", lambda e, c=c, k=k, pga=pga: e.matmul(
                        pga, lhsT=wG[:, k, c * 128:(c + 1) * 128], rhs=hT[:, k, :], start=(k == 0), stop=(k == 7)),
                        reads=["wG", "hT"], writes=[PSB[par * 2]])
                for k in range(8):
                    P.op("pe", lambda e, c=c, k=k, pgb=pgb: e.matmul(
                        pgb, lhsT=wG[:, k, 1024 + c * 128:1024 + (c + 1) * 128], rhs=hT[:, k, :], start=(k == 0), stop=(k == 7)),
                        reads=["wG", "hT"], writes=[PSB[par * 2 + 1]])
                for k in range(4):
                    P.op("pe", lambda e, c=c, k=k, pya=pya: e.matmul(
                        pya, lhsT=wfo[:, k, c * 128:(c + 1) * 128], rhs=FTb[:, k, :], start=(k == 0), stop=(k == 3)),
                        reads=["wfo", "FTb"], writes=[PSB[4 + par * 2]])
                for k in range(4):
                    P.op("pe", lambda e, c=c, k=k, pyb=pyb: e.matmul(
                        pyb, lhsT=wmo[:, k, c * 128:(c + 1) * 128], rhs=OTb[:, k, :], start=(k == 0), stop=(k == 3)),
                        reads=["wmo", "OTb"], writes=[PSB[5 + par * 2]])
                P.op("act", lambda e, par=par, pga=pga: e.activation(out=sga[par], in_=pga, func=AF.Sigmoid),
                     reads=[PSB[par * 2]], writes=["sga%d" % par])
                P.op("act", lambda e, par=par, pgb=pgb: e.activation(out=sgb[par], in_=pgb, func=AF.Sigmoid),
                     reads=[PSB[par * 2 + 1]], writes=["sgb%d" % par])
                P.op("dve", lambda e, par=par, pya=pya: e.tensor_tensor(out=u1, in0=pya, in1=sga[par], op=ALU.mult),
                     reads=[PSB[4 + par * 2], "sga%d" % par], writes=["u1"])
                P.op("dve", lambda e, par=par, pyb=pyb: e.tensor_tensor(out=u2, in0=pyb, in1=sgb[par], op=ALU.mult),
                     reads=[PSB[5 + par * 2], "sgb%d" % par], writes=["u2"])
                P.op("pool", lambda e, c=c: e.tensor_tensor(out=mT[:, c, :], in0=u1, in1=u2, op=ALU.add),
                     reads=["u1", "u2"], writes=["mT"])
            for t in range(4):
                b0 = 4 + 2 * (t % 2)
                pd = P.bank(b0, F32, 2)
                for c in range(8):
                    for hf in range(2):
                        P.op("pe", lambda e, t=t, c=c, hf=hf, pd=pd: e.matmul(
                            pd[:, hf * 512:(hf + 1) * 512], lhsT=mT[:, c, t * 128:(t + 1) * 128],
                            rhs=wo[:, c, hf * 512:(hf + 1) * 512], start=(c == 0), stop=(c == 7)),
                            reads=["mT", "wo"], writes=[PSB[b0 + hf]])
                P.op("dve", lambda e, pd=pd: e.tensor_tensor(out=tmp, in0=pd, in1=g2b, op=ALU.mult),
                     reads=[PSB[b0], PSB[b0 + 1], "g2b"], writes=["tmp"])
                P.op("pool", lambda e, t=t: e.tensor_tensor(out=xt[t], in0=xt[t], in1=tmp, op=ALU.add),
                     reads=[xn[t], "tmp"], writes=[xn[t]])
                P.op("sp", lambda e, t=t, r0=r0: e.dma_start(out=x2_d[r0 + t * 128:r0 + (t + 1) * 128, :], in_=xt[t]),
                     reads=[xn[t]], dma=True)

    phases.append(phase_a2)
    phases.append(phase_attn)
    phases.append(phase_fourier)
    phases.append(phase_c1)
    phases.append(lambda: ffn_phase(x2_d, out_d, NOWN // TB, f2g_d, f2u_d, f2d_d, 2, 64 * 128, True))

    for i, ph in enumerate(phases):
        if i > stop_after:
            break
        ph()
        P.barrier()
        P.release()
    P.emit()
    return nc


def _pp(v, n):
    return np.ascontiguousarray(v.reshape(n, 128).T).astype(np.float32)


def _consts(p):
    t = {}
    c = np.arange(128, dtype=np.float64)
    ang = 2 * np.pi * np.outer(c, c) / 128.0
    t["t_fc"] = np.concatenate([np.cos(ang), -np.sin(ang)], axis=1).astype(np.float32) / 1024.0
    j = np.arange(128)
    s1 = np.where(j < 64, 2 * j + p, 2 * (j - 64) + (1 - p)).astype(np.float64)
    k1 = (64 * p + np.arange(64)).astype(np.float64)
    a = 2 * np.pi * np.outer(s1, k1) / 128.0
    C, Sn = np.cos(a), np.sin(a)
    t["t_f128r"] = np.concatenate([C, -Sn], axis=1).astype(np.float32)
    t["t_f128i"] = np.concatenate([Sn, C], axis=1).astype(np.float32)
    s2 = np.arange(64, dtype=np.float64)
    k = (64 * p + np.arange(64)[None, :] + 128 * np.arange(64)[:, None]).reshape(-1).astype(np.float64)
    a = 2 * np.pi * np.outer(s2, k) / 8192.0
    t["t_cos"] = np.cos(a).astype(np.float32)
    t["t_sin"] = np.sin(a).astype(np.float32)
    half = 16
    inv = (1.0 / (np.float32(10000.0) ** (np.arange(half, dtype=np.float32) * np.float32(2.0) / np.float32(32)))).astype(np.float32)
    r = np.zeros((32, 4), np.float32)
    r[:, 0] = np.concatenate([inv, inv])
    r[:16, 1] = -2.0 * np.pi
    r[16:, 1] = 2.0 * np.pi
    t["t_rope"] = r
    return t


def _own_perm(p):
    k2 = np.arange(64)[:, None]
    own = (128 * k2 + 64 * p + np.arange(64)[None, :]).reshape(-1)
    oth = (128 * k2 + 64 * (1 - p) + np.arange(64)[None, :]).reshape(-1)
    return own, oth


def prep_inputs(x, c, positions, ada_w, ada_b, ffn1_norm, ffn1_w_gate, ffn1_w_up, ffn1_w_down,
                mix_norm, w_in, q_norm, w_q_up, kv_norm, w_kv_up, w_fourier_out, w_mla_out, w_out,
                ffn2_norm, ffn2_w_gate, ffn2_w_up, ffn2_w_down, final_norm, cores=range(8)):
    f = lambda a: np.ascontiguousarray(np.asarray(a))
    x, c, positions = f(x), f(c), f(positions)
    w_in0 = f(w_in)[0]
    swap = np.concatenate([np.arange(16, 32), np.arange(0, 16)])
    kr = w_in0[:, 1152:1184]
    winA = np.zeros((1024, 1280), np.float32)
    winA[:, 0:1184] = w_in0[:, 0:1184]
    winA[:, 1184:1216] = kr[:, swap]
    wq0 = f(w_q_up)[0].reshape(384, 8, 96)
    wq = np.zeros((384, 8, 128), np.float32)
    wq[:, :, 0:64] = wq0[:, :, 0:64]
    wq[:, :, 64:96] = wq0[:, :, 64:96]
    wq[:, :, 96:128] = wq0[:, :, 64:96][:, :, swap]
    wkv0 = f(w_kv_up)[0].reshape(256, 8, 128)
    shared = {
        "ada_w": f(ada_w)[0], "ada_b_pp": _pp(f(ada_b)[0], 72),
        "norms_pp": np.concatenate([_pp(f(ffn1_norm)[0], 8), _pp(f(mix_norm)[0], 8), _pp(f(ffn2_norm)[0], 8)], axis=1),
        "final_norm": f(final_norm).reshape(1, 1024),
        "f1_wg": f(ffn1_w_gate)[0], "f1_wu": f(ffn1_w_up)[0], "f1_wd": f(ffn1_w_down)[0],
        "f2_wg": f(ffn2_w_gate)[0], "f2_wu": f(ffn2_w_up)[0], "f2_wd": f(ffn2_w_down)[0],
        "w_inA": winA, "w_inG": np.ascontiguousarray(w_in0[:, 1184:3232]),
        "qkn_pp": np.concatenate([_pp(f(q_norm)[0], 3), _pp(f(kv_norm)[0], 2)], axis=1),
        "w_q": np.ascontiguousarray(wq.reshape(384, 1024)),
        "w_kn": np.ascontiguousarray(wkv0[:, :, 0:64].reshape(256, 512)),
        "w_v": np.ascontiguousarray(wkv0[:, :, 64:128].reshape(256, 512)),
        "w_fo": f(w_fourier_out)[0], "w_mo": f(w_mla_out)[0], "w_out": f(w_out)[0],
    }
    maps = []
    for core in cores:
        b, p = core // 2, core % 2
        own, oth = _own_perm(p)
        perm = np.concatenate([own, oth])
        m = dict(shared)
        m["x"] = np.ascontiguousarray(x[b][perm])
        m["pos"] = np.ascontiguousarray(positions[b][perm]).reshape(1, 8192).astype(np.int32)
        m["c_pp"] = _pp(c[b], 8)
        m.update(_consts(p))
        maps.append(m)
    return maps


_NC_CACHE = {}


def kernel(**inputs):
    if "nc" not in _NC_CACHE:
        _NC_CACHE["nc"] = build()
    nc = _NC_CACHE["nc"]
    maps = prep_inputs(**inputs)
    res = run_bass_kernel_spmd(nc, maps, core_ids=list(range(8)))
    out = np.zeros((4, 8192, 1024), np.float32)
    for core in range(8):
        b, p = core // 2, core % 2
        own, _ = _own_perm(p)
        out[b][own] = res.results[core]["out"]
    return out
```

```python
import ml_dtypes
from concourse.bass_utils import run_bass_kernel_spmd
import numpy as np
import concourse.bass as bass
import concourse.mybir as mybir

F32 = mybir.dt.float32
BF16 = mybir.dt.bfloat16
I32 = mybir.dt.int32
U8 = mybir.dt.uint8
AF = mybir.ActivationFunctionType
ALU = mybir.AluOpType
AX = mybir.AxisListType
DTSIZE = {F32: 4, BF16: 2, I32: 4, U8: 1}


class Buf:
    __slots__ = ("name", "w", "rs", "rd")

    def __init__(self, name):
        self.name = name
        self.w = None
        self.rs = {}
        self.rd = []


class Op:
    __slots__ = ("eng", "fn", "seq", "signal", "is_dma", "dsem", "dval", "waits",
                 "cnt", "phase", "edeps", "ddeps")


class Prog:
    ENGS = ("pe", "act", "dve", "pool", "sp")
    KDMA = 12

    def __init__(self, nc):
        self.nc = nc
        self.q = {e: [] for e in self.ENGS}
        self.phase = 0
        self.esem = {}
        for e in ("pe", "act", "dve", "pool"):
            self.esem[e] = nc.alloc_semaphore("s_" + e)
        self.bar_sem = nc.alloc_semaphore("s_bar")
        self.bar_cnt = 0
        self.dsems = {}
        self.dcount = {}
        self.dma_ops = {}
        for e in ("sp", "act", "pool"):
            self.dsems[e] = [nc.alloc_semaphore("d_%s_%d" % (e, i)) for i in range(self.KDMA)]
            self.dcount[e] = 0
            self.dma_ops[e] = []
        self.waited = {}
        self.dwaited = {}
        self.bar_wait_pending = {e: 0 for e in self.ENGS}
        self.arena = None
        self.sb_off = 0
        self.sb_mark = 0
        self.sb_cap = 0
        self.bufs = {}

    def init_mem(self, sbuf_bytes=206 * 1024):
        nc = self.nc
        self.arena = nc.alloc_sbuf_tensor("arena", [128, sbuf_bytes], U8)
        self.sb_cap = sbuf_bytes
        self.psum = nc.alloc_psum_tensor("psum", [128, 4096], F32)

    def sb(self, shape, dtype, name=None):
        n = 1
        for s in shape[1:]:
            n *= s
        nbytes = n * DTSIZE[dtype]
        off = (self.sb_off + 63) // 64 * 64
        assert off + nbytes <= self.sb_cap, "SBUF overflow: %s need %d at %d" % (name, nbytes, off)
        self.sb_off = off + nbytes
        ap = self.arena[0:shape[0], off:off + nbytes].bitcast(dtype)
        if len(shape) > 2:
            names = " ".join("d%d" % i for i in range(1, len(shape)))
            kw = {"d%d" % i: shape[i] for i in range(1, len(shape))}
            ap = ap.rearrange("p (%s) -> p %s" % (names, names), **kw)
        return ap

    def mark(self):
        self.sb_mark = self.sb_off

    def release(self):
        self.sb_off = self.sb_mark

    def bank(self, b, dtype=F32, nb=1):
        ap = self.psum[:, b * 512:(b + nb) * 512]
        if dtype != F32:
            ap = ap.bitcast(dtype)
        return ap

    def buf(self, name):
        b = self.bufs.get(name)
        if b is None:
            b = Buf(name)
            self.bufs[name] = b
        return b

    def _mk(self, eng, fn, dma):
        op = Op()
        op.eng = eng
        op.fn = fn
        op.seq = len(self.q[eng])
        op.signal = False
        op.is_dma = dma
        op.dsem = None
        op.dval = 0
        op.waits = []
        op.cnt = 0
        op.phase = self.phase
        op.edeps = {}
        op.ddeps = []
        return op

    def op(self, eng, fn, reads=(), writes=(), dma=False):
        op = self._mk(eng, fn, dma)
        deps = []
        for b in reads:
            if isinstance(b, str):
                b = self.buf(b)
            if b.w is not None:
                deps.append((b.w, True))
        for b in writes:
            if isinstance(b, str):
                b = self.buf(b)
            if b.w is not None:
                deps.append((b.w, True))
            for r in b.rs.values():
                deps.append((r, False))
            for r in b.rd:
                deps.append((r, False))
        best = {}
        for d, strong in deps:
            if d is op or d.phase != self.phase:
                continue
            if d.is_dma:
                key = (eng, id(d))
                if key in self.dwaited:
                    continue
                self.dwaited[key] = True
                op.ddeps.append(d)
            else:
                if d.eng == eng and eng == "pe":
                    continue
                b = best.get(d.eng)
                if b is None or b.seq < d.seq:
                    best[d.eng] = d
        for te, d in best.items():
            key = (eng, te)
            if self.waited.get(key, -1) >= d.seq:
                continue
            self.waited[key] = d.seq
            d.signal = True
            op.edeps[te] = d
        if dma:
            i = self.dcount[eng]
            self.dcount[eng] = i + 1
            K = self.KDMA
            op.dsem = self.dsems[eng][i % K]
            op.dval = 16 * (i // K + 1)
            if i >= K:
                old = self.dma_ops[eng][i - K]
                key = (eng, id(old))
                if key not in self.dwaited:
                    self.dwaited[key] = True
                    op.ddeps.append(old)
            self.dma_ops[eng].append(op)
        if self.bar_wait_pending[eng]:
            op.waits.append((self.bar_sem, self.bar_wait_pending[eng]))
            self.bar_wait_pending[eng] = 0
        for b in writes:
            if isinstance(b, str):
                b = self.buf(b)
            b.w = op
            b.rs = {}
            b.rd = []
        for b in reads:
            if isinstance(b, str):
                b = self.buf(b)
            if b.w is not op:
                if dma:
                    b.rd.append(op)
                else:
                    b.rs[eng] = op
        self.q[eng].append(op)
        return op

    def barrier(self):
        sp_op = self._mk("sp", None, False)
        for e in ("pe", "act", "dve", "pool"):
            if self.q[e]:
                last = self.q[e][-1]
                if last.is_dma:
                    for o in reversed(self.q[e]):
                        if not o.is_dma:
                            last = o
                            break
                if not last.is_dma:
                    last.signal = True
                    sp_op.edeps[e] = last
        for e in ("sp", "act", "pool"):
            n = self.dcount[e]
            for o in self.dma_ops[e][max(0, n - self.KDMA):]:
                sp_op.ddeps.append(o)
        self.bar_cnt += 1
        bc = self.bar_cnt
        bs = self.bar_sem
        sp_op.fn = lambda eng: eng.sem_inc(bs, 1)
        if self.bar_wait_pending["sp"]:
            sp_op.waits.append((self.bar_sem, self.bar_wait_pending["sp"]))
            self.bar_wait_pending["sp"] = 0
        self.q["sp"].append(sp_op)
        for e in ("pe", "act", "dve", "pool"):
            self.bar_wait_pending[e] = bc
        self.phase += 1
        for b in self.bufs.values():
            b.w = None
            b.rs = {}
            b.rd = []

    def emit(self):
        nc = self.nc
        for e in ("pe", "act", "dve", "pool"):
            c = 0
            for o in self.q[e]:
                if o.signal:
                    c += 1
                    o.cnt = c
        esem = self.esem

        def run(eng_name, eng):
            for o in self.q[eng_name]:
                for (s, v) in o.waits:
                    eng.wait_ge(s, v)
                for te, d in o.edeps.items():
                    eng.wait_ge(esem[te], d.cnt)
                for d in o.ddeps:
                    eng.wait_ge(d.dsem, d.dval)
                inst = o.fn(eng)
                if o.signal:
                    inst.then_inc(esem[eng_name], 1)
                if o.is_dma:
                    inst.then_inc(o.dsem, 16)

        with nc.Block() as block:
            @block.tensor
            def _(e):
                run("pe", e)

            @block.scalar
            def _(e):
                run("act", e)

            @block.vector
            def _(e):
                run("dve", e)

            @block.gpsimd
            def _(e):
                run("pool", e)

            @block.sync
            def _(e):
                run("sp", e)

D = 1024
DFF = 2816
NJ = DFF // 128
S = 8192
NOWN = 4096
TB = 512
EPS = 1e-6
NH = 8
SM_SCALE = 96.0 ** -0.5


def build(stop_after=99, dbg=()):
    nc = bass.Bass("TRN2", target_bir_lowering=False)
    P = Prog(nc)
    P.init_mem()

    def din(name, shape, dt=F32):
        return nc.dram_tensor(name, list(shape), dt, kind="ExternalInput").ap()

    def dscr(name, shape, dt):
        kind = "ExternalOutput" if name in dbg else "Internal"
        return nc.dram_tensor(name, list(shape), dt, kind=kind).ap()

    x_d = din("x", [S, D])
    pos_d = din("pos", [1, S], I32)
    cpp_d = din("c_pp", [128, 8])
    adaw_d = din("ada_w", [D, 9 * D])
    adab_d = din("ada_b_pp", [128, 72])
    norms_d = din("norms_pp", [128, 24])
    fnorm_d = din("final_norm", [1, D])
    f1g_d = din("f1_wg", [D, DFF])
    f1u_d = din("f1_wu", [D, DFF])
    f1d_d = din("f1_wd", [DFF, D])
    f2g_d = din("f2_wg", [D, DFF])
    f2u_d = din("f2_wu", [D, DFF])
    f2d_d = din("f2_wd", [DFF, D])
    winA_d = din("w_inA", [D, 1280])
    winG_d = din("w_inG", [D, 2048])
    qkn_d = din("qkn_pp", [128, 5])
    wq_d = din("w_q", [384, 1024])
    wkn_d = din("w_kn", [256, 512])
    wv_d = din("w_v", [256, 512])
    wfo_d = din("w_fo", [512, D])
    wmo_d = din("w_mo", [512, D])
    wout_d = din("w_out", [D, D])
    fc_d = din("t_fc", [128, 256])
    f128r_d = din("t_f128r", [128, 128])
    f128i_d = din("t_f128i", [128, 128])
    tcos_d = din("t_cos", [64, 4096])
    tsin_d = din("t_sin", [64, 4096])
    ropec_d = din("t_rope", [32, 4])

    out_d = nc.dram_tensor("out", [NOWN, D], F32, kind="ExternalOutput").ap()
    mods_d = dscr("mods", [1, 9 * D], F32)
    x1_d = dscr("x1s", [S, D], F32)
    zu_d = dscr("zu", [512, S], BF16)
    zl_d = dscr("zl", [768, S], F32)
    ft_d = dscr("fts", [512, NOWN], BF16)
    ot_d = dscr("ots", [512, NOWN], BF16)
    x2_d = dscr("x2s", [NOWN, D], F32)

    ident_f = P.sb([128, 128], F32)
    ident_b = P.sb([128, 128], BF16)
    ones_b = P.sb([128, 128], BF16)
    epsb = P.sb([128, 1], F32)
    modpp = P.sb([128, 72], F32)
    gv = P.sb([128, 24], F32)
    normspp = P.sb([128, 24], F32)
    P.op("pool", lambda e: e.memset(ident_f, 0.0), writes=["ident_f"])
    P.op("pool", lambda e: e.affine_select(out=ident_f, in_=ident_f, pattern=[[-1, 128]],
                                          compare_op=ALU.not_equal, fill=1.0, base=0,
                                          channel_multiplier=1),
         reads=["ident_f"], writes=["ident_f"])
    P.op("dve", lambda e: e.tensor_copy(out=ident_b, in_=ident_f), reads=["ident_f"], writes=["ident_b"])
    P.op("dve", lambda e: e.memset(ones_b, 1.0), writes=["ones_b"])
    P.op("dve", lambda e: e.memset(epsb, EPS), writes=["epsb"])
    P.mark()

    PSB = ["psb%d" % i for i in range(8)]

    def phase_adaln():
        cpp = P.sb([128, 8], F32)
        cact = P.sb([128, 8], BF16)
        adab = P.sb([128, 72], F32)
        modT = P.sb([128, 128], F32)
        wblk = [P.sb([128, 8, 1024], BF16) for _ in range(2)]
        P.op("sp", lambda e: e.dma_start(out=cpp, in_=cpp_d), writes=["cpp"], dma=True)
        P.op("sp", lambda e: e.dma_start(out=adab, in_=adab_d), writes=["adab"], dma=True)
        P.op("sp", lambda e: e.dma_start(out=normspp, in_=norms_d), writes=["normspp"], dma=True)
        P.op("act", lambda e: e.activation(out=cact, in_=cpp, func=AF.Silu), reads=["cpp"], writes=["cact"])
        ps0 = P.bank(0)
        for blk in range(9):
            wb = wblk[blk % 2]
            wn = "wblk%d" % (blk % 2)
            src = adaw_d[:, blk * 1024:(blk + 1) * 1024].rearrange("(k p) n -> p k n", p=128)
            P.op("pool", lambda e, wb=wb, src=src: e.dma_start(out=wb, in_=src), writes=[wn], dma=True)
            for jj in range(8):
                j = blk * 8 + jj
                for k in range(8):
                    P.op("pe", lambda e, wb=wb, j=j, jj=jj, k=k: e.matmul(
                        ps0[:, j:j + 1], lhsT=wb[:, k, jj * 128:(jj + 1) * 128], rhs=cact[:, k:k + 1],
                        start=(k == 0), stop=(k == 7)),
                        reads=[wn, "cact"], writes=[PSB[0]])
        P.op("dve", lambda e: e.tensor_tensor(out=modpp, in0=ps0[:, 0:72], in1=adab, op=ALU.add),
             reads=[PSB[0], "adab"], writes=["modpp"])
        for i in range(3):
            sc = modpp[:, i * 24 + 8:i * 24 + 16]
            P.op("dve", lambda e, i=i, sc=sc: e.scalar_tensor_tensor(
                out=gv[:, i * 8:(i + 1) * 8], in0=sc, scalar=1.0, in1=normspp[:, i * 8:(i + 1) * 8],
                op0=ALU.add, op1=ALU.mult),
                reads=["modpp", "normspp"], writes=["gv"])
        ps1 = P.bank(1)
        P.op("pe", lambda e: e.transpose(ps1[0:72, 0:128], modpp[:, 0:72], ident_f),
             reads=["modpp", "ident_f"], writes=[PSB[1]])
        P.op("dve", lambda e: e.tensor_copy(out=modT[0:72, :], in_=ps1[0:72, 0:128]),
             reads=[PSB[1]], writes=["modT"])
        P.op("sp", lambda e: e.dma_start(
            out=mods_d.rearrange("o (j f) -> (o j) f", f=128), in_=modT[0:72, :]),
            reads=["modT"], dma=True)

    def load_bcast(dst, name, off):
        P.op("sp", lambda e: e.dma_start(out=dst, in_=mods_d[0:1, off:off + D].broadcast_to([128, D])),
             writes=[name], dma=True)

    def norm_to_hT(xt, xnames, ni, hT, hname, scr):
        ss, rstd, xs, junk = scr["ss"], scr["rstd"], scr["xs"], scr["junk"]
        SSN = ["ss0", "ss1", "ss2", "ss3"]
        P.op("dve", lambda e: e.memset(ss, 0.0), writes=SSN)
        for t in range(4):
            P.op("act", lambda e, t=t: e.activation(out=junk, in_=xt[t], func=AF.Square,
                                                    accum_out=ss[:, t:t + 1]),
                 reads=[xnames[t]], writes=[SSN[t], "junk"])
        P.op("act", lambda e: e.activation(out=rstd, in_=ss, func=AF.Sqrt, scale=1.0 / D, bias=epsb),
             reads=SSN + ["epsb"], writes=["rstd"])
        P.op("dve", lambda e: e.reciprocal(out=rstd, in_=rstd), reads=["rstd"], writes=["rstd"])
        for t in range(4):
            P.op("act", lambda e, t=t: e.activation(out=xs[t % 2], in_=xt[t], func=AF.Copy,
                                                    scale=rstd[:, t:t + 1]),
                 reads=[xnames[t], "rstd"], writes=["xs%d" % (t % 2)])
            for c in range(8):
                pb = P.bank(4 + c // 2, BF16)
                P.op("pe", lambda e, t=t, c=c, pb=pb: e.transpose(
                    pb[:, (c % 2) * 512 + t * 128:(c % 2) * 512 + (t + 1) * 128],
                    xs[t % 2][:, c * 128:(c + 1) * 128], ident_b),
                    reads=["xs%d" % (t % 2), "ident_b"], writes=[PSB[4 + c // 2]])
        for c in range(8):
            pb = P.bank(4 + c // 2, BF16)
            P.op("dve", lambda e, c=c, pb=pb: e.tensor_scalar(
                out=hT[:, c, :], in0=pb[:, (c % 2) * 512:(c % 2 + 1) * 512],
                scalar1=gv[:, ni * 8 + c:ni * 8 + c + 1],
                scalar2=modpp[:, ni * 24 + c:ni * 24 + c + 1],
                op0=ALU.mult, op1=ALU.add),
                reads=[PSB[4 + c // 2], "gv", "modpp"], writes=[hname])

    def norm_scratch():
        return {"ss": P.sb([128, 4], F32), "rstd": P.sb([128, 4], F32),
                "xs": [P.sb([128, D], BF16) for _ in range(2)],
                "junk": P.sb([128, D], BF16)}

    def ffn_phase(src_d, dst_d, nblk, wg_d, wu_d, wd_d, ni, goff, final):
        wg = P.sb([128, 8, DFF], BF16)
        wu = P.sb([128, 8, DFF], BF16)
        wd = P.sb([128, NJ, D], BF16)
        xt = [P.sb([128, D], F32) for _ in range(4)]
        xn = ["xt%d" % t for t in range(4)]
        scr = norm_scratch()
        hT = P.sb([128, 8, TB], BF16)
        aT = P.sb([128, NJ, TB], BF16)
        sg = [P.sb([128, TB], BF16) for _ in range(2)]
        tmp = P.sb([128, D], F32)
        ghb = P.sb([128, D], F32)
        if final:
            fnb = P.sb([128, D], F32)
            ss2 = P.sb([128, 1], F32)
            P.op("sp", lambda e: e.dma_start(out=fnb, in_=fnorm_d.broadcast_to([128, D])),
                 writes=["fnb"], dma=True)
        load_bcast(ghb, "ghb", goff)
        P.op("dve", lambda e: e.tensor_scalar(out=ghb, in0=ghb, scalar1=0.5, scalar2=None, op0=ALU.mult),
             reads=["ghb"], writes=["ghb"])
        for i in range(NJ // 2):
            cs = slice(i * 256, (i + 1) * 256)
            P.op("pool", lambda e, cs=cs: e.dma_start(
                out=wg[:, :, cs], in_=wg_d[:, cs].rearrange("(k p) n -> p k n", p=128)),
                writes=["wg%d" % i], dma=True)
            P.op("pool", lambda e, cs=cs: e.dma_start(
                out=wu[:, :, cs], in_=wu_d[:, cs].rearrange("(k p) n -> p k n", p=128)),
                writes=["wu%d" % i], dma=True)
        for i in range(NJ // 2):
            P.op("pool", lambda e, i=i: e.dma_start(
                out=wd[:, 2 * i:2 * i + 2, :],
                in_=wd_d[i * 256:(i + 1) * 256, :].rearrange("(j p) n -> p j n", p=128)),
                writes=["wd%d" % i], dma=True)
        for blk in range(nblk):
            r0 = blk * TB
            for t in range(4):
                P.op("sp", lambda e, t=t, r0=r0: e.dma_start(out=xt[t], in_=src_d[r0 + t * 128:r0 + (t + 1) * 128, :]),
                     writes=[xn[t]], dma=True)
            norm_to_hT(xt, xn, ni, hT, "hT", scr)
            for j in range(NJ):
                pg = P.bank(j % 2)
                pu = P.bank(2 + j % 2)
                for k in range(8):
                    P.op("pe", lambda e, j=j, k=k, pg=pg: e.matmul(
                        pg, lhsT=wg[:, k, j * 128:(j + 1) * 128], rhs=hT[:, k, :], start=(k == 0), stop=(k == 7)),
                        reads=["wg%d" % (j // 2), "hT"], writes=[PSB[j % 2]])
                for k in range(8):
                    P.op("pe", lambda e, j=j, k=k, pu=pu: e.matmul(
                        pu, lhsT=wu[:, k, j * 128:(j + 1) * 128], rhs=hT[:, k, :], start=(k == 0), stop=(k == 7)),
                        reads=["wu%d" % (j // 2), "hT"], writes=[PSB[2 + j % 2]])
                P.op("act", lambda e, j=j, pg=pg: e.activation(out=sg[j % 2], in_=pg, func=AF.Silu),
                     reads=[PSB[j % 2]], writes=["sg%d" % (j % 2)])
                P.op("dve", lambda e, j=j, pu=pu: e.tensor_tensor(out=aT[:, j, :], in0=pu, in1=sg[j % 2], op=ALU.mult),
                     reads=[PSB[2 + j % 2], "sg%d" % (j % 2)], writes=["aT"])
            for t in range(4):
                b0 = 4 + 2 * (t % 2)
                pd = P.bank(b0, F32, 2)
                for j in range(NJ):
                    for hf in range(2):
                        P.op("pe", lambda e, t=t, j=j, hf=hf, pd=pd: e.matmul(
                            pd[:, hf * 512:(hf + 1) * 512], lhsT=aT[:, j, t * 128:(t + 1) * 128],
                            rhs=wd[:, j, hf * 512:(hf + 1) * 512], start=(j == 0), stop=(j == NJ - 1)),
                            reads=["aT", "wd%d" % (j // 2)], writes=[PSB[b0 + hf]])
                P.op("dve", lambda e, pd=pd: e.tensor_tensor(out=tmp, in0=pd, in1=ghb, op=ALU.mult),
                     reads=[PSB[b0], PSB[b0 + 1], "ghb"], writes=["tmp"])
                P.op("pool", lambda e, t=t: e.tensor_tensor(out=xt[t], in0=xt[t], in1=tmp, op=ALU.add),
                     reads=[xn[t], "tmp"], writes=[xn[t]])
                if final:
                    P.op("dve", lambda e: e.memset(ss2, 0.0), writes=["ss2"])
                    P.op("act", lambda e, t=t: e.activation(out=tmp, in_=xt[t], func=AF.Square, accum_out=ss2),
                         reads=[xn[t], "ss2"], writes=["tmp", "ss2"])
                    P.op("act", lambda e: e.activation(out=ss2, in_=ss2, func=AF.Sqrt, scale=1.0 / D, bias=epsb),
                         reads=["ss2", "epsb"], writes=["ss2"])
                    P.op("dve", lambda e: e.reciprocal(out=ss2, in_=ss2), reads=["ss2"], writes=["ss2"])
                    P.op("dve", lambda e, t=t: e.scalar_tensor_tensor(
                        out=xt[t], in0=xt[t], scalar=ss2, in1=fnb, op0=ALU.mult, op1=ALU.mult),
                        reads=[xn[t], "ss2", "fnb"], writes=[xn[t]])
                P.op("sp", lambda e, t=t, r0=r0: e.dma_start(out=dst_d[r0 + t * 128:r0 + (t + 1) * 128, :], in_=xt[t]),
                     reads=[xn[t]], dma=True)

    phases = []
    phases.append(phase_adaln)
    phases.append(lambda: ffn_phase(x_d, x1_d, S // TB, f1g_d, f1u_d, f1d_d, 0, 16 * 128, False))
    def phase_a2():
        wA = P.sb([128, 8, 1280], BF16)
        xt = [P.sb([128, D], F32) for _ in range(4)]
        xn = ["xt%d" % t for t in range(4)]
        scr = norm_scratch()
        hT = P.sb([128, 8, TB], BF16)
        zu_sb = P.sb([128, 4, TB], BF16)
        zl_sb = P.sb([128, 6, TB], F32)
        for k in range(8):
            P.op("pool", lambda e, k=k: e.dma_start(out=wA[:, k, :], in_=winA_d[k * 128:(k + 1) * 128, :]),
                 writes=["wA"], dma=True)
        for blk in range(S // TB):
            r0 = blk * TB
            for t in range(4):
                P.op("sp", lambda e, t=t, r0=r0: e.dma_start(out=xt[t], in_=x1_d[r0 + t * 128:r0 + (t + 1) * 128, :]),
                     writes=[xn[t]], dma=True)
            norm_to_hT(xt, xn, 1, hT, "hT", scr)
            for cc in range(10):
                pb = P.bank(cc % 4)
                for k in range(8):
                    P.op("pe", lambda e, cc=cc, k=k, pb=pb: e.matmul(
                        pb, lhsT=wA[:, k, cc * 128:(cc + 1) * 128], rhs=hT[:, k, :], start=(k == 0), stop=(k == 7)),
                        reads=["wA", "hT"], writes=[PSB[cc % 4]])
                if cc < 4:
                    P.op("act", lambda e, cc=cc, pb=pb: e.activation(out=zu_sb[:, cc, :], in_=pb, func=AF.Copy),
                         reads=[PSB[cc % 4]], writes=["zu_sb"])
                else:
                    P.op("dve", lambda e, cc=cc, pb=pb: e.tensor_copy(out=zl_sb[:, cc - 4, :], in_=pb),
                         reads=[PSB[cc % 4]], writes=["zl_sb"])
            P.op("sp", lambda e, r0=r0: e.dma_start(
                out=zu_d[:, r0:r0 + TB].rearrange("(c p) n -> p c n", p=128), in_=zu_sb),
                reads=["zu_sb"], dma=True)
            P.op("sp", lambda e, r0=r0: e.dma_start(
                out=zl_d[:, r0:r0 + TB].rearrange("(c p) n -> p c n", p=128), in_=zl_sb),
                reads=["zl_sb"], dma=True)

    def phase_attn():
        ropec = P.sb([32, 4], F32)
        qkn = P.sb([128, 5], F32)
        cosT = P.sb([32, S], BF16)
        ssT = P.sb([32, S], BF16)
        kvnT = P.sb([128, 2, S], BF16)
        kropeT = P.sb([32, S], BF16)
        qnT = P.sb([128, 3, NOWN], BF16)
        wkn = P.sb([128, 2, 512], BF16)
        wv = P.sb([128, 2, 512], BF16)
        wq = P.sb([128, 3, 1024], BF16)
        KhT = P.sb([128, S], BF16)
        QhT = P.sb([128, NOWN], BF16)
        Vaug = P.sb([128, 64, 128], BF16)
        PT = [P.sb([128, 1024], BF16) for _ in range(3)]
        posi = P.sb([32, 1024], I32)
        ang = P.sb([32, 1024], F32)
        ang2 = P.sb([32, 1024], F32)
        lat = P.sb([128, 3, TB], F32)
        kra = P.sb([32, TB], F32)
        krb = P.sb([32, TB], F32)
        sq = P.sb([128, 3, TB], BF16)
        rstdk = P.sb([128, TB], F32)
        t1 = P.sb([32, TB], F32)
        t2 = P.sb([32, TB], F32)
        rden = P.sb([64, TB], F32)
        oT = [P.sb([64, TB], BF16) for _ in range(2)]
        P.op("sp", lambda e: e.dma_start(out=ropec, in_=ropec_d), writes=["ropec"], dma=True)
        P.op("sp", lambda e: e.dma_start(out=qkn, in_=qkn_d), writes=["qkn"], dma=True)
        P.op("pool", lambda e: e.dma_start(out=wkn, in_=wkn_d.rearrange("(k p) n -> p k n", p=128)), writes=["wkn"], dma=True)
        P.op("pool", lambda e: e.dma_start(out=wv, in_=wv_d.rearrange("(k p) n -> p k n", p=128)), writes=["wv"], dma=True)
        P.op("pool", lambda e: e.dma_start(out=wq, in_=wq_d.rearrange("(k p) n -> p k n", p=128)), writes=["wq"], dma=True)
        PI = float(np.pi)
        for ch in range(S // 1024):
            cs = slice(ch * 1024, (ch + 1) * 1024)
            P.op("sp", lambda e, cs=cs: e.dma_start(out=posi, in_=pos_d[0:1, cs].broadcast_to([32, 1024])),
                 writes=["posi"], dma=True)
            P.op("dve", lambda e: e.tensor_copy(out=ang, in_=posi), reads=["posi"], writes=["ang"])
            P.op("dve", lambda e: e.tensor_scalar(out=ang, in0=ang, scalar1=ropec[:, 0:1], scalar2=None, op0=ALU.mult),
                 reads=["ang", "ropec"], writes=["ang"])
            P.op("dve", lambda e: e.tensor_scalar(out=ang, in0=ang, scalar1=1.0 / (2 * PI), scalar2=None, op0=ALU.mult),
                 reads=["ang"], writes=["ang"])
            for which in range(2):
                if which == 1:
                    P.op("dve", lambda e: e.tensor_scalar(out=ang, in0=ang, scalar1=0.25, scalar2=None, op0=ALU.add),
                         reads=["ang"], writes=["ang"])
                P.op("dve", lambda e: e.tensor_copy(out=posi, in_=ang), reads=["ang"], writes=["posi"])
                P.op("dve", lambda e: e.tensor_copy(out=ang2, in_=posi), reads=["posi"], writes=["ang2"])
                P.op("dve", lambda e: e.tensor_tensor(out=ang2, in0=ang, in1=ang2, op=ALU.subtract),
                     reads=["ang", "ang2"], writes=["ang2"])
                if which == 0:
                    P.op("act", lambda e, cs=cs: e.activation(out=ssT[:, cs], in_=ang2, func=AF.Sin, scale=ropec[:, 1:2]),
                         reads=["ang2", "ropec"], writes=["ssT"])
                else:
                    P.op("act", lambda e, cs=cs: e.activation(out=cosT[:, cs], in_=ang2, func=AF.Sin, scale=2 * PI),
                         reads=["ang2"], writes=["cosT"])

        def latent_norm(row0, nch, nfeat, gcol, dstT, dname, blk):
            r0 = blk * TB
            P.op("sp", lambda e: e.dma_start(
                out=lat[:, 0:nch, :], in_=zl_d[row0:row0 + nch * 128, r0:r0 + TB].rearrange("(c p) n -> p c n", p=128)),
                writes=["lat"], dma=True)
            P.op("dve", lambda e: e.tensor_tensor(out=sq[:, 0:nch, :], in0=lat[:, 0:nch, :], in1=lat[:, 0:nch, :], op=ALU.mult),
                 reads=["lat"], writes=["sq"])
            pb = P.bank(0)
            for c in range(nch):
                P.op("pe", lambda e, c=c: e.matmul(pb, lhsT=ones_b, rhs=sq[:, c, :], start=(c == 0), stop=(c == nch - 1)),
                     reads=["ones_b", "sq"], writes=[PSB[0]])
            P.op("act", lambda e: e.activation(out=rstdk, in_=pb, func=AF.Sqrt, scale=1.0 / nfeat, bias=epsb),
                 reads=[PSB[0], "epsb"], writes=["rstdk"])
            P.op("dve", lambda e: e.reciprocal(out=rstdk, in_=rstdk), reads=["rstdk"], writes=["rstdk"])
            for c in range(nch):
                P.op("dve", lambda e, c=c: e.scalar_tensor_tensor(
                    out=dstT[:, c, r0:r0 + TB], in0=lat[:, c, :], scalar=qkn[:, gcol + c:gcol + c + 1], in1=rstdk,
                    op0=ALU.mult, op1=ALU.mult),
                    reads=["lat", "qkn", "rstdk"], writes=[dname])

        for blk in range(S // TB):
            r0 = blk * TB
            latent_norm(384, 2, 256, 3, kvnT, "kvnT", blk)
            P.op("sp", lambda e, r0=r0: e.dma_start(out=kra, in_=zl_d[640:672, r0:r0 + TB]), writes=["kra"], dma=True)
            P.op("sp", lambda e, r0=r0: e.dma_start(out=krb, in_=zl_d[672:704, r0:r0 + TB]), writes=["krb"], dma=True)
            P.op("dve", lambda e, r0=r0: e.tensor_tensor(out=t1, in0=kra, in1=cosT[:, r0:r0 + TB], op=ALU.mult),
                 reads=["kra", "cosT"], writes=["t1"])
            P.op("dve", lambda e, r0=r0: e.tensor_tensor(out=t2, in0=krb, in1=ssT[:, r0:r0 + TB], op=ALU.mult),
                 reads=["krb", "ssT"], writes=["t2"])
            P.op("dve", lambda e, r0=r0: e.tensor_tensor(out=kropeT[:, r0:r0 + TB], in0=t1, in1=t2, op=ALU.add),
                 reads=["t1", "t2"], writes=["kropeT"])
        for blk in range(NOWN // TB):
            latent_norm(0, 3, 384, 0, qnT, "qnT", blk)
        P.op("dve", lambda e: e.memset(Vaug[:, :, 64:128], 1.0), writes=["Vaug"])
        P.op("pool", lambda e: e.memset(KhT[96:128, :], 0.0), writes=["KhT"])
        P.op("pool", lambda e: e.memset(QhT[96:128, :], 0.0), writes=["QhT"])

        def do_head(h):
            for blk in range(S // TB):
                r0 = blk * TB
                pb = P.bank(blk % 2)
                for c in range(2):
                    P.op("pe", lambda e, c=c, r0=r0, pb=pb: e.matmul(
                        pb[0:64, :], lhsT=wkn[:, c, h * 64:(h + 1) * 64], rhs=kvnT[:, c, r0:r0 + TB],
                        start=(c == 0), stop=(c == 1)),
                        reads=["wkn", "kvnT"], writes=[PSB[blk % 2]])
                P.op("act", lambda e, r0=r0, pb=pb: e.activation(out=KhT[0:64, r0:r0 + TB], in_=pb[0:64, :], func=AF.Copy),
                     reads=[PSB[blk % 2]], writes=["KhT"])
            P.op("dve", lambda e: e.tensor_copy(out=KhT[64:96, :], in_=kropeT), reads=["kropeT"], writes=["KhT"])
            for g8 in range(8):
                pb = P.bank(2 + g8 % 2)
                for i in range(8):
                    tt = g8 * 8 + i
                    for c in range(2):
                        P.op("pe", lambda e, c=c, tt=tt, i=i, pb=pb: e.matmul(
                            pb[:, i * 64:(i + 1) * 64], lhsT=kvnT[:, c, tt * 128:(tt + 1) * 128],
                            rhs=wv[:, c, h * 64:(h + 1) * 64], start=(c == 0), stop=(c == 1)),
                            reads=["kvnT", "wv"], writes=[PSB[2 + g8 % 2]])
                P.op("dve", lambda e, g8=g8, pb=pb: e.tensor_copy(
                    out=Vaug[:, g8 * 8:(g8 + 1) * 8, 0:64], in_=pb.rearrange("p (a d) -> p a d", a=8)),
                    reads=[PSB[2 + g8 % 2]], writes=["Vaug"])
            for blk in range(NOWN // TB):
                r0 = blk * TB
                pq = P.bank(4)
                pr = P.bank(5)
                pw = P.bank(6)
                for c in range(3):
                    P.op("pe", lambda e, c=c, r0=r0: e.matmul(
                        pq[0:64, :], lhsT=wq[:, c, h * 128:h * 128 + 64], rhs=qnT[:, c, r0:r0 + TB],
                        start=(c == 0), stop=(c == 2)),
                        reads=["wq", "qnT"], writes=[PSB[4]])
                for c in range(3):
                    P.op("pe", lambda e, c=c, r0=r0: e.matmul(
                        pr[0:32, :], lhsT=wq[:, c, h * 128 + 64:h * 128 + 96], rhs=qnT[:, c, r0:r0 + TB],
                        start=(c == 0), stop=(c == 2)),
                        reads=["wq", "qnT"], writes=[PSB[5]])
                for c in range(3):
                    P.op("pe", lambda e, c=c, r0=r0: e.matmul(
                        pw[0:32, :], lhsT=wq[:, c, h * 128 + 96:h * 128 + 128], rhs=qnT[:, c, r0:r0 + TB],
                        start=(c == 0), stop=(c == 2)),
                        reads=["wq", "qnT"], writes=[PSB[6]])
                P.op("act", lambda e, r0=r0: e.activation(out=QhT[0:64, r0:r0 + TB], in_=pq[0:64, :], func=AF.Copy, scale=SM_SCALE),
                     reads=[PSB[4]], writes=["QhT"])
                P.op("dve", lambda e, r0=r0: e.scalar_tensor_tensor(
                    out=t1, in0=pr[0:32, :], scalar=SM_SCALE, in1=cosT[:, r0:r0 + TB], op0=ALU.mult, op1=ALU.mult),
                    reads=[PSB[5], "cosT"], writes=["t1"])
                P.op("dve", lambda e, r0=r0: e.scalar_tensor_tensor(
                    out=t2, in0=pw[0:32, :], scalar=SM_SCALE, in1=ssT[:, r0:r0 + TB], op0=ALU.mult, op1=ALU.mult),
                    reads=[PSB[6], "ssT"], writes=["t2"])
                P.op("dve", lambda e, r0=r0: e.tensor_tensor(out=QhT[64:96, r0:r0 + TB], in0=t1, in1=t2, op=ALU.add),
                     reads=["t1", "t2"], writes=["QhT"])
            seq = [(qb, kp) for qb in range(NOWN // TB) for kp in range(32)]

            def emit_qk(n):
                qb, kp = seq[n]
                q0 = qb * TB
                b0 = 2 * (n % 3)
                ps = P.bank(b0, F32, 2)
                for i in range(2):
                    kt = kp * 2 + i
                    P.op("pe", lambda e, kt=kt, i=i, ps=ps, q0=q0: e.matmul(
                        ps[:, i * 512:(i + 1) * 512], lhsT=KhT[:, kt * 128:(kt + 1) * 128],
                        rhs=QhT[:, q0:q0 + TB], start=True, stop=True),
                        reads=["KhT", "QhT"], writes=[PSB[b0 + i]])

            def emit_rest(n):
                qb, kp = seq[n]
                q0 = qb * TB
                b0 = 2 * (n % 3)
                ps = P.bank(b0, F32, 2)
                ptb = PT[n % 3]
                ptn = "PT%d" % (n % 3)
                po = P.bank(6 + qb % 2)
                P.op("act", lambda e, ps=ps, ptb=ptb: e.activation(out=ptb, in_=ps, func=AF.Exp),
                     reads=[PSB[b0], PSB[b0 + 1]], writes=[ptn])
                for i in range(2):
                    kt = kp * 2 + i
                    P.op("pe", lambda e, kt=kt, i=i, ptb=ptb, po=po: e.matmul(
                        po, lhsT=Vaug[:, kt, :], rhs=ptb[:, i * 512:(i + 1) * 512],
                        start=(kt == 0), stop=(kt == 63)),
                        reads=["Vaug", ptn], writes=[PSB[6 + qb % 2]])
                if kp == 31:
                    P.op("dve", lambda e, po=po: e.reciprocal(out=rden, in_=po[64:128, :]),
                         reads=[PSB[6 + qb % 2]], writes=["rden"])
                    ob = oT[qb % 2]
                    on = "oT%d" % (qb % 2)
                    P.op("dve", lambda e, po=po, ob=ob: e.tensor_tensor(out=ob, in0=po[0:64, :], in1=rden, op=ALU.mult),
                         reads=[PSB[6 + qb % 2], "rden"], writes=[on])
                    P.op("sp", lambda e, ob=ob, q0=q0: e.dma_start(out=ot_d[h * 64:(h + 1) * 64, q0:q0 + TB], in_=ob),
                         reads=[on], dma=True)

            emit_qk(0)
            for n in range(len(seq)):
                if n + 1 < len(seq):
                    emit_qk(n + 1)
                emit_rest(n)

        for h in range(NH):
            do_head(h)

    def phase_fourier():
        fc = P.sb([128, 256], BF16)
        f128r = P.sb([128, 128], BF16)
        f128i = P.sb([128, 128], BF16)
        tcos = P.sb([64, 4096], BF16)
        tsin = P.sb([64, 4096], BF16)
        uT = P.sb([128, S], BF16)
        Z = P.sb([128, 64, 256], BF16)
        W = P.sb([64, 64, 2, 128], BF16)
        FT = P.sb([128, NOWN], BF16)
        for dst, src, nm in ((fc, fc_d, "fc"), (f128r, f128r_d, "f128r"), (f128i, f128i_d, "f128i"),
                             (tcos, tcos_d, "tcos"), (tsin, tsin_d, "tsin")):
            P.op("pool", lambda e, dst=dst, src=src: e.dma_start(out=dst, in_=src), writes=[nm], dma=True)
        uTv = uT.rearrange("p (j s) -> p s j", s=64)
        tcv = tcos.rearrange("p (k2 k1) -> p k1 k2", k1=64)
        tsv = tsin.rearrange("p (k2 k1) -> p k1 k2", k1=64)
        FTv = FT.rearrange("p (k2 k1) -> p k1 k2", k1=64)
        Wv = W.rearrange("p k r c -> p c r k")
        for g in range(4):
            P.op("sp", lambda e, g=g: e.dma_start(out=uT, in_=zu_d[g * 128:(g + 1) * 128, :]), writes=["uT"], dma=True)
            for sp_ in range(32):
                pb = P.bank(sp_ % 2)
                for i in range(2):
                    s2 = sp_ * 2 + i
                    P.op("pe", lambda e, s2=s2, i=i, pb=pb: e.matmul(
                        pb[:, i * 256:(i + 1) * 256], lhsT=uTv[:, s2, :], rhs=fc, start=True, stop=True),
                        reads=["uT", "fc"], writes=[PSB[sp_ % 2]])
                eng = "act" if sp_ % 2 == 0 else "dve"
                if eng == "act":
                    P.op("act", lambda e, sp_=sp_, pb=pb: e.activation(
                        out=Z[:, 2 * sp_:2 * sp_ + 2, :], in_=pb.rearrange("p (a c) -> p a c", a=2), func=AF.Copy),
                        reads=[PSB[sp_ % 2]], writes=["Z"])
                else:
                    P.op("dve", lambda e, sp_=sp_, pb=pb: e.tensor_copy(
                        out=Z[:, 2 * sp_:2 * sp_ + 2, :], in_=pb.rearrange("p (a c) -> p a c", a=2)),
                        reads=[PSB[sp_ % 2]], writes=["Z"])
            for c4 in range(32):
                pb = P.bank(2 + c4 % 2)
                for i in range(4):
                    cp = c4 * 4 + i
                    P.op("pe", lambda e, cp=cp, i=i, pb=pb: e.matmul(
                        pb[0:64, i * 128:(i + 1) * 128], lhsT=Z[:, :, cp], rhs=f128r, start=True, stop=False),
                        reads=["Z", "f128r"], writes=[PSB[2 + c4 % 2]])
                    P.op("pe", lambda e, cp=cp, i=i, pb=pb: e.matmul(
                        pb[0:64, i * 128:(i + 1) * 128], lhsT=Z[:, :, 128 + cp], rhs=f128i, start=False, stop=True),
                        reads=["Z", "f128i"], writes=[PSB[2 + c4 % 2]])
                src = pb[0:64, :].rearrange("p (c r k) -> p c r k", c=4, r=2)
                dstv = Wv[:, c4 * 4:(c4 + 1) * 4, :, :]
                if c4 % 2 == 0:
                    P.op("act", lambda e, src=src, dstv=dstv: e.activation(out=dstv, in_=src, func=AF.Copy),
                         reads=[PSB[2 + c4 % 2]], writes=["W"])
                else:
                    P.op("dve", lambda e, src=src, dstv=dstv: e.tensor_copy(out=dstv, in_=src),
                         reads=[PSB[2 + c4 % 2]], writes=["W"])
            for k8 in range(8):
                pb = P.bank(4 + k8 % 2)
                for i in range(8):
                    k1 = k8 * 8 + i
                    P.op("pe", lambda e, k1=k1, i=i, pb=pb: e.matmul(
                        pb[:, i * 64:(i + 1) * 64], lhsT=W[:, k1, 0, :], rhs=tcv[:, k1, :], start=True, stop=False),
                        reads=["W", "tcos"], writes=[PSB[4 + k8 % 2]])
                    P.op("pe", lambda e, k1=k1, i=i, pb=pb: e.matmul(
                        pb[:, i * 64:(i + 1) * 64], lhsT=W[:, k1, 1, :], rhs=tsv[:, k1, :], start=False, stop=True),
                        reads=["W", "tsin"], writes=[PSB[4 + k8 % 2]])
                P.op("dve", lambda e, k8=k8, pb=pb: e.tensor_copy(
                    out=FTv[:, k8 * 8:(k8 + 1) * 8, :], in_=pb.rearrange("p (a k) -> p a k", a=8)),
                    reads=[PSB[4 + k8 % 2]], writes=["FT"])
            P.op("sp", lambda e, g=g: e.dma_start(out=ft_d[g * 128:(g + 1) * 128, :], in_=FT), reads=["FT"], dma=True)

    def phase_c1():
        wG = P.sb([128, 8, 2048], BF16)
        wfo = P.sb([128, 4, D], BF16)
        wmo = P.sb([128, 4, D], BF16)
        wo = P.sb([128, 8, D], BF16)
        xt = [P.sb([128, D], F32) for _ in range(4)]
        xn = ["xt%d" % t for t in range(4)]
        scr = norm_scratch()
        hT = P.sb([128, 8, TB], BF16)
        FTb = P.sb([128, 4, TB], BF16)
        OTb = P.sb([128, 4, TB], BF16)
        mT = P.sb([128, 8, TB], BF16)
        sga = [P.sb([128, TB], F32) for _ in range(2)]
        sgb = [P.sb([128, TB], F32) for _ in range(2)]
        u1 = P.sb([128, TB], F32)
        u2 = P.sb([128, TB], F32)
        tmp = P.sb([128, D], F32)
        g2b = P.sb([128, D], F32)
        load_bcast(g2b, "g2b", 40 * 128)
        for k in range(8):
            P.op("pool", lambda e, k=k: e.dma_start(out=wG[:, k, :], in_=winG_d[k * 128:(k + 1) * 128, :]), writes=["wG"], dma=True)
        P.op("pool", lambda e: e.dma_start(out=wfo, in_=wfo_d.rearrange("(k p) n -> p k n", p=128)), writes=["wfo"], dma=True)
        P.op("pool", lambda e: e.dma_start(out=wmo, in_=wmo_d.rearrange("(k p) n -> p k n", p=128)), writes=["wmo"], dma=True)
        P.op("pool", lambda e: e.dma_start(out=wo, in_=wout_d.rearrange("(k p) n -> p k n", p=128)), writes=["wo"], dma=True)
        for blk in range(NOWN // TB):
            r0 = blk * TB
            for t in range(4):
                P.op("sp", lambda e, t=t, r0=r0: e.dma_start(out=xt[t], in_=x1_d[r0 + t * 128:r0 + (t + 1) * 128, :]),
                     writes=[xn[t]], dma=True)
            P.op("sp", lambda e, r0=r0: e.dma_start(out=FTb, in_=ft_d[:, r0:r0 + TB].rearrange("(c p) n -> p c n", p=128)),
                 writes=["FTb"], dma=True)
            P.op("sp", lambda e, r0=r0: e.dma_start(out=OTb, in_=ot_d[:, r0:r0 + TB].rearrange("(c p) n -> p c n", p=128)),
                 writes=["OTb"], dma=True)
            norm_to_hT(xt, xn, 1, hT, "hT", scr)
            for c in range(8):
                par = c % 2
                pga = P.bank(par * 2)
                pgb = P.bank(par * 2 + 1)
                pya = P.bank(4 + par * 2)
                pyb = P.bank(5 + par * 2)
                for k in range(8):
                    P.op("pe", lambda e, c=c, k=k, pga=pga: e.matmul(
                        pga, lhsT=wG[:, k, c * 128:(c + 1) * 128], rhs=hT[:, k, :], start=(k == 0), stop=(k == 7)),
                        reads=["wG", "hT"], writes=[PSB[par * 2]])
                for k in range(8):
                    P.op("pe", lambda e, c=c, k=k, pgb=pgb: e.matmul(
                        pgb, lhsT=wG[:, k, 1024 + c * 128:1024 + (c + 1) * 128], rhs=hT[:, k, :], start=(k == 0), stop=(k == 7)),
                        reads=["wG", "hT"], writes=[PSB[par * 2 + 1]])
                for k in range(4):
                    P.op("pe", lambda e, c=c, k=k, pya=pya: e.matmul(
                        pya, lhsT=wfo[:, k, c * 128:(c + 1) * 128], rhs=FTb[:, k, :], start=(k == 0), stop=(k == 3)),
                        reads=["wfo", "FTb"], writes=[PSB[4 + par * 2]])
                for k in range(4):
                    P.op("pe", lambda e, c=c, k=k, pyb=pyb: e.matmul(
                        pyb, lhsT=wmo[:, k, c * 128:(c + 1) * 128], rhs=OTb[:, k, :], start=(k == 0), stop=(k == 3)),
                        reads=["wmo", "OTb"], writes=[PSB[5 + par * 2]])
                P.op("act", lambda e, par=par, pga=pga: e.activation(out=sga[par], in_=pga, func=AF.Sigmoid),
                     reads=[PSB[par * 2]], writes=["sga%d" % par])
                P.op("act", lambda e, par=par, pgb=pgb: e.activation(out=sgb[par], in_=pgb, func=AF.Sigmoid),
                     reads=[PSB[par * 2 + 1]], writes=["sgb%d" % par])
                P.op("dve", lambda e, par=par, pya=pya: e.tensor_tensor(out=u1, in0=pya, in1=sga[par], op=ALU.mult),
                     reads=[PSB[4 + par * 2], "sga%d" % par], writes=["u1"])
                P.op("dve", lambda e, par=par, pyb=pyb: e.tensor_tensor(out=u2, in0=pyb, in1=sgb[par], op=ALU.mult),
                     reads=[PSB[5 + par * 2], "sgb%d" % par], writes=["u2"])
                P.op("pool", lambda e, c=c: e.tensor_tensor(out=mT[:, c, :], in0=u1, in1=u2, op=ALU.add),
                     reads=["u1", "u2"], writes=["mT"])
            for t in range(4):
                b0 = 4 + 2 * (t % 2)
                pd = P.bank(b0, F32, 2)
                for c in range(8):
                    for hf in range(2):
                        P.op("pe", lambda e, t=t, c=c, hf=hf, pd=pd: e.matmul(
                            pd[:, hf * 512:(hf + 1) * 512], lhsT=mT[:, c, t * 128:(t + 1) * 128],
                            rhs=wo[:, c, hf * 512:(hf + 1) * 512], start=(c == 0), stop=(c == 7)),
                            reads=["mT", "wo"], writes=[PSB[b0 + hf]])
                P.op("dve", lambda e, pd=pd: e.tensor_tensor(out=tmp, in0=pd, in1=g2b, op=ALU.mult),
                     reads=[PSB[b0], PSB[b0 + 1], "g2b"], writes=["tmp"])
                P.op("pool", lambda e, t=t: e.tensor_tensor(out=xt[t], in0=xt[t], in1=tmp, op=ALU.add),
                     reads=[xn[t], "tmp"], writes=[xn[t]])
                P.op("sp", lambda e, t=t, r0=r0: e.dma_start(out=x2_d[r0 + t * 128:r0 + (t + 1) * 128, :], in_=xt[t]),
                     reads=[xn[t]], dma=True)

    phases.append(phase_a2)
    phases.append(phase_attn)
    phases.append(phase_fourier)
    phases.append(phase_c1)
    phases.append(lambda: ffn_phase(x2_d, out_d, NOWN // TB, f2g_d, f2u_d, f2d_d, 2, 64 * 128, True))

    for i, ph in enumerate(phases):
        if i > stop_after:
            break
        ph()
        P.barrier()
        P.release()
    P.emit()
    return nc


def _pp(v, n):
    return np.ascontiguousarray(v.reshape(n, 128).T).astype(np.float32)


def _consts(p):
    t = {}
    c = np.arange(128, dtype=np.float64)
    ang = 2 * np.pi * np.outer(c, c) / 128.0
    t["t_fc"] = np.concatenate([np.cos(ang), -np.sin(ang)], axis=1).astype(np.float32) / 1024.0
    j = np.arange(128)
    s1 = np.where(j < 64, 2 * j + p, 2 * (j - 64) + (1 - p)).astype(np.float64)
    k1 = (64 * p + np.arange(64)).astype(np.float64)
    a = 2 * np.pi * np.outer(s1, k1) / 128.0
    C, Sn = np.cos(a), np.sin(a)
    t["t_f128r"] = np.concatenate([C, -Sn], axis=1).astype(np.float32)
    t["t_f128i"] = np.concatenate([Sn, C], axis=1).astype(np.float32)
    s2 = np.arange(64, dtype=np.float64)
    k = (64 * p + np.arange(64)[None, :] + 128 * np.arange(64)[:, None]).reshape(-1).astype(np.float64)
    a = 2 * np.pi * np.outer(s2, k) / 8192.0
    t["t_cos"] = np.cos(a).astype(np.float32)
    t["t_sin"] = np.sin(a).astype(np.float32)
    half = 16
    inv = (1.0 / (np.float32(10000.0) ** (np.arange(half, dtype=np.float32) * np.float32(2.0) / np.float32(32)))).astype(np.float32)
    r = np.zeros((32, 4), np.float32)
    r[:, 0] = np.concatenate([inv, inv])
    r[:16, 1] = -2.0 * np.pi
    r[16:, 1] = 2.0 * np.pi
    t["t_rope"] = r
    return t


def _own_perm(p):
    k2 = np.arange(64)[:, None]
    own = (128 * k2 + 64 * p + np.arange(64)[None, :]).reshape(-1)
    oth = (128 * k2 + 64 * (1 - p) + np.arange(64)[None, :]).reshape(-1)
    return own, oth


def prep_inputs(x, c, positions, ada_w, ada_b, ffn1_norm, ffn1_w_gate, ffn1_w_up, ffn1_w_down,
                mix_norm, w_in, q_norm, w_q_up, kv_norm, w_kv_up, w_fourier_out, w_mla_out, w_out,
                ffn2_norm, ffn2_w_gate, ffn2_w_up, ffn2_w_down, final_norm, cores=range(8)):
    f = lambda a: np.ascontiguousarray(np.asarray(a))
    x, c, positions = f(x), f(c), f(positions)
    w_in0 = f(w_in)[0]
    swap = np.concatenate([np.arange(16, 32), np.arange(0, 16)])
    kr = w_in0[:, 1152:1184]
    winA = np.zeros((1024, 1280), np.float32)
    winA[:, 0:1184] = w_in0[:, 0:1184]
    winA[:, 1184:1216] = kr[:, swap]
    wq0 = f(w_q_up)[0].reshape(384, 8, 96)
    wq = np.zeros((384, 8, 128), np.float32)
    wq[:, :, 0:64] = wq0[:, :, 0:64]
    wq[:, :, 64:96] = wq0[:, :, 64:96]
    wq[:, :, 96:128] = wq0[:, :, 64:96][:, :, swap]
    wkv0 = f(w_kv_up)[0].reshape(256, 8, 128)
    shared = {
        "ada_w": f(ada_w)[0], "ada_b_pp": _pp(f(ada_b)[0], 72),
        "norms_pp": np.concatenate([_pp(f(ffn1_norm)[0], 8), _pp(f(mix_norm)[0], 8), _pp(f(ffn2_norm)[0], 8)], axis=1),
        "final_norm": f(final_norm).reshape(1, 1024),
        "f1_wg": f(ffn1_w_gate)[0], "f1_wu": f(ffn1_w_up)[0], "f1_wd": f(ffn1_w_down)[0],
        "f2_wg": f(ffn2_w_gate)[0], "f2_wu": f(ffn2_w_up)[0], "f2_wd": f(ffn2_w_down)[0],
        "w_inA": winA, "w_inG": np.ascontiguousarray(w_in0[:, 1184:3232]),
        "qkn_pp": np.concatenate([_pp(f(q_norm)[0], 3), _pp(f(kv_norm)[0], 2)], axis=1),
        "w_q": np.ascontiguousarray(wq.reshape(384, 1024)),
        "w_kn": np.ascontiguousarray(wkv0[:, :, 0:64].reshape(256, 512)),
        "w_v": np.ascontiguousarray(wkv0[:, :, 64:128].reshape(256, 512)),
        "w_fo": f(w_fourier_out)[0], "w_mo": f(w_mla_out)[0], "w_out": f(w_out)[0],
    }
    maps = []
    for core in cores:
        b, p = core // 2, core % 2
        own, oth = _own_perm(p)
        perm = np.concatenate([own, oth])
        m = dict(shared)
        m["x"] = np.ascontiguousarray(x[b][perm])
        m["pos"] = np.ascontiguousarray(positions[b][perm]).reshape(1, 8192).astype(np.int32)
        m["c_pp"] = _pp(c[b], 8)
        m.update(_consts(p))
        maps.append(m)
    return maps


_NC_CACHE = {}


def kernel(**inputs):
    if "nc" not in _NC_CACHE:
        _NC_CACHE["nc"] = build()
    nc = _NC_CACHE["nc"]
    maps = prep_inputs(**inputs)
    res = run_bass_kernel_spmd(nc, maps, core_ids=list(range(8)))
    out = np.zeros((4, 8192, 1024), np.float32)
    for core in range(8):
        b, p = core // 2, core % 2
        own, _ = _own_perm(p)
        out[b][own] = res.results[core]["out"]
    return out
```

```python
import ml_dtypes
from concourse.bass_utils import run_bass_kernel_spmd
import numpy as np
import concourse.bass as bass
import concourse.mybir as mybir

F32 = mybir.dt.float32
BF16 = mybir.dt.bfloat16
I32 = mybir.dt.int32
U8 = mybir.dt.uint8
AF = mybir.ActivationFunctionType
ALU = mybir.AluOpType
AX = mybir.AxisListType
DTSIZE = {F32: 4, BF16: 2, I32: 4, U8: 1}


class Buf:
    __slots__ = ("name", "w", "rs", "rd")

    def __init__(self, name):
        self.name = name
        self.w = None
        self.rs = {}
        self.rd = []


class Op:
    __slots__ = ("eng", "fn", "seq", "signal", "is_dma", "dsem", "dval", "waits",
                 "cnt", "phase", "edeps", "ddeps")


class Prog:
    ENGS = ("pe", "act", "dve", "pool", "sp")
    KDMA = 12

    def __init__(self, nc):
        self.nc = nc
        self.q = {e: [] for e in self.ENGS}
        self.phase = 0
        self.esem = {}
        for e in ("pe", "act", "dve", "pool"):
            self.esem[e] = nc.alloc_semaphore("s_" + e)
        self.bar_sem = nc.alloc_semaphore("s_bar")
        self.bar_cnt = 0
        self.dsems = {}
        self.dcount = {}
        self.dma_ops = {}
        for e in ("sp", "act", "pool"):
            self.dsems[e] = [nc.alloc_semaphore("d_%s_%d" % (e, i)) for i in range(self.KDMA)]
            self.dcount[e] = 0
            self.dma_ops[e] = []
        self.waited = {}
        self.dwaited = {}
        self.bar_wait_pending = {e: 0 for e in self.ENGS}
        self.arena = None
        self.sb_off = 0
        self.sb_mark = 0
        self.sb_cap = 0
        self.bufs = {}

    def init_mem(self, sbuf_bytes=206 * 1024):
        nc = self.nc
        self.arena = nc.alloc_sbuf_tensor("arena", [128, sbuf_bytes], U8)
        self.sb_cap = sbuf_bytes
        self.psum = nc.alloc_psum_tensor("psum", [128, 4096], F32)

    def sb(self, shape, dtype, name=None):
        n = 1
        for s in shape[1:]:
            n *= s
        nbytes = n * DTSIZE[dtype]
        off = (self.sb_off + 63) // 64 * 64
        assert off + nbytes <= self.sb_cap, "SBUF overflow: %s need %d at %d" % (name, nbytes, off)
        self.sb_off = off + nbytes
        ap = self.arena[0:shape[0], off:off + nbytes].bitcast(dtype)
        if len(shape) > 2:
            names = " ".join("d%d" % i for i in range(1, len(shape)))
            kw = {"d%d" % i: shape[i] for i in range(1, len(shape))}
            ap = ap.rearrange("p (%s) -> p %s" % (names, names), **kw)
        return ap

    def mark(self):
        self.sb_mark = self.sb_off

    def release(self):
        self.sb_off = self.sb_mark

    def bank(self, b, dtype=F32, nb=1):
        ap = self.psum[:, b * 512:(b + nb) * 512]
        if dtype != F32:
            ap = ap.bitcast(dtype)
        return ap

    def buf(self, name):
        b = self.bufs.get(name)
        if b is None:
            b = Buf(name)
            self.bufs[name] = b
        return b

    def _mk(self, eng, fn, dma):
        op = Op()
        op.eng = eng
        op.fn = fn
        op.seq = len(self.q[eng])
        op.signal = False
        op.is_dma = dma
        op.dsem = None
        op.dval = 0
        op.waits = []
        op.cnt = 0
        op.phase = self.phase
        op.edeps = {}
        op.ddeps = []
        return op

    def op(self, eng, fn, reads=(), writes=(), dma=False):
        op = self._mk(eng, fn, dma)
        deps = []
        for b in reads:
            if isinstance(b, str):
                b = self.buf(b)
            if b.w is not None:
                deps.append((b.w, True))
        for b in writes:
            if isinstance(b, str):
                b = self.buf(b)
            if b.w is not None:
                deps.append((b.w, True))
            for r in b.rs.values():
                deps.append((r, False))
            for r in b.rd:
                deps.append((r, False))
        best = {}
        for d, strong in deps:
            if d is op or d.phase != self.phase:
                continue
            if d.is_dma:
                key = (eng, id(d))
                if key in self.dwaited:
                    continue
                self.dwaited[key] = True
                op.ddeps.append(d)
            else:
                if d.eng == eng and eng == "pe":
                    continue
                b = best.get(d.eng)
                if b is None or b.seq < d.seq:
                    best[d.eng] = d
        for te, d in best.items():
            key = (eng, te)
            if self.waited.get(key, -1) >= d.seq:
                continue
            self.waited[key] = d.seq
            d.signal = True
            op.edeps[te] = d
        if dma:
            i = self.dcount[eng]
            self.dcount[eng] = i + 1
            K = self.KDMA
            op.dsem = self.dsems[eng][i % K]
            op.dval = 16 * (i // K + 1)
            if i >= K:
                old = self.dma_ops[eng][i - K]
                key = (eng, id(old))
                if key not in self.dwaited:
                    self.dwaited[key] = True
                    op.ddeps.append(old)
            self.dma_ops[eng].append(op)
        if self.bar_wait_pending[eng]:
            op.waits.append((self.bar_sem, self.bar_wait_pending[eng]))
            self.bar_wait_pending[eng] = 0
        for b in writes:
            if isinstance(b, str):
                b = self.buf(b)
            b.w = op
            b.rs = {}
            b.rd = []
        for b in reads:
            if isinstance(b, str):
                b = self.buf(b)
            if b.w is not op:
                if dma:
                    b.rd.append(op)
                else:
                    b.rs[eng] = op
        self.q[eng].append(op)
        return op

    def barrier(self):
        sp_op = self._mk("sp", None, False)
        for e in ("pe", "act", "dve", "pool"):
            if self.q[e]:
                last = self.q[e][-1]
                if last.is_dma:
                    for o in reversed(self.q[e]):
                        if not o.is_dma:
                            last = o
                            break
                if not last.is_dma:
                    last.signal = True
                    sp_op.edeps[e] = last
        for e in ("sp", "act", "pool"):
            n = self.dcount[e]
            for o in self.dma_ops[e][max(0, n - self.KDMA):]:
                sp_op.ddeps.append(o)
        self.bar_cnt += 1
        bc = self.bar_cnt
        bs = self.bar_sem
        sp_op.fn = lambda eng: eng.sem_inc(bs, 1)
        if self.bar_wait_pending["sp"]:
            sp_op.waits.append((self.bar_sem, self.bar_wait_pending["sp"]))
            self.bar_wait_pending["sp"] = 0
        self.q["sp"].append(sp_op)
        for e in ("pe", "act", "dve", "pool"):
            self.bar_wait_pending[e] = bc
        self.phase += 1
        for b in self.bufs.values():
            b.w = None
            b.rs = {}
            b.rd = []

    def emit(self):
        nc = self.nc
        for e in ("pe", "act", "dve", "pool"):
            c = 0
            for o in self.q[e]:
                if o.signal:
                    c += 1
                    o.cnt = c
        esem = self.esem

        def run(eng_name, eng):
            for o in self.q[eng_name]:
                for (s, v) in o.waits:
                    eng.wait_ge(s, v)
                for te, d in o.edeps.items():
                    eng.wait_ge(esem[te], d.cnt)
                for d in o.ddeps:
                    eng.wait_ge(d.dsem, d.dval)
                inst = o.fn(eng)
                if o.signal:
                    inst.then_inc(esem[eng_name], 1)
                if o.is_dma:
                    inst.then_inc(o.dsem, 16)

        with nc.Block() as block:
            @block.tensor
            def _(e):
                run("pe", e)

            @block.scalar
            def _(e):
                run("act", e)

            @block.vector
            def _(e):
                run("dve", e)

            @block.gpsimd
            def _(e):
                run("pool", e)

            @block.sync
            def _(e):
                run("sp", e)

D = 1024
DFF = 2816
NJ = DFF // 128
S = 8192
NOWN = 4096
TB = 512
EPS = 1e-6
NH = 8
SM_SCALE = 96.0 ** -0.5


def build(stop_after=99, dbg=()):
    nc = bass.Bass("TRN2", target_bir_lowering=False)
    P = Prog(nc)
    P.init_mem()

    def din(name, shape, dt=F32):
        return nc.dram_tensor(name, list(shape), dt, kind="ExternalInput").ap()

    def dscr(name, shape, dt):
        kind = "ExternalOutput" if name in dbg else "Internal"
        return nc.dram_tensor(name, list(shape), dt, kind=kind).ap()

    x_d = din("x", [S, D])
    pos_d = din("pos", [1, S], I32)
    cpp_d = din("c_pp", [128, 8])
    adaw_d = din("ada_w", [D, 9 * D])
    adab_d = din("ada_b_pp", [128, 72])
    norms_d = din("norms_pp", [128, 24])
    fnorm_d = din("final_norm", [1, D])
    f1g_d = din("f1_wg", [D, DFF])
    f1u_d = din("f1_wu", [D, DFF])
    f1d_d = din("f1_wd", [DFF, D])
    f2g_d = din("f2_wg", [D, DFF])
    f2u_d = din("f2_wu", [D, DFF])
    f2d_d = din("f2_wd", [DFF, D])
    winA_d = din("w_inA", [D, 1280])
    winG_d = din("w_inG", [D, 2048])
    qkn_d = din("qkn_pp", [128, 5])
    wq_d = din("w_q", [384, 1024])
    wkn_d = din("w_kn", [256, 512])
    wv_d = din("w_v", [256, 512])
    wfo_d = din("w_fo", [512, D])
    wmo_d = din("w_mo", [512, D])
    wout_d = din("w_out", [D, D])
    fc_d = din("t_fc", [128, 256])
    f128r_d = din("t_f128r", [128, 128])
    f128i_d = din("t_f128i", [128, 128])
    tcos_d = din("t_cos", [64, 4096])
    tsin_d = din("t_sin", [64, 4096])
    ropec_d = din("t_rope", [32, 4])

    out_d = nc.dram_tensor("out", [NOWN, D], F32, kind="ExternalOutput").ap()
    mods_d = dscr("mods", [1, 9 * D], F32)
    x1_d = dscr("x1s", [S, D], F32)
    zu_d = dscr("zu", [512, S], BF16)
    zl_d = dscr("zl", [768, S], F32)
    ft_d = dscr("fts", [512, NOWN], BF16)
    ot_d = dscr("ots", [512, NOWN], BF16)
    x2_d = dscr("x2s", [NOWN, D], F32)

    ident_f = P.sb([128, 128], F32)
    ident_b = P.sb([128, 128], BF16)
    ones_b = P.sb([128, 128], BF16)
    epsb = P.sb([128, 1], F32)
    modpp = P.sb([128, 72], F32)
    gv = P.sb([128, 24], F32)
    normspp = P.sb([128, 24], F32)
    P.op("pool", lambda e: e.memset(ident_f, 0.0), writes=["ident_f"])
    P.op("pool", lambda e: e.affine_select(out=ident_f, in_=ident_f, pattern=[[-1, 128]],
                                          compare_op=ALU.not_equal, fill=1.0, base=0,
                                          channel_multiplier=1),
         reads=["ident_f"], writes=["ident_f"])
    P.op("dve", lambda e: e.tensor_copy(out=ident_b, in_=ident_f), reads=["ident_f"], writes=["ident_b"])
    P.op("dve", lambda e: e.memset(ones_b, 1.0), writes=["ones_b"])
    P.op("dve", lambda e: e.memset(epsb, EPS), writes=["epsb"])
    P.mark()

    PSB = ["psb%d" % i for i in range(8)]

    def phase_adaln():
        cpp = P.sb([128, 8], F32)
        cact = P.sb([128, 8], BF16)
        adab = P.sb([128, 72], F32)
        modT = P.sb([128, 128], F32)
        wblk = [P.sb([128, 8, 1024], BF16) for _ in range(2)]
        P.op("sp", lambda e: e.dma_start(out=cpp, in_=cpp_d), writes=["cpp"], dma=True)
        P.op("sp", lambda e: e.dma_start(out=adab, in_=adab_d), writes=["adab"], dma=True)
        P.op("sp", lambda e: e.dma_start(out=normspp, in_=norms_d), writes=["normspp"], dma=True)
        P.op("act", lambda e: e.activation(out=cact, in_=cpp, func=AF.Silu), reads=["cpp"], writes=["cact"])
        ps0 = P.bank(0)
        for blk in range(9):
            wb = wblk[blk % 2]
            wn = "wblk%d" % (blk % 2)
            src = adaw_d[:, blk * 1024:(blk + 1) * 1024].rearrange("(k p) n -> p k n", p=128)
            P.op("pool", lambda e, wb=wb, src=src: e.dma_start(out=wb, in_=src), writes=[wn], dma=True)
            for jj in range(8):
                j = blk * 8 + jj
                for k in range(8):
                    P.op("pe", lambda e, wb=wb, j=j, jj=jj, k=k: e.matmul(
                        ps0[:, j:j + 1], lhsT=wb[:, k, jj * 128:(jj + 1) * 128], rhs=cact[:, k:k + 1],
                        start=(k == 0), stop=(k == 7)),
                        reads=[wn, "cact"], writes=[PSB[0]])
        P.op("dve", lambda e: e.tensor_tensor(out=modpp, in0=ps0[:, 0:72], in1=adab, op=ALU.add),
             reads=[PSB[0], "adab"], writes=["modpp"])
        for i in range(3):
            sc = modpp[:, i * 24 + 8:i * 24 + 16]
            P.op("dve", lambda e, i=i, sc=sc: e.scalar_tensor_tensor(
                out=gv[:, i * 8:(i + 1) * 8], in0=sc, scalar=1.0, in1=normspp[:, i * 8:(i + 1) * 8],
                op0=ALU.add, op1=ALU.mult),
                reads=["modpp", "normspp"], writes=["gv"])
        ps1 = P.bank(1)
        P.op("pe", lambda e: e.transpose(ps1[0:72, 0:128], modpp[:, 0:72], ident_f),
             reads=["modpp", "ident_f"], writes=[PSB[1]])
        P.op("dve", lambda e: e.tensor_copy(out=modT[0:72, :], in_=ps1[0:72, 0:128]),
             reads=[PSB[1]], writes=["modT"])
        P.op("sp", lambda e: e.dma_start(
            out=mods_d.rearrange("o (j f) -> (o j) f", f=128), in_=modT[0:72, :]),
            reads=["modT"], dma=True)

    def load_bcast(dst, name, off):
        P.op("sp", lambda e: e.dma_start(out=dst, in_=mods_d[0:1, off:off + D].broadcast_to([128, D])),
             writes=[name], dma=True)

    def norm_to_hT(xt, xnames, ni, hT, hname, scr, sfx=""):
        ss, rstd, xs, junk = scr["ss"], scr["rstd"], scr["xs"], scr["junk"]
        SSN = ["ss0" + sfx, "ss1" + sfx, "ss2" + sfx, "ss3" + sfx]
        RS, JK = "rstd" + sfx, "junk" + sfx
        XS = ["xs0" + sfx, "xs1" + sfx]
        P.op("dve", lambda e: e.memset(ss, 0.0), writes=SSN)
        for t in range(4):
            P.op("act", lambda e, t=t: e.activation(out=junk, in_=xt[t], func=AF.Square,
                                                    accum_out=ss[:, t:t + 1]),
                 reads=[xnames[t]], writes=[SSN[t], JK])
        P.op("act", lambda e: e.activation(out=rstd, in_=ss, func=AF.Sqrt, scale=1.0 / D, bias=epsb),
             reads=SSN + ["epsb"], writes=[RS])
        P.op("dve", lambda e: e.reciprocal(out=rstd, in_=rstd), reads=[RS], writes=[RS])
        for t in range(4):
            P.op("act", lambda e, t=t: e.activation(out=xs[t % 2], in_=xt[t], func=AF.Copy,
                                                    scale=rstd[:, t:t + 1]),
                 reads=[xnames[t], RS], writes=[XS[t % 2]])
            for c in range(8):
                pb = P.bank(4 + c // 2, BF16)
                P.op("pe", lambda e, t=t, c=c, pb=pb: e.transpose(
                    pb[:, (c % 2) * 512 + t * 128:(c % 2) * 512 + (t + 1) * 128],
                    xs[t % 2][:, c * 128:(c + 1) * 128], ident_b),
                    reads=[XS[t % 2], "ident_b"], writes=[PSB[4 + c // 2]])
        for c in range(8):
            pb = P.bank(4 + c // 2, BF16)
            P.op("dve", lambda e, c=c, pb=pb: e.tensor_scalar(
                out=hT[:, c, :], in0=pb[:, (c % 2) * 512:(c % 2 + 1) * 512],
                scalar1=gv[:, ni * 8 + c:ni * 8 + c + 1],
                scalar2=modpp[:, ni * 24 + c:ni * 24 + c + 1],
                op0=ALU.mult, op1=ALU.add),
                reads=[PSB[4 + c // 2], "gv", "modpp"], writes=[hname])

    def norm_scratch():
        return {"ss": P.sb([128, 4], F32), "rstd": P.sb([128, 4], F32),
                "xs": [P.sb([128, D], BF16) for _ in range(2)],
                "junk": P.sb([128, D], BF16)}

    def ffn_phase(src_d, dst_d, nblk, wg_d, wu_d, wd_d, ni, goff, final):
        wg = P.sb([128, 8, DFF], BF16)
        wu = P.sb([128, 8, DFF], BF16)
        wd = P.sb([128, NJ, D], BF16)
        xt = [P.sb([128, D], F32) for _ in range(4)]
        xn = ["xt%d" % t for t in range(4)]
        scr = norm_scratch()
        hT = P.sb([128, 8, TB], BF16)
        aT = P.sb([128, NJ, TB], BF16)
        sg = [P.sb([128, TB], BF16) for _ in range(2)]
        tmp = P.sb([128, D], F32)
        ghb = P.sb([128, D], F32)
        if final:
            fnb = P.sb([128, D], F32)
            ss2 = P.sb([128, 1], F32)
            P.op("sp", lambda e: e.dma_start(out=fnb, in_=fnorm_d.broadcast_to([128, D])),
                 writes=["fnb"], dma=True)
        load_bcast(ghb, "ghb", goff)
        P.op("dve", lambda e: e.tensor_scalar(out=ghb, in0=ghb, scalar1=0.5, scalar2=None, op0=ALU.mult),
             reads=["ghb"], writes=["ghb"])
        for i in range(NJ // 2):
            cs = slice(i * 256, (i + 1) * 256)
            P.op("pool", lambda e, cs=cs: e.dma_start(
                out=wg[:, :, cs], in_=wg_d[:, cs].rearrange("(k p) n -> p k n", p=128)),
                writes=["wg%d" % i], dma=True)
            P.op("pool", lambda e, cs=cs: e.dma_start(
                out=wu[:, :, cs], in_=wu_d[:, cs].rearrange("(k p) n -> p k n", p=128)),
                writes=["wu%d" % i], dma=True)
        for i in range(NJ // 2):
            P.op("pool", lambda e, i=i: e.dma_start(
                out=wd[:, 2 * i:2 * i + 2, :],
                in_=wd_d[i * 256:(i + 1) * 256, :].rearrange("(j p) n -> p j n", p=128)),
                writes=["wd%d" % i], dma=True)
        for blk in range(nblk):
            r0 = blk * TB
            for t in range(4):
                P.op("sp", lambda e, t=t, r0=r0: e.dma_start(out=xt[t], in_=src_d[r0 + t * 128:r0 + (t + 1) * 128, :]),
                     writes=[xn[t]], dma=True)
            norm_to_hT(xt, xn, ni, hT, "hT", scr)
            for j in range(NJ):
                pg = P.bank(j % 2)
                pu = P.bank(2 + j % 2)
                for k in range(8):
                    P.op("pe", lambda e, j=j, k=k, pg=pg: e.matmul(
                        pg, lhsT=wg[:, k, j * 128:(j + 1) * 128], rhs=hT[:, k, :], start=(k == 0), stop=(k == 7)),
                        reads=["wg%d" % (j // 2), "hT"], writes=[PSB[j % 2]])
                for k in range(8):
                    P.op("pe", lambda e, j=j, k=k, pu=pu: e.matmul(
                        pu, lhsT=wu[:, k, j * 128:(j + 1) * 128], rhs=hT[:, k, :], start=(k == 0), stop=(k == 7)),
                        reads=["wu%d" % (j // 2), "hT"], writes=[PSB[2 + j % 2]])
                P.op("act", lambda e, j=j, pg=pg: e.activation(out=sg[j % 2], in_=pg, func=AF.Silu),
                     reads=[PSB[j % 2]], writes=["sg%d" % (j % 2)])
                P.op("dve", lambda e, j=j, pu=pu: e.tensor_tensor(out=aT[:, j, :], in0=pu, in1=sg[j % 2], op=ALU.mult),
                     reads=[PSB[2 + j % 2], "sg%d" % (j % 2)], writes=["aT"])
            for t in range(4):
                b0 = 4 + 2 * (t % 2)
                pd = P.bank(b0, F32, 2)
                for j in range(NJ):
                    for hf in range(2):
                        P.op("pe", lambda e, t=t, j=j, hf=hf, pd=pd: e.matmul(
                            pd[:, hf * 512:(hf + 1) * 512], lhsT=aT[:, j, t * 128:(t + 1) * 128],
                            rhs=wd[:, j, hf * 512:(hf + 1) * 512], start=(j == 0), stop=(j == NJ - 1)),
                            reads=["aT", "wd%d" % (j // 2)], writes=[PSB[b0 + hf]])
                P.op("dve", lambda e, pd=pd: e.tensor_tensor(out=tmp, in0=pd, in1=ghb, op=ALU.mult),
                     reads=[PSB[b0], PSB[b0 + 1], "ghb"], writes=["tmp"])
                P.op("pool", lambda e, t=t: e.tensor_tensor(out=xt[t], in0=xt[t], in1=tmp, op=ALU.add),
                     reads=[xn[t], "tmp"], writes=[xn[t]])
                if final:
                    P.op("dve", lambda e: e.memset(ss2, 0.0), writes=["ss2"])
                    P.op("act", lambda e, t=t: e.activation(out=tmp, in_=xt[t], func=AF.Square, accum_out=ss2),
                         reads=[xn[t], "ss2"], writes=["tmp", "ss2"])
                    P.op("act", lambda e: e.activation(out=ss2, in_=ss2, func=AF.Sqrt, scale=1.0 / D, bias=epsb),
                         reads=["ss2", "epsb"], writes=["ss2"])
                    P.op("dve", lambda e: e.reciprocal(out=ss2, in_=ss2), reads=["ss2"], writes=["ss2"])
                    P.op("dve", lambda e, t=t: e.scalar_tensor_tensor(
                        out=xt[t], in0=xt[t], scalar=ss2, in1=fnb, op0=ALU.mult, op1=ALU.mult),
                        reads=[xn[t], "ss2", "fnb"], writes=[xn[t]])
                P.op("sp", lambda e, t=t, r0=r0: e.dma_start(out=dst_d[r0 + t * 128:r0 + (t + 1) * 128, :], in_=xt[t]),
                     reads=[xn[t]], dma=True)

    phases = []
    phases.append(phase_adaln)
    phases.append(lambda: ffn_phase(x_d, x1_d, S // TB, f1g_d, f1u_d, f1d_d, 0, 16 * 128, False))
    def phase_a2():
        wA = P.sb([128, 8, 1280], BF16)
        sets = []
        for i in range(2):
            sets.append({
                "xt": [P.sb([128, D], F32) for _ in range(4)],
                "xn": ["xt%d_%d" % (t, i) for t in range(4)],
                "scr": norm_scratch(), "hT": P.sb([128, 8, TB], BF16), "hn": "hT_%d" % i,
                "zu": P.sb([128, 4, TB], BF16), "zun": "zu_sb%d" % i,
                "zl": P.sb([128, 6, TB], F32), "zln": "zl_sb%d" % i, "sfx": "_%d" % i})
        for k in range(8):
            P.op("pool", lambda e, k=k: e.dma_start(out=wA[:, k, :], in_=winA_d[k * 128:(k + 1) * 128, :]),
                 writes=["wA"], dma=True)

        def load_norm(blk):
            st = sets[blk % 2]
            r0 = blk * TB
            for t in range(4):
                P.op("sp", lambda e, t=t, r0=r0, st=st: e.dma_start(out=st["xt"][t], in_=x1_d[r0 + t * 128:r0 + (t + 1) * 128, :]),
                     writes=[st["xn"][t]], dma=True)
            norm_to_hT(st["xt"], st["xn"], 1, st["hT"], st["hn"], st["scr"], st["sfx"])

        def project(blk):
            st = sets[blk % 2]
            r0 = blk * TB
            hT, zu_sb, zl_sb = st["hT"], st["zu"], st["zl"]
            for cc in range(10):
                pb = P.bank(cc % 4)
                for k in range(8):
                    P.op("pe", lambda e, cc=cc, k=k, pb=pb, hT=hT: e.matmul(
                        pb, lhsT=wA[:, k, cc * 128:(cc + 1) * 128], rhs=hT[:, k, :], start=(k == 0), stop=(k == 7)),
                        reads=["wA", st["hn"]], writes=[PSB[cc % 4]])
                if cc < 4:
                    P.op("act", lambda e, cc=cc, pb=pb, zu_sb=zu_sb: e.activation(out=zu_sb[:, cc, :], in_=pb, func=AF.Copy),
                         reads=[PSB[cc % 4]], writes=[st["zun"]])
                else:
                    P.op("dve", lambda e, cc=cc, pb=pb, zl_sb=zl_sb: e.tensor_copy(out=zl_sb[:, cc - 4, :], in_=pb),
                         reads=[PSB[cc % 4]], writes=[st["zln"]])
            P.op("sp", lambda e, r0=r0, zu_sb=zu_sb: e.dma_start(
                out=zu_d[:, r0:r0 + TB].rearrange("(c p) n -> p c n", p=128), in_=zu_sb),
                reads=[st["zun"]], dma=True)
            P.op("sp", lambda e, r0=r0, zl_sb=zl_sb: e.dma_start(
                out=zl_d[:, r0:r0 + TB].rearrange("(c p) n -> p c n", p=128), in_=zl_sb),
                reads=[st["zln"]], dma=True)

        nb = S // TB
        load_norm(0)
        for blk in range(nb):
            if blk + 1 < nb:
                load_norm(blk + 1)
            project(blk)

    def phase_attn():
        ropec = P.sb([32, 4], F32)
        qkn = P.sb([128, 5], F32)
        cosT = P.sb([32, S], BF16)
        ssT = P.sb([32, S], BF16)
        kvnT = P.sb([128, 2, S], BF16)
        kropeT = P.sb([32, S], BF16)
        qnT = P.sb([128, 3, NOWN], BF16)
        wkn = P.sb([128, 2, 512], BF16)
        wv = P.sb([128, 2, 512], BF16)
        wq = P.sb([128, 3, 1024], BF16)
        KhT = P.sb([128, S], BF16)
        QhT = P.sb([128, NOWN], BF16)
        Vaug = P.sb([128, 64, 128], BF16)
        PT = [P.sb([128, 1024], BF16) for _ in range(3)]
        posi = P.sb([32, 1024], I32)
        ang = P.sb([32, 1024], F32)
        ang2 = P.sb([32, 1024], F32)
        lat = P.sb([128, 3, TB], F32)
        kra = P.sb([32, TB], F32)
        krb = P.sb([32, TB], F32)
        sq = P.sb([128, 3, TB], BF16)
        rstdk = P.sb([128, TB], F32)
        t1 = P.sb([32, TB], F32)
        t2 = P.sb([32, TB], F32)
        rden = P.sb([64, TB], F32)
        oT = [P.sb([64, TB], BF16) for _ in range(2)]
        P.op("sp", lambda e: e.dma_start(out=ropec, in_=ropec_d), writes=["ropec"], dma=True)
        P.op("sp", lambda e: e.dma_start(out=qkn, in_=qkn_d), writes=["qkn"], dma=True)
        P.op("pool", lambda e: e.dma_start(out=wkn, in_=wkn_d.rearrange("(k p) n -> p k n", p=128)), writes=["wkn"], dma=True)
        P.op("pool", lambda e: e.dma_start(out=wv, in_=wv_d.rearrange("(k p) n -> p k n", p=128)), writes=["wv"], dma=True)
        P.op("pool", lambda e: e.dma_start(out=wq, in_=wq_d.rearrange("(k p) n -> p k n", p=128)), writes=["wq"], dma=True)
        PI = float(np.pi)
        for ch in range(S // 1024):
            cs = slice(ch * 1024, (ch + 1) * 1024)
            P.op("sp", lambda e, cs=cs: e.dma_start(out=posi, in_=pos_d[0:1, cs].broadcast_to([32, 1024])),
                 writes=["posi"], dma=True)
            P.op("dve", lambda e: e.tensor_copy(out=ang, in_=posi), reads=["posi"], writes=["ang"])
            P.op("dve", lambda e: e.tensor_scalar(out=ang, in0=ang, scalar1=ropec[:, 0:1], scalar2=None, op0=ALU.mult),
                 reads=["ang", "ropec"], writes=["ang"])
            P.op("dve", lambda e: e.tensor_scalar(out=ang, in0=ang, scalar1=1.0 / (2 * PI), scalar2=None, op0=ALU.mult),
                 reads=["ang"], writes=["ang"])
            for which in range(2):
                if which == 1:
                    P.op("dve", lambda e: e.tensor_scalar(out=ang, in0=ang, scalar1=0.25, scalar2=None, op0=ALU.add),
                         reads=["ang"], writes=["ang"])
                P.op("dve", lambda e: e.tensor_copy(out=posi, in_=ang), reads=["ang"], writes=["posi"])
                P.op("dve", lambda e: e.tensor_copy(out=ang2, in_=posi), reads=["posi"], writes=["ang2"])
                P.op("dve", lambda e: e.tensor_tensor(out=ang2, in0=ang, in1=ang2, op=ALU.subtract),
                     reads=["ang", "ang2"], writes=["ang2"])
                if which == 0:
                    P.op("act", lambda e, cs=cs: e.activation(out=ssT[:, cs], in_=ang2, func=AF.Sin, scale=ropec[:, 1:2]),
                         reads=["ang2", "ropec"], writes=["ssT"])
                else:
                    P.op("act", lambda e, cs=cs: e.activation(out=cosT[:, cs], in_=ang2, func=AF.Sin, scale=2 * PI),
                         reads=["ang2"], writes=["cosT"])

        def latent_norm(row0, nch, nfeat, gcol, dstT, dname, blk):
            r0 = blk * TB
            P.op("sp", lambda e: e.dma_start(
                out=lat[:, 0:nch, :], in_=zl_d[row0:row0 + nch * 128, r0:r0 + TB].rearrange("(c p) n -> p c n", p=128)),
                writes=["lat"], dma=True)
            P.op("dve", lambda e: e.tensor_tensor(out=sq[:, 0:nch, :], in0=lat[:, 0:nch, :], in1=lat[:, 0:nch, :], op=ALU.mult),
                 reads=["lat"], writes=["sq"])
            pb = P.bank(0)
            for c in range(nch):
                P.op("pe", lambda e, c=c: e.matmul(pb, lhsT=ones_b, rhs=sq[:, c, :], start=(c == 0), stop=(c == nch - 1)),
                     reads=["ones_b", "sq"], writes=[PSB[0]])
            P.op("act", lambda e: e.activation(out=rstdk, in_=pb, func=AF.Sqrt, scale=1.0 / nfeat, bias=epsb),
                 reads=[PSB[0], "epsb"], writes=["rstdk"])
            P.op("dve", lambda e: e.reciprocal(out=rstdk, in_=rstdk), reads=["rstdk"], writes=["rstdk"])
            for c in range(nch):
                P.op("dve", lambda e, c=c: e.scalar_tensor_tensor(
                    out=dstT[:, c, r0:r0 + TB], in0=lat[:, c, :], scalar=qkn[:, gcol + c:gcol + c + 1], in1=rstdk,
                    op0=ALU.mult, op1=ALU.mult),
                    reads=["lat", "qkn", "rstdk"], writes=[dname])

        for blk in range(S // TB):
            r0 = blk * TB
            latent_norm(384, 2, 256, 3, kvnT, "kvnT", blk)
            P.op("sp", lambda e, r0=r0: e.dma_start(out=kra, in_=zl_d[640:672, r0:r0 + TB]), writes=["kra"], dma=True)
            P.op("sp", lambda e, r0=r0: e.dma_start(out=krb, in_=zl_d[672:704, r0:r0 + TB]), writes=["krb"], dma=True)
            P.op("dve", lambda e, r0=r0: e.tensor_tensor(out=t1, in0=kra, in1=cosT[:, r0:r0 + TB], op=ALU.mult),
                 reads=["kra", "cosT"], writes=["t1"])
            P.op("dve", lambda e, r0=r0: e.tensor_tensor(out=t2, in0=krb, in1=ssT[:, r0:r0 + TB], op=ALU.mult),
                 reads=["krb", "ssT"], writes=["t2"])
            P.op("dve", lambda e, r0=r0: e.tensor_tensor(out=kropeT[:, r0:r0 + TB], in0=t1, in1=t2, op=ALU.add),
                 reads=["t1", "t2"], writes=["kropeT"])
        for blk in range(NOWN // TB):
            latent_norm(0, 3, 384, 0, qnT, "qnT", blk)
        P.op("dve", lambda e: e.memset(Vaug[:, :, 64:128], 1.0), writes=["Vaug"])
        P.op("pool", lambda e: e.memset(KhT[96:128, :], 0.0), writes=["KhT"])
        P.op("pool", lambda e: e.memset(QhT[96:128, :], 0.0), writes=["QhT"])

        def do_head(h):
            for blk in range(S // TB):
                r0 = blk * TB
                pb = P.bank(blk % 2)
                for c in range(2):
                    P.op("pe", lambda e, c=c, r0=r0, pb=pb: e.matmul(
                        pb[0:64, :], lhsT=wkn[:, c, h * 64:(h + 1) * 64], rhs=kvnT[:, c, r0:r0 + TB],
                        start=(c == 0), stop=(c == 1)),
                        reads=["wkn", "kvnT"], writes=[PSB[blk % 2]])
                P.op("act", lambda e, r0=r0, pb=pb: e.activation(out=KhT[0:64, r0:r0 + TB], in_=pb[0:64, :], func=AF.Copy),
                     reads=[PSB[blk % 2]], writes=["KhT"])
            P.op("dve", lambda e: e.tensor_copy(out=KhT[64:96, :], in_=kropeT), reads=["kropeT"], writes=["KhT"])
            for g8 in range(8):
                pb = P.bank(2 + g8 % 2)
                for i in range(8):
                    tt = g8 * 8 + i
                    for c in range(2):
                        P.op("pe", lambda e, c=c, tt=tt, i=i, pb=pb: e.matmul(
                            pb[:, i * 64:(i + 1) * 64], lhsT=kvnT[:, c, tt * 128:(tt + 1) * 128],
                            rhs=wv[:, c, h * 64:(h + 1) * 64], start=(c == 0), stop=(c == 1)),
                            reads=["kvnT", "wv"], writes=[PSB[2 + g8 % 2]])
                P.op("dve", lambda e, g8=g8, pb=pb: e.tensor_copy(
                    out=Vaug[:, g8 * 8:(g8 + 1) * 8, 0:64], in_=pb.rearrange("p (a d) -> p a d", a=8)),
                    reads=[PSB[2 + g8 % 2]], writes=["Vaug"])
            for blk in range(NOWN // TB):
                r0 = blk * TB
                pq = P.bank(4)
                pr = P.bank(5)
                pw = P.bank(6)
                for c in range(3):
                    P.op("pe", lambda e, c=c, r0=r0: e.matmul(
                        pq[0:64, :], lhsT=wq[:, c, h * 128:h * 128 + 64], rhs=qnT[:, c, r0:r0 + TB],
                        start=(c == 0), stop=(c == 2)),
                        reads=["wq", "qnT"], writes=[PSB[4]])
                for c in range(3):
                    P.op("pe", lambda e, c=c, r0=r0: e.matmul(
                        pr[0:32, :], lhsT=wq[:, c, h * 128 + 64:h * 128 + 96], rhs=qnT[:, c, r0:r0 + TB],
                        start=(c == 0), stop=(c == 2)),
                        reads=["wq", "qnT"], writes=[PSB[5]])
                for c in range(3):
                    P.op("pe", lambda e, c=c, r0=r0: e.matmul(
                        pw[0:32, :], lhsT=wq[:, c, h * 128 + 96:h * 128 + 128], rhs=qnT[:, c, r0:r0 + TB],
                        start=(c == 0), stop=(c == 2)),
                        reads=["wq", "qnT"], writes=[PSB[6]])
                P.op("act", lambda e, r0=r0: e.activation(out=QhT[0:64, r0:r0 + TB], in_=pq[0:64, :], func=AF.Copy, scale=SM_SCALE),
                     reads=[PSB[4]], writes=["QhT"])
                P.op("dve", lambda e, r0=r0: e.scalar_tensor_tensor(
                    out=t1, in0=pr[0:32, :], scalar=SM_SCALE, in1=cosT[:, r0:r0 + TB], op0=ALU.mult, op1=ALU.mult),
                    reads=[PSB[5], "cosT"], writes=["t1"])
                P.op("dve", lambda e, r0=r0: e.scalar_tensor_tensor(
                    out=t2, in0=pw[0:32, :], scalar=SM_SCALE, in1=ssT[:, r0:r0 + TB], op0=ALU.mult, op1=ALU.mult),
                    reads=[PSB[6], "ssT"], writes=["t2"])
                P.op("dve", lambda e, r0=r0: e.tensor_tensor(out=QhT[64:96, r0:r0 + TB], in0=t1, in1=t2, op=ALU.add),
                     reads=["t1", "t2"], writes=["QhT"])
            seq = [(qb, kp) for qb in range(NOWN // TB) for kp in range(32)]

            def emit_qk(n):
                qb, kp = seq[n]
                q0 = qb * TB
                b0 = 2 * (n % 3)
                ps = P.bank(b0, F32, 2)
                for i in range(2):
                    kt = kp * 2 + i
                    P.op("pe", lambda e, kt=kt, i=i, ps=ps, q0=q0: e.matmul(
                        ps[:, i * 512:(i + 1) * 512], lhsT=KhT[:, kt * 128:(kt + 1) * 128],
                        rhs=QhT[:, q0:q0 + TB], start=True, stop=True),
                        reads=["KhT", "QhT"], writes=[PSB[b0 + i]])

            def emit_rest(n):
                qb, kp = seq[n]
                q0 = qb * TB
                b0 = 2 * (n % 3)
                ps = P.bank(b0, F32, 2)
                ptb = PT[n % 3]
                ptn = "PT%d" % (n % 3)
                po = P.bank(6 + qb % 2)
                P.op("act", lambda e, ps=ps, ptb=ptb: e.activation(out=ptb, in_=ps, func=AF.Exp),
                     reads=[PSB[b0], PSB[b0 + 1]], writes=[ptn])
                for i in range(2):
                    kt = kp * 2 + i
                    P.op("pe", lambda e, kt=kt, i=i, ptb=ptb, po=po: e.matmul(
                        po, lhsT=Vaug[:, kt, :], rhs=ptb[:, i * 512:(i + 1) * 512],
                        start=(kt == 0), stop=(kt == 63)),
                        reads=["Vaug", ptn], writes=[PSB[6 + qb % 2]])
                if kp == 31:
                    P.op("dve", lambda e, po=po: e.reciprocal(out=rden, in_=po[64:128, :]),
                         reads=[PSB[6 + qb % 2]], writes=["rden"])
                    ob = oT[qb % 2]
                    on = "oT%d" % (qb % 2)
                    P.op("dve", lambda e, po=po, ob=ob: e.tensor_tensor(out=ob, in0=po[0:64, :], in1=rden, op=ALU.mult),
                         reads=[PSB[6 + qb % 2], "rden"], writes=[on])
                    P.op("sp", lambda e, ob=ob, q0=q0: e.dma_start(out=ot_d[h * 64:(h + 1) * 64, q0:q0 + TB], in_=ob),
                         reads=[on], dma=True)

            emit_qk(0)
            for n in range(len(seq)):
                if n + 1 < len(seq):
                    emit_qk(n + 1)
                emit_rest(n)

        for h in range(NH):
            do_head(h)

    def phase_fourier():
        fc = P.sb([128, 256], BF16)
        f128r = P.sb([128, 128], BF16)
        f128i = P.sb([128, 128], BF16)
        tcos = P.sb([64, 4096], BF16)
        tsin = P.sb([64, 4096], BF16)
        uT = P.sb([128, S], BF16)
        Z = P.sb([128, 64, 256], BF16)
        W = P.sb([64, 64, 2, 128], BF16)
        FT = P.sb([128, NOWN], BF16)
        for dst, src, nm in ((fc, fc_d, "fc"), (f128r, f128r_d, "f128r"), (f128i, f128i_d, "f128i"),
                             (tcos, tcos_d, "tcos"), (tsin, tsin_d, "tsin")):
            P.op("pool", lambda e, dst=dst, src=src: e.dma_start(out=dst, in_=src), writes=[nm], dma=True)
        uTv = uT.rearrange("p (j s) -> p s j", s=64)
        tcv = tcos.rearrange("p (k2 k1) -> p k1 k2", k1=64)
        tsv = tsin.rearrange("p (k2 k1) -> p k1 k2", k1=64)
        FTv = FT.rearrange("p (k2 k1) -> p k1 k2", k1=64)
        Wv = W.rearrange("p k r c -> p c r k")
        for g in range(4):
            P.op("sp", lambda e, g=g: e.dma_start(out=uT, in_=zu_d[g * 128:(g + 1) * 128, :]), writes=["uT"], dma=True)
            for sp_ in range(32):
                pb = P.bank(sp_ % 2)
                for i in range(2):
                    s2 = sp_ * 2 + i
                    P.op("pe", lambda e, s2=s2, i=i, pb=pb: e.matmul(
                        pb[:, i * 256:(i + 1) * 256], lhsT=uTv[:, s2, :], rhs=fc, start=True, stop=True),
                        reads=["uT", "fc"], writes=[PSB[sp_ % 2]])
                eng = "act" if sp_ % 2 == 0 else "dve"
                if eng == "act":
                    P.op("act", lambda e, sp_=sp_, pb=pb: e.activation(
                        out=Z[:, 2 * sp_:2 * sp_ + 2, :], in_=pb.rearrange("p (a c) -> p a c", a=2), func=AF.Copy),
                        reads=[PSB[sp_ % 2]], writes=["Z"])
                else:
                    P.op("dve", lambda e, sp_=sp_, pb=pb: e.tensor_copy(
                        out=Z[:, 2 * sp_:2 * sp_ + 2, :], in_=pb.rearrange("p (a c) -> p a c", a=2)),
                        reads=[PSB[sp_ % 2]], writes=["Z"])
            for c4 in range(32):
                pb = P.bank(2 + c4 % 2)
                for i in range(4):
                    cp = c4 * 4 + i
                    P.op("pe", lambda e, cp=cp, i=i, pb=pb: e.matmul(
                        pb[0:64, i * 128:(i + 1) * 128], lhsT=Z[:, :, cp], rhs=f128r, start=True, stop=False),
                        reads=["Z", "f128r"], writes=[PSB[2 + c4 % 2]])
                    P.op("pe", lambda e, cp=cp, i=i, pb=pb: e.matmul(
                        pb[0:64, i * 128:(i + 1) * 128], lhsT=Z[:, :, 128 + cp], rhs=f128i, start=False, stop=True),
                        reads=["Z", "f128i"], writes=[PSB[2 + c4 % 2]])
                src = pb[0:64, :].rearrange("p (c r k) -> p c r k", c=4, r=2)
                dstv = Wv[:, c4 * 4:(c4 + 1) * 4, :, :]
                if c4 % 2 == 0:
                    P.op("act", lambda e, src=src, dstv=dstv: e.activation(out=dstv, in_=src, func=AF.Copy),
                         reads=[PSB[2 + c4 % 2]], writes=["W"])
                else:
                    P.op("dve", lambda e, src=src, dstv=dstv: e.tensor_copy(out=dstv, in_=src),
                         reads=[PSB[2 + c4 % 2]], writes=["W"])
            for k8 in range(8):
                pb = P.bank(4 + k8 % 2)
                for i in range(8):
                    k1 = k8 * 8 + i
                    P.op("pe", lambda e, k1=k1, i=i, pb=pb: e.matmul(
                        pb[:, i * 64:(i + 1) * 64], lhsT=W[:, k1, 0, :], rhs=tcv[:, k1, :], start=True, stop=False),
                        reads=["W", "tcos"], writes=[PSB[4 + k8 % 2]])
                    P.op("pe", lambda e, k1=k1, i=i, pb=pb: e.matmul(
                        pb[:, i * 64:(i + 1) * 64], lhsT=W[:, k1, 1, :], rhs=tsv[:, k1, :], start=False, stop=True),
                        reads=["W", "tsin"], writes=[PSB[4 + k8 % 2]])
                P.op("dve", lambda e, k8=k8, pb=pb: e.tensor_copy(
                    out=FTv[:, k8 * 8:(k8 + 1) * 8, :], in_=pb.rearrange("p (a k) -> p a k", a=8)),
                    reads=[PSB[4 + k8 % 2]], writes=["FT"])
            P.op("sp", lambda e, g=g: e.dma_start(out=ft_d[g * 128:(g + 1) * 128, :], in_=FT), reads=["FT"], dma=True)

    def phase_c1():
        wG = P.sb([128, 8, 2048], BF16)
        wfo = P.sb([128, 4, D], BF16)
        wmo = P.sb([128, 4, D], BF16)
        wo = P.sb([128, 8, D], BF16)
        xt = [P.sb([128, D], F32) for _ in range(4)]
        xn = ["xt%d" % t for t in range(4)]
        scr = norm_scratch()
        hT = P.sb([128, 8, TB], BF16)
        FTb = P.sb([128, 4, TB], BF16)
        OTb = P.sb([128, 4, TB], BF16)
        mT = P.sb([128, 8, TB], BF16)
        sga = [P.sb([128, TB], F32) for _ in range(2)]
        sgb = [P.sb([128, TB], F32) for _ in range(2)]
        u1 = P.sb([128, TB], F32)
        u2 = P.sb([128, TB], F32)
        tmp = P.sb([128, D], F32)
        g2b = P.sb([128, D], F32)
        load_bcast(g2b, "g2b", 40 * 128)
        for k in range(8):
            P.op("pool", lambda e, k=k: e.dma_start(out=wG[:, k, :], in_=winG_d[k * 128:(k + 1) * 128, :]), writes=["wG"], dma=True)
        P.op("pool", lambda e: e.dma_start(out=wfo, in_=wfo_d.rearrange("(k p) n -> p k n", p=128)), writes=["wfo"], dma=True)
        P.op("pool", lambda e: e.dma_start(out=wmo, in_=wmo_d.rearrange("(k p) n -> p k n", p=128)), writes=["wmo"], dma=True)
        P.op("pool", lambda e: e.dma_start(out=wo, in_=wout_d.rearrange("(k p) n -> p k n", p=128)), writes=["wo"], dma=True)
        for blk in range(NOWN // TB):
            r0 = blk * TB
            for t in range(4):
                P.op("sp", lambda e, t=t, r0=r0: e.dma_start(out=xt[t], in_=x1_d[r0 + t * 128:r0 + (t + 1) * 128, :]),
                     writes=[xn[t]], dma=True)
            P.op("sp", lambda e, r0=r0: e.dma_start(out=FTb, in_=ft_d[:, r0:r0 + TB].rearrange("(c p) n -> p c n", p=128)),
                 writes=["FTb"], dma=True)
            P.op("sp", lambda e, r0=r0: e.dma_start(out=OTb, in_=ot_d[:, r0:r0 + TB].rearrange("(c p) n -> p c n", p=128)),
                 writes=["OTb"], dma=True)
            norm_to_hT(xt, xn, 1, hT, "hT", scr)
            for c in range(8):
                par = c % 2
                pga = P.bank(par * 2)
                pgb = P.bank(par * 2 + 1)
                pya = P.bank(4 + par * 2)
                pyb = P.bank(5 + par * 2)
                for k in range(8):
                    P.op("pe", lambda e, c=c, k=k, pga=pga: e.matmul(
                        pga, lhsT=wG[:, k, c * 128:(c + 1) * 128], rhs=hT[:, k, :], start=(k == 0), stop=(k == 7)),
                        reads=["wG", "hT"], writes=[PSB[par * 2]])
                for k in range(8):
                    P.op("pe", lambda e, c=c, k=k, pgb=pgb: e.matmul(
                        pgb, lhsT=wG[:, k, 1024 + c * 128:1024 + (c + 1) * 128], rhs=hT[:, k, :], start=(k == 0), stop=(k == 7)),
                        reads=["wG", "hT"], writes=[PSB[par * 2 + 1]])
                for k in range(4):
                    P.op("pe", lambda e, c=c, k=k, pya=pya: e.matmul(
                        pya, lhsT=wfo[:, k, c * 128:(c + 1) * 128], rhs=FTb[:, k, :], start=(k == 0), stop=(k == 3)),
                        reads=["wfo", "FTb"], writes=[PSB[4 + par * 2]])
                for k in range(4):
                    P.op("pe", lambda e, c=c, k=k, pyb=pyb: e.matmul(
                        pyb, lhsT=wmo[:, k, c * 128:(c + 1) * 128], rhs=OTb[:, k, :], start=(k == 0), stop=(k == 3)),
                        reads=["wmo", "OTb"], writes=[PSB[5 + par * 2]])
                P.op("act", lambda e, par=par, pga=pga: e.activation(out=sga[par], in_=pga, func=AF.Sigmoid),
                     reads=[PSB[par * 2]], writes=["sga%d" % par])
                P.op("act", lambda e, par=par, pgb=pgb: e.activation(out=sgb[par], in_=pgb, func=AF.Sigmoid),
                     reads=[PSB[par * 2 + 1]], writes=["sgb%d" % par])
                P.op("dve", lambda e, par=par, pya=pya: e.tensor_tensor(out=u1, in0=pya, in1=sga[par], op=ALU.mult),
                     reads=[PSB[4 + par * 2], "sga%d" % par], writes=["u1"])
                P.op("dve", lambda e, par=par, pyb=pyb: e.tensor_tensor(out=u2, in0=pyb, in1=sgb[par], op=ALU.mult),
                     reads=[PSB[5 + par * 2], "sgb%d" % par], writes=["u2"])
                P.op("pool", lambda e, c=c: e.tensor_tensor(out=mT[:, c, :], in0=u1, in1=u2, op=ALU.add),
                     reads=["u1", "u2"], writes=["mT"])
            for t in range(4):
                b0 = 4 + 2 * (t % 2)
                pd = P.bank(b0, F32, 2)
                for c in range(8):
                    for hf in range(2):
                        P.op("pe", lambda e, t=t, c=c, hf=hf, pd=pd: e.matmul(
                            pd[:, hf * 512:(hf + 1) * 512], lhsT=mT[:, c, t * 128:(t + 1) * 128],
                            rhs=wo[:, c, hf * 512:(hf + 1) * 512], start=(c == 0), stop=(c == 7)),
                            reads=["mT", "wo"], writes=[PSB[b0 + hf]])
                P.op("dve", lambda e, pd=pd: e.tensor_tensor(out=tmp, in0=pd, in1=g2b, op=ALU.mult),
                     reads=[PSB[b0], PSB[b0 + 1], "g2b"], writes=["tmp"])
                P.op("pool", lambda e, t=t: e.tensor_tensor(out=xt[t], in0=xt[t], in1=tmp, op=ALU.add),
                     reads=[xn[t], "tmp"], writes=[xn[t]])
                P.op("sp", lambda e, t=t, r0=r0: e.dma_start(out=x2_d[r0 + t * 128:r0 + (t + 1) * 128, :], in_=xt[t]),
                     reads=[xn[t]], dma=True)

    phases.append(phase_a2)
    phases.append(phase_attn)
    phases.append(phase_fourier)
    phases.append(phase_c1)
    phases.append(lambda: ffn_phase(x2_d, out_d, NOWN // TB, f2g_d, f2u_d, f2d_d, 2, 64 * 128, True))

    for i, ph in enumerate(phases):
        if i > stop_after:
            break
        ph()
        P.barrier()
        P.release()
    P.emit()
    return nc


def _pp(v, n):
    return np.ascontiguousarray(v.reshape(n, 128).T).astype(np.float32)


def _consts(p):
    t = {}
    c = np.arange(128, dtype=np.float64)
    ang = 2 * np.pi * np.outer(c, c) / 128.0
    t["t_fc"] = np.concatenate([np.cos(ang), -np.sin(ang)], axis=1).astype(np.float32) / 1024.0
    j = np.arange(128)
    s1 = np.where(j < 64, 2 * j + p, 2 * (j - 64) + (1 - p)).astype(np.float64)
    k1 = (64 * p + np.arange(64)).astype(np.float64)
    a = 2 * np.pi * np.outer(s1, k1) / 128.0
    C, Sn = np.cos(a), np.sin(a)
    t["t_f128r"] = np.concatenate([C, -Sn], axis=1).astype(np.float32)
    t["t_f128i"] = np.concatenate([Sn, C], axis=1).astype(np.float32)
    s2 = np.arange(64, dtype=np.float64)
    k = (64 * p + np.arange(64)[None, :] + 128 * np.arange(64)[:, None]).reshape(-1).astype(np.float64)
    a = 2 * np.pi * np.outer(s2, k) / 8192.0
    t["t_cos"] = np.cos(a).astype(np.float32)
    t["t_sin"] = np.sin(a).astype(np.float32)
    half = 16
    inv = (1.0 / (np.float32(10000.0) ** (np.arange(half, dtype=np.float32) * np.float32(2.0) / np.float32(32)))).astype(np.float32)
    r = np.zeros((32, 4), np.float32)
    r[:, 0] = np.concatenate([inv, inv])
    r[:16, 1] = -2.0 * np.pi
    r[16:, 1] = 2.0 * np.pi
    t["t_rope"] = r
    return t


def _own_perm(p):
    k2 = np.arange(64)[:, None]
    own = (128 * k2 + 64 * p + np.arange(64)[None, :]).reshape(-1)
    oth = (128 * k2 + 64 * (1 - p) + np.arange(64)[None, :]).reshape(-1)
    return own, oth


def prep_inputs(x, c, positions, ada_w, ada_b, ffn1_norm, ffn1_w_gate, ffn1_w_up, ffn1_w_down,
                mix_norm, w_in, q_norm, w_q_up, kv_norm, w_kv_up, w_fourier_out, w_mla_out, w_out,
                ffn2_norm, ffn2_w_gate, ffn2_w_up, ffn2_w_down, final_norm, cores=range(8)):
    f = lambda a: np.ascontiguousarray(np.asarray(a))
    x, c, positions = f(x), f(c), f(positions)
    w_in0 = f(w_in)[0]
    swap = np.concatenate([np.arange(16, 32), np.arange(0, 16)])
    kr = w_in0[:, 1152:1184]
    winA = np.zeros((1024, 1280), np.float32)
    winA[:, 0:1184] = w_in0[:, 0:1184]
    winA[:, 1184:1216] = kr[:, swap]
    wq0 = f(w_q_up)[0].reshape(384, 8, 96)
    wq = np.zeros((384, 8, 128), np.float32)
    wq[:, :, 0:64] = wq0[:, :, 0:64]
    wq[:, :, 64:96] = wq0[:, :, 64:96]
    wq[:, :, 96:128] = wq0[:, :, 64:96][:, :, swap]
    wkv0 = f(w_kv_up)[0].reshape(256, 8, 128)
    shared = {
        "ada_w": f(ada_w)[0], "ada_b_pp": _pp(f(ada_b)[0], 72),
        "norms_pp": np.concatenate([_pp(f(ffn1_norm)[0], 8), _pp(f(mix_norm)[0], 8), _pp(f(ffn2_norm)[0], 8)], axis=1),
        "final_norm": f(final_norm).reshape(1, 1024),
        "f1_wg": f(ffn1_w_gate)[0], "f1_wu": f(ffn1_w_up)[0], "f1_wd": f(ffn1_w_down)[0],
        "f2_wg": f(ffn2_w_gate)[0], "f2_wu": f(ffn2_w_up)[0], "f2_wd": f(ffn2_w_down)[0],
        "w_inA": winA, "w_inG": np.ascontiguousarray(w_in0[:, 1184:3232]),
        "qkn_pp": np.concatenate([_pp(f(q_norm)[0], 3), _pp(f(kv_norm)[0], 2)], axis=1),
        "w_q": np.ascontiguousarray(wq.reshape(384, 1024)),
        "w_kn": np.ascontiguousarray(wkv0[:, :, 0:64].reshape(256, 512)),
        "w_v": np.ascontiguousarray(wkv0[:, :, 64:128].reshape(256, 512)),
        "w_fo": f(w_fourier_out)[0], "w_mo": f(w_mla_out)[0], "w_out": f(w_out)[0],
    }
    maps = []
    for core in cores:
        b, p = core // 2, core % 2
        own, oth = _own_perm(p)
        perm = np.concatenate([own, oth])
        m = dict(shared)
        m["x"] = np.ascontiguousarray(x[b][perm])
        m["pos"] = np.ascontiguousarray(positions[b][perm]).reshape(1, 8192).astype(np.int32)
        m["c_pp"] = _pp(c[b], 8)
        m.update(_consts(p))
        maps.append(m)
    return maps


_NC_CACHE = {}


def kernel(**inputs):
    if "nc" not in _NC_CACHE:
        _NC_CACHE["nc"] = build()
    nc = _NC_CACHE["nc"]
    maps = prep_inputs(**inputs)
    res = run_bass_kernel_spmd(nc, maps, core_ids=list(range(8)))
    out = np.zeros((4, 8192, 1024), np.float32)
    for core in range(8):
        b, p = core // 2, core % 2
        own, _ = _own_perm(p)
        out[b][own] = res.results[core]["out"]
    return out
```

```python
import ml_dtypes
from concourse.bass_utils import run_bass_kernel_spmd
import numpy as np
import concourse.bass as bass
import concourse.mybir as mybir

F32 = mybir.dt.float32
BF16 = mybir.dt.bfloat16
I32 = mybir.dt.int32
U8 = mybir.dt.uint8
AF = mybir.ActivationFunctionType
ALU = mybir.AluOpType
AX = mybir.AxisListType
DTSIZE = {F32: 4, BF16: 2, I32: 4, U8: 1}


class Buf:
    __slots__ = ("name", "w", "rs", "rd")

    def __init__(self, name):
        self.name = name
        self.w = None
        self.rs = {}
        self.rd = []


class Op:
    __slots__ = ("eng", "fn", "seq", "signal", "is_dma", "dsem", "dval", "waits",
                 "cnt", "phase", "edeps", "ddeps")


class Prog:
    ENGS = ("pe", "act", "dve", "pool", "sp")
    KDMA = 12

    def __init__(self, nc):
        self.nc = nc
        self.q = {e: [] for e in self.ENGS}
        self.phase = 0
        self.esem = {}
        for e in ("pe", "act", "dve", "pool"):
            self.esem[e] = nc.alloc_semaphore("s_" + e)
        self.bar_sem = nc.alloc_semaphore("s_bar")
        self.bar_cnt = 0
        self.dsems = {}
        self.dcount = {}
        self.dma_ops = {}
        for e in ("sp", "act", "pool"):
            self.dsems[e] = [nc.alloc_semaphore("d_%s_%d" % (e, i)) for i in range(self.KDMA)]
            self.dcount[e] = 0
            self.dma_ops[e] = []
        self.waited = {}
        self.dwaited = {}
        self.bar_wait_pending = {e: 0 for e in self.ENGS}
        self.arena = None
        self.sb_off = 0
        self.sb_mark = 0
        self.sb_cap = 0
        self.bufs = {}

    def init_mem(self, sbuf_bytes=206 * 1024):
        nc = self.nc
        self.arena = nc.alloc_sbuf_tensor("arena", [128, sbuf_bytes], U8)
        self.sb_cap = sbuf_bytes
        self.psum = nc.alloc_psum_tensor("psum", [128, 4096], F32)

    def sb(self, shape, dtype, name=None):
        n = 1
        for s in shape[1:]:
            n *= s
        nbytes = n * DTSIZE[dtype]
        off = (self.sb_off + 63) // 64 * 64
        assert off + nbytes <= self.sb_cap, "SBUF overflow: %s need %d at %d" % (name, nbytes, off)
        self.sb_off = off + nbytes
        ap = self.arena[0:shape[0], off:off + nbytes].bitcast(dtype)
        if len(shape) > 2:
            names = " ".join("d%d" % i for i in range(1, len(shape)))
            kw = {"d%d" % i: shape[i] for i in range(1, len(shape))}
            ap = ap.rearrange("p (%s) -> p %s" % (names, names), **kw)
        return ap

    def mark(self):
        self.sb_mark = self.sb_off

    def release(self):
        self.sb_off = self.sb_mark

    def bank(self, b, dtype=F32, nb=1):
        ap = self.psum[:, b * 512:(b + nb) * 512]
        if dtype != F32:
            ap = ap.bitcast(dtype)
        return ap

    def buf(self, name):
        b = self.bufs.get(name)
        if b is None:
            b = Buf(name)
            self.bufs[name] = b
        return b

    def _mk(self, eng, fn, dma):
        op = Op()
        op.eng = eng
        op.fn = fn
        op.seq = len(self.q[eng])
        op.signal = False
        op.is_dma = dma
        op.dsem = None
        op.dval = 0
        op.waits = []
        op.cnt = 0
        op.phase = self.phase
        op.edeps = {}
        op.ddeps = []
        return op

    def op(self, eng, fn, reads=(), writes=(), dma=False):
        op = self._mk(eng, fn, dma)
        deps = []
        for b in reads:
            if isinstance(b, str):
                b = self.buf(b)
            if b.w is not None:
                deps.append((b.w, True))
        for b in writes:
            if isinstance(b, str):
                b = self.buf(b)
            if b.w is not None:
                deps.append((b.w, True))
            for r in b.rs.values():
                deps.append((r, False))
            for r in b.rd:
                deps.append((r, False))
        best = {}
        for d, strong in deps:
            if d is op or d.phase != self.phase:
                continue
            if d.is_dma:
                key = (eng, id(d))
                if key in self.dwaited:
                    continue
                self.dwaited[key] = True
                op.ddeps.append(d)
            else:
                if d.eng == eng and eng == "pe":
                    continue
                b = best.get(d.eng)
                if b is None or b.seq < d.seq:
                    best[d.eng] = d
        for te, d in best.items():
            key = (eng, te)
            if self.waited.get(key, -1) >= d.seq:
                continue
            self.waited[key] = d.seq
            d.signal = True
            op.edeps[te] = d
        if dma:
            i = self.dcount[eng]
            self.dcount[eng] = i + 1
            K = self.KDMA
            op.dsem = self.dsems[eng][i % K]
            op.dval = 16 * (i // K + 1)
            if i >= K:
                old = self.dma_ops[eng][i - K]
                key = (eng, id(old))
                if key not in self.dwaited:
                    self.dwaited[key] = True
                    op.ddeps.append(old)
            self.dma_ops[eng].append(op)
        if self.bar_wait_pending[eng]:
            op.waits.append((self.bar_sem, self.bar_wait_pending[eng]))
            self.bar_wait_pending[eng] = 0
        for b in writes:
            if isinstance(b, str):
                b = self.buf(b)
            b.w = op
            b.rs = {}
            b.rd = []
        for b in reads:
            if isinstance(b, str):
                b = self.buf(b)
            if b.w is not op:
                if dma:
                    b.rd.append(op)
                else:
                    b.rs[eng] = op
        self.q[eng].append(op)
        return op

    def barrier(self):
        sp_op = self._mk("sp", None, False)
        for e in ("pe", "act", "dve", "pool"):
            if self.q[e]:
                last = self.q[e][-1]
                if last.is_dma:
                    for o in reversed(self.q[e]):
                        if not o.is_dma:
                            last = o
                            break
                if not last.is_dma:
                    last.signal = True
                    sp_op.edeps[e] = last
        for e in ("sp", "act", "pool"):
            n = self.dcount[e]
            for o in self.dma_ops[e][max(0, n - self.KDMA):]:
                sp_op.ddeps.append(o)
        self.bar_cnt += 1
        bc = self.bar_cnt
        bs = self.bar_sem
        sp_op.fn = lambda eng: eng.sem_inc(bs, 1)
        if self.bar_wait_pending["sp"]:
            sp_op.waits.append((self.bar_sem, self.bar_wait_pending["sp"]))
            self.bar_wait_pending["sp"] = 0
        self.q["sp"].append(sp_op)
        for e in ("pe", "act", "dve", "pool"):
            self.bar_wait_pending[e] = bc
        self.phase += 1
        for b in self.bufs.values():
            b.w = None
            b.rs = {}
            b.rd = []

    def emit(self):
        nc = self.nc
        for e in ("pe", "act", "dve", "pool"):
            c = 0
            for o in self.q[e]:
                if o.signal:
                    c += 1
                    o.cnt = c
        esem = self.esem

        def run(eng_name, eng):
            for o in self.q[eng_name]:
                for (s, v) in o.waits:
                    eng.wait_ge(s, v)
                for te, d in o.edeps.items():
                    eng.wait_ge(esem[te], d.cnt)
                for d in o.ddeps:
                    eng.wait_ge(d.dsem, d.dval)
                inst = o.fn(eng)
                if o.signal:
                    inst.then_inc(esem[eng_name], 1)
                if o.is_dma:
                    inst.then_inc(o.dsem, 16)

        with nc.Block() as block:
            @block.tensor
            def _(e):
                run("pe", e)

            @block.scalar
            def _(e):
                run("act", e)

            @block.vector
            def _(e):
                run("dve", e)

            @block.gpsimd
            def _(e):
                run("pool", e)

            @block.sync
            def _(e):
                run("sp", e)

D = 1024
DFF = 2816
NJ = DFF // 128
S = 8192
NOWN = 4096
TB = 512
EPS = 1e-6
NH = 8
SM_SCALE = 96.0 ** -0.5


def build(stop_after=99, dbg=()):
    nc = bass.Bass("TRN2", target_bir_lowering=False)
    P = Prog(nc)
    P.init_mem()

    def din(name, shape, dt=F32):
        return nc.dram_tensor(name, list(shape), dt, kind="ExternalInput").ap()

    def dscr(name, shape, dt):
        kind = "ExternalOutput" if name in dbg else "Internal"
        return nc.dram_tensor(name, list(shape), dt, kind=kind).ap()

    x_d = din("x", [S, D])
    pos_d = din("pos", [1, S], I32)
    cpp_d = din("c_pp", [128, 8])
    adaw_d = din("ada_w", [D, 9 * D])
    adab_d = din("ada_b_pp", [128, 72])
    norms_d = din("norms_pp", [128, 24])
    fnorm_d = din("final_norm", [1, D])
    f1g_d = din("f1_wg", [D, DFF])
    f1u_d = din("f1_wu", [D, DFF])
    f1d_d = din("f1_wd", [DFF, D])
    f2g_d = din("f2_wg", [D, DFF])
    f2u_d = din("f2_wu", [D, DFF])
    f2d_d = din("f2_wd", [DFF, D])
    winA_d = din("w_inA", [D, 1280])
    winG_d = din("w_inG", [D, 2048])
    qkn_d = din("qkn_pp", [128, 5])
    wq_d = din("w_q", [384, 1024])
    wkn_d = din("w_kn", [256, 512])
    wv_d = din("w_v", [256, 512])
    wfo_d = din("w_fo", [512, D])
    wmo_d = din("w_mo", [512, D])
    wout_d = din("w_out", [D, D])
    fc_d = din("t_fc", [128, 256])
    f128r_d = din("t_f128r", [128, 128])
    f128i_d = din("t_f128i", [128, 128])
    tcos_d = din("t_cos", [64, 4096])
    tsin_d = din("t_sin", [64, 4096])
    ropec_d = din("t_rope", [32, 4])

    out_d = nc.dram_tensor("out", [NOWN, D], F32, kind="ExternalOutput").ap()
    mods_d = dscr("mods", [1, 9 * D], F32)
    x1_d = dscr("x1s", [S, D], F32)
    zu_d = dscr("zu", [512, S], BF16)
    zl_d = dscr("zl", [768, S], F32)
    ft_d = dscr("fts", [512, NOWN], BF16)
    ot_d = dscr("ots", [512, NOWN], BF16)
    x2_d = dscr("x2s", [NOWN, D], F32)

    ident_f = P.sb([128, 128], F32)
    ident_b = P.sb([128, 128], BF16)
    ones_b = P.sb([128, 128], BF16)
    epsb = P.sb([128, 1], F32)
    modpp = P.sb([128, 72], F32)
    gv = P.sb([128, 24], F32)
    normspp = P.sb([128, 24], F32)
    P.op("pool", lambda e: e.memset(ident_f, 0.0), writes=["ident_f"])
    P.op("pool", lambda e: e.affine_select(out=ident_f, in_=ident_f, pattern=[[-1, 128]],
                                          compare_op=ALU.not_equal, fill=1.0, base=0,
                                          channel_multiplier=1),
         reads=["ident_f"], writes=["ident_f"])
    P.op("dve", lambda e: e.tensor_copy(out=ident_b, in_=ident_f), reads=["ident_f"], writes=["ident_b"])
    P.op("dve", lambda e: e.memset(ones_b, 1.0), writes=["ones_b"])
    P.op("dve", lambda e: e.memset(epsb, EPS), writes=["epsb"])
    P.mark()

    PSB = ["psb%d" % i for i in range(8)]

    def phase_adaln():
        cpp = P.sb([128, 8], F32)
        cact = P.sb([128, 8], BF16)
        adab = P.sb([128, 72], F32)
        modT = P.sb([128, 128], F32)
        wblk = [P.sb([128, 8, 1024], BF16) for _ in range(2)]
        P.op("sp", lambda e: e.dma_start(out=cpp, in_=cpp_d), writes=["cpp"], dma=True)
        P.op("sp", lambda e: e.dma_start(out=adab, in_=adab_d), writes=["adab"], dma=True)
        P.op("sp", lambda e: e.dma_start(out=normspp, in_=norms_d), writes=["normspp"], dma=True)
        P.op("act", lambda e: e.activation(out=cact, in_=cpp, func=AF.Silu), reads=["cpp"], writes=["cact"])
        ps0 = P.bank(0)
        for blk in range(9):
            wb = wblk[blk % 2]
            wn = "wblk%d" % (blk % 2)
            src = adaw_d[:, blk * 1024:(blk + 1) * 1024].rearrange("(k p) n -> p k n", p=128)
            P.op("pool", lambda e, wb=wb, src=src: e.dma_start(out=wb, in_=src), writes=[wn], dma=True)
            for jj in range(8):
                j = blk * 8 + jj
                for k in range(8):
                    P.op("pe", lambda e, wb=wb, j=j, jj=jj, k=k: e.matmul(
                        ps0[:, j:j + 1], lhsT=wb[:, k, jj * 128:(jj + 1) * 128], rhs=cact[:, k:k + 1],
                        start=(k == 0), stop=(k == 7)),
                        reads=[wn, "cact"], writes=[PSB[0]])
        P.op("dve", lambda e: e.tensor_tensor(out=modpp, in0=ps0[:, 0:72], in1=adab, op=ALU.add),
             reads=[PSB[0], "adab"], writes=["modpp"])
        for i in range(3):
            sc = modpp[:, i * 24 + 8:i * 24 + 16]
            P.op("dve", lambda e, i=i, sc=sc: e.scalar_tensor_tensor(
                out=gv[:, i * 8:(i + 1) * 8], in0=sc, scalar=1.0, in1=normspp[:, i * 8:(i + 1) * 8],
                op0=ALU.add, op1=ALU.mult),
                reads=["modpp", "normspp"], writes=["gv"])
        ps1 = P.bank(1)
        P.op("pe", lambda e: e.transpose(ps1[0:72, 0:128], modpp[:, 0:72], ident_f),
             reads=["modpp", "ident_f"], writes=[PSB[1]])
        P.op("dve", lambda e: e.tensor_copy(out=modT[0:72, :], in_=ps1[0:72, 0:128]),
             reads=[PSB[1]], writes=["modT"])
        P.op("sp", lambda e: e.dma_start(
            out=mods_d.rearrange("o (j f) -> (o j) f", f=128), in_=modT[0:72, :]),
            reads=["modT"], dma=True)

    def load_bcast(dst, name, off):
        P.op("sp", lambda e: e.dma_start(out=dst, in_=mods_d[0:1, off:off + D].broadcast_to([128, D])),
             writes=[name], dma=True)

    def norm_to_hT(xt, xnames, ni, hT, hname, scr, sfx=""):
        ss, rstd, xs, junk = scr["ss"], scr["rstd"], scr["xs"], scr["junk"]
        SSN = ["ss0" + sfx, "ss1" + sfx, "ss2" + sfx, "ss3" + sfx]
        RS, JK = "rstd" + sfx, "junk" + sfx
        XS = ["xs0" + sfx, "xs1" + sfx]
        P.op("dve", lambda e: e.memset(ss, 0.0), writes=SSN)
        for t in range(4):
            P.op("act", lambda e, t=t: e.activation(out=junk, in_=xt[t], func=AF.Square,
                                                    accum_out=ss[:, t:t + 1]),
                 reads=[xnames[t]], writes=[SSN[t], JK])
        P.op("act", lambda e: e.activation(out=rstd, in_=ss, func=AF.Sqrt, scale=1.0 / D, bias=epsb),
             reads=SSN + ["epsb"], writes=[RS])
        P.op("dve", lambda e: e.reciprocal(out=rstd, in_=rstd), reads=[RS], writes=[RS])
        for t in range(4):
            P.op("act", lambda e, t=t: e.activation(out=xs[t % 2], in_=xt[t], func=AF.Copy,
                                                    scale=rstd[:, t:t + 1]),
                 reads=[xnames[t], RS], writes=[XS[t % 2]])
            for c in range(8):
                pb = P.bank(4 + c // 2, BF16)
                P.op("pe", lambda e, t=t, c=c, pb=pb: e.transpose(
                    pb[:, (c % 2) * 512 + t * 128:(c % 2) * 512 + (t + 1) * 128],
                    xs[t % 2][:, c * 128:(c + 1) * 128], ident_b),
                    reads=[XS[t % 2], "ident_b"], writes=[PSB[4 + c // 2]])
        for c in range(8):
            pb = P.bank(4 + c // 2, BF16)
            P.op("dve", lambda e, c=c, pb=pb: e.tensor_scalar(
                out=hT[:, c, :], in0=pb[:, (c % 2) * 512:(c % 2 + 1) * 512],
                scalar1=gv[:, ni * 8 + c:ni * 8 + c + 1],
                scalar2=modpp[:, ni * 24 + c:ni * 24 + c + 1],
                op0=ALU.mult, op1=ALU.add),
                reads=[PSB[4 + c // 2], "gv", "modpp"], writes=[hname])

    def norm_scratch():
        return {"ss": P.sb([128, 4], F32), "rstd": P.sb([128, 4], F32),
                "xs": [P.sb([128, D], BF16) for _ in range(2)],
                "junk": P.sb([128, D], BF16)}

    def ffn_phase(src_d, dst_d, nblk, wg_d, wu_d, wd_d, ni, goff, final):
        wg = P.sb([128, 8, DFF], BF16)
        wu = P.sb([128, 8, DFF], BF16)
        wd = P.sb([128, NJ, D], BF16)
        xt = [P.sb([128, D], F32) for _ in range(4)]
        xn = ["xt%d" % t for t in range(4)]
        scr = norm_scratch()
        hT = P.sb([128, 8, TB], BF16)
        aT = P.sb([128, NJ, TB], BF16)
        sg = [P.sb([128, TB], BF16) for _ in range(2)]
        tmp = P.sb([128, D], F32)
        ghb = P.sb([128, D], F32)
        if final:
            fnb = P.sb([128, D], F32)
            ss2 = P.sb([128, 1], F32)
            P.op("sp", lambda e: e.dma_start(out=fnb, in_=fnorm_d.broadcast_to([128, D])),
                 writes=["fnb"], dma=True)
        load_bcast(ghb, "ghb", goff)
        P.op("dve", lambda e: e.tensor_scalar(out=ghb, in0=ghb, scalar1=0.5, scalar2=None, op0=ALU.mult),
             reads=["ghb"], writes=["ghb"])
        for i in range(NJ // 2):
            cs = slice(i * 256, (i + 1) * 256)
            P.op("pool", lambda e, cs=cs: e.dma_start(
                out=wg[:, :, cs], in_=wg_d[:, cs].rearrange("(k p) n -> p k n", p=128)),
                writes=["wg%d" % i], dma=True)
            P.op("pool", lambda e, cs=cs: e.dma_start(
                out=wu[:, :, cs], in_=wu_d[:, cs].rearrange("(k p) n -> p k n", p=128)),
                writes=["wu%d" % i], dma=True)
        for i in range(NJ // 2):
            P.op("pool", lambda e, i=i: e.dma_start(
                out=wd[:, 2 * i:2 * i + 2, :],
                in_=wd_d[i * 256:(i + 1) * 256, :].rearrange("(j p) n -> p j n", p=128)),
                writes=["wd%d" % i], dma=True)
        for blk in range(nblk):
            r0 = blk * TB
            for t in range(4):
                P.op("sp", lambda e, t=t, r0=r0: e.dma_start(out=xt[t], in_=src_d[r0 + t * 128:r0 + (t + 1) * 128, :]),
                     writes=[xn[t]], dma=True)
            norm_to_hT(xt, xn, ni, hT, "hT", scr)
            for j in range(NJ):
                pg = P.bank(j % 2)
                pu = P.bank(2 + j % 2)
                for k in range(8):
                    P.op("pe", lambda e, j=j, k=k, pg=pg: e.matmul(
                        pg, lhsT=wg[:, k, j * 128:(j + 1) * 128], rhs=hT[:, k, :], start=(k == 0), stop=(k == 7)),
                        reads=["wg%d" % (j // 2), "hT"], writes=[PSB[j % 2]])
                for k in range(8):
                    P.op("pe", lambda e, j=j, k=k, pu=pu: e.matmul(
                        pu, lhsT=wu[:, k, j * 128:(j + 1) * 128], rhs=hT[:, k, :], start=(k == 0), stop=(k == 7)),
                        reads=["wu%d" % (j // 2), "hT"], writes=[PSB[2 + j % 2]])
                P.op("act", lambda e, j=j, pg=pg: e.activation(out=sg[j % 2], in_=pg, func=AF.Silu),
                     reads=[PSB[j % 2]], writes=["sg%d" % (j % 2)])
                P.op("dve", lambda e, j=j, pu=pu: e.tensor_tensor(out=aT[:, j, :], in0=pu, in1=sg[j % 2], op=ALU.mult),
                     reads=[PSB[2 + j % 2], "sg%d" % (j % 2)], writes=["aT"])
            for t in range(4):
                b0 = 4 + 2 * (t % 2)
                pd = P.bank(b0, F32, 2)
                for j in range(NJ):
                    for hf in range(2):
                        P.op("pe", lambda e, t=t, j=j, hf=hf, pd=pd: e.matmul(
                            pd[:, hf * 512:(hf + 1) * 512], lhsT=aT[:, j, t * 128:(t + 1) * 128],
                            rhs=wd[:, j, hf * 512:(hf + 1) * 512], start=(j == 0), stop=(j == NJ - 1)),
                            reads=["aT", "wd%d" % (j // 2)], writes=[PSB[b0 + hf]])
                P.op("dve", lambda e, pd=pd: e.tensor_tensor(out=tmp, in0=pd, in1=ghb, op=ALU.mult),
                     reads=[PSB[b0], PSB[b0 + 1], "ghb"], writes=["tmp"])
                P.op("pool", lambda e, t=t: e.tensor_tensor(out=xt[t], in0=xt[t], in1=tmp, op=ALU.add),
                     reads=[xn[t], "tmp"], writes=[xn[t]])
                if final:
                    P.op("dve", lambda e: e.memset(ss2, 0.0), writes=["ss2"])
                    P.op("act", lambda e, t=t: e.activation(out=tmp, in_=xt[t], func=AF.Square, accum_out=ss2),
                         reads=[xn[t], "ss2"], writes=["tmp", "ss2"])
                    P.op("act", lambda e: e.activation(out=ss2, in_=ss2, func=AF.Sqrt, scale=1.0 / D, bias=epsb),
                         reads=["ss2", "epsb"], writes=["ss2"])
                    P.op("dve", lambda e: e.reciprocal(out=ss2, in_=ss2), reads=["ss2"], writes=["ss2"])
                    P.op("dve", lambda e, t=t: e.scalar_tensor_tensor(
                        out=xt[t], in0=xt[t], scalar=ss2, in1=fnb, op0=ALU.mult, op1=ALU.mult),
                        reads=[xn[t], "ss2", "fnb"], writes=[xn[t]])
                P.op("sp", lambda e, t=t, r0=r0: e.dma_start(out=dst_d[r0 + t * 128:r0 + (t + 1) * 128, :], in_=xt[t]),
                     reads=[xn[t]], dma=True)

    phases = []
    phases.append(phase_adaln)
    phases.append(lambda: ffn_phase(x_d, x1_d, S // TB, f1g_d, f1u_d, f1d_d, 0, 16 * 128, False))
    def phase_a2():
        wA = P.sb([128, 8, 1280], BF16)
        sets = []
        for i in range(2):
            sets.append({
                "xt": [P.sb([128, D], F32) for _ in range(4)],
                "xn": ["xt%d_%d" % (t, i) for t in range(4)],
                "scr": norm_scratch(), "hT": P.sb([128, 8, TB], BF16), "hn": "hT_%d" % i,
                "zu": P.sb([128, 4, TB], BF16), "zun": "zu_sb%d" % i,
                "zl": P.sb([128, 6, TB], F32), "zln": "zl_sb%d" % i, "sfx": "_%d" % i})
        for k in range(8):
            P.op("pool", lambda e, k=k: e.dma_start(out=wA[:, k, :], in_=winA_d[k * 128:(k + 1) * 128, :]),
                 writes=["wA"], dma=True)

        def load_norm(blk):
            st = sets[blk % 2]
            r0 = blk * TB
            for t in range(4):
                P.op("sp", lambda e, t=t, r0=r0, st=st: e.dma_start(out=st["xt"][t], in_=x1_d[r0 + t * 128:r0 + (t + 1) * 128, :]),
                     writes=[st["xn"][t]], dma=True)
            norm_to_hT(st["xt"], st["xn"], 1, st["hT"], st["hn"], st["scr"], st["sfx"])

        def project(blk):
            st = sets[blk % 2]
            r0 = blk * TB
            hT, zu_sb, zl_sb = st["hT"], st["zu"], st["zl"]
            for cc in range(10):
                pb = P.bank(cc % 4)
                for k in range(8):
                    P.op("pe", lambda e, cc=cc, k=k, pb=pb, hT=hT: e.matmul(
                        pb, lhsT=wA[:, k, cc * 128:(cc + 1) * 128], rhs=hT[:, k, :], start=(k == 0), stop=(k == 7)),
                        reads=["wA", st["hn"]], writes=[PSB[cc % 4]])
                if cc < 4:
                    P.op("act", lambda e, cc=cc, pb=pb, zu_sb=zu_sb: e.activation(out=zu_sb[:, cc, :], in_=pb, func=AF.Copy),
                         reads=[PSB[cc % 4]], writes=[st["zun"]])
                else:
                    P.op("dve", lambda e, cc=cc, pb=pb, zl_sb=zl_sb: e.tensor_copy(out=zl_sb[:, cc - 4, :], in_=pb),
                         reads=[PSB[cc % 4]], writes=[st["zln"]])
            P.op("sp", lambda e, r0=r0, zu_sb=zu_sb: e.dma_start(
                out=zu_d[:, r0:r0 + TB].rearrange("(c p) n -> p c n", p=128), in_=zu_sb),
                reads=[st["zun"]], dma=True)
            P.op("sp", lambda e, r0=r0, zl_sb=zl_sb: e.dma_start(
                out=zl_d[:, r0:r0 + TB].rearrange("(c p) n -> p c n", p=128), in_=zl_sb),
                reads=[st["zln"]], dma=True)

        nb = S // TB
        load_norm(0)
        for blk in range(nb):
            if blk + 1 < nb:
                load_norm(blk + 1)
            project(blk)

    def phase_attn():
        ropec = P.sb([32, 4], F32)
        qkn = P.sb([128, 5], F32)
        cosT = P.sb([32, S], BF16)
        ssT = P.sb([32, S], BF16)
        kvnT = P.sb([128, 2, S], BF16)
        kropeT = P.sb([32, S], BF16)
        qnT = P.sb([128, 3, NOWN], BF16)
        wkn = P.sb([128, 2, 512], BF16)
        wv = P.sb([128, 2, 512], BF16)
        wq = P.sb([128, 3, 1024], BF16)
        KhT = P.sb([128, S], BF16)
        QhT = P.sb([128, NOWN], BF16)
        Vaug = P.sb([128, 64, 128], BF16)
        PT = [P.sb([128, 1024], BF16) for _ in range(3)]
        posi = P.sb([32, 1024], I32)
        ang = P.sb([32, 1024], F32)
        ang2 = P.sb([32, 1024], F32)
        lat = P.sb([128, 3, TB], F32)
        kra = P.sb([32, TB], F32)
        krb = P.sb([32, TB], F32)
        sq = P.sb([128, 3, TB], BF16)
        rstdk = P.sb([128, TB], F32)
        t1 = P.sb([32, TB], F32)
        t2 = P.sb([32, TB], F32)
        rden = P.sb([64, TB], F32)
        oT = [P.sb([64, TB], BF16) for _ in range(2)]
        P.op("sp", lambda e: e.dma_start(out=ropec, in_=ropec_d), writes=["ropec"], dma=True)
        P.op("sp", lambda e: e.dma_start(out=qkn, in_=qkn_d), writes=["qkn"], dma=True)
        P.op("pool", lambda e: e.dma_start(out=wkn, in_=wkn_d.rearrange("(k p) n -> p k n", p=128)), writes=["wkn"], dma=True)
        P.op("pool", lambda e: e.dma_start(out=wv, in_=wv_d.rearrange("(k p) n -> p k n", p=128)), writes=["wv"], dma=True)
        P.op("pool", lambda e: e.dma_start(out=wq, in_=wq_d.rearrange("(k p) n -> p k n", p=128)), writes=["wq"], dma=True)
        PI = float(np.pi)
        for ch in range(S // 1024):
            cs = slice(ch * 1024, (ch + 1) * 1024)
            P.op("sp", lambda e, cs=cs: e.dma_start(out=posi, in_=pos_d[0:1, cs].broadcast_to([32, 1024])),
                 writes=["posi"], dma=True)
            P.op("dve", lambda e: e.tensor_copy(out=ang, in_=posi), reads=["posi"], writes=["ang"])
            P.op("dve", lambda e: e.tensor_scalar(out=ang, in0=ang, scalar1=ropec[:, 0:1], scalar2=None, op0=ALU.mult),
                 reads=["ang", "ropec"], writes=["ang"])
            P.op("dve", lambda e: e.tensor_scalar(out=ang, in0=ang, scalar1=1.0 / (2 * PI), scalar2=None, op0=ALU.mult),
                 reads=["ang"], writes=["ang"])
            for which in range(2):
                if which == 1:
                    P.op("dve", lambda e: e.tensor_scalar(out=ang, in0=ang, scalar1=0.25, scalar2=None, op0=ALU.add),
                         reads=["ang"], writes=["ang"])
                P.op("dve", lambda e: e.tensor_copy(out=posi, in_=ang), reads=["ang"], writes=["posi"])
                P.op("dve", lambda e: e.tensor_copy(out=ang2, in_=posi), reads=["posi"], writes=["ang2"])
                P.op("dve", lambda e: e.tensor_tensor(out=ang2, in0=ang, in1=ang2, op=ALU.subtract),
                     reads=["ang", "ang2"], writes=["ang2"])
                if which == 0:
                    P.op("act", lambda e, cs=cs: e.activation(out=ssT[:, cs], in_=ang2, func=AF.Sin, scale=ropec[:, 1:2]),
                         reads=["ang2", "ropec"], writes=["ssT"])
                else:
                    P.op("act", lambda e, cs=cs: e.activation(out=cosT[:, cs], in_=ang2, func=AF.Sin, scale=2 * PI),
                         reads=["ang2"], writes=["cosT"])

        def latent_norm(row0, nch, nfeat, gcol, dstT, dname, blk):
            r0 = blk * TB
            P.op("sp", lambda e: e.dma_start(
                out=lat[:, 0:nch, :], in_=zl_d[row0:row0 + nch * 128, r0:r0 + TB].rearrange("(c p) n -> p c n", p=128)),
                writes=["lat"], dma=True)
            P.op("dve", lambda e: e.tensor_tensor(out=sq[:, 0:nch, :], in0=lat[:, 0:nch, :], in1=lat[:, 0:nch, :], op=ALU.mult),
                 reads=["lat"], writes=["sq"])
            pb = P.bank(0)
            for c in range(nch):
                P.op("pe", lambda e, c=c: e.matmul(pb, lhsT=ones_b, rhs=sq[:, c, :], start=(c == 0), stop=(c == nch - 1)),
                     reads=["ones_b", "sq"], writes=[PSB[0]])
            P.op("act", lambda e: e.activation(out=rstdk, in_=pb, func=AF.Sqrt, scale=1.0 / nfeat, bias=epsb),
                 reads=[PSB[0], "epsb"], writes=["rstdk"])
            P.op("dve", lambda e: e.reciprocal(out=rstdk, in_=rstdk), reads=["rstdk"], writes=["rstdk"])
            for c in range(nch):
                P.op("dve", lambda e, c=c: e.scalar_tensor_tensor(
                    out=dstT[:, c, r0:r0 + TB], in0=lat[:, c, :], scalar=qkn[:, gcol + c:gcol + c + 1], in1=rstdk,
                    op0=ALU.mult, op1=ALU.mult),
                    reads=["lat", "qkn", "rstdk"], writes=[dname])

        for blk in range(S // TB):
            r0 = blk * TB
            latent_norm(384, 2, 256, 3, kvnT, "kvnT", blk)
            P.op("sp", lambda e, r0=r0: e.dma_start(out=kra, in_=zl_d[640:672, r0:r0 + TB]), writes=["kra"], dma=True)
            P.op("sp", lambda e, r0=r0: e.dma_start(out=krb, in_=zl_d[672:704, r0:r0 + TB]), writes=["krb"], dma=True)
            P.op("dve", lambda e, r0=r0: e.tensor_tensor(out=t1, in0=kra, in1=cosT[:, r0:r0 + TB], op=ALU.mult),
                 reads=["kra", "cosT"], writes=["t1"])
            P.op("dve", lambda e, r0=r0: e.tensor_tensor(out=t2, in0=krb, in1=ssT[:, r0:r0 + TB], op=ALU.mult),
                 reads=["krb", "ssT"], writes=["t2"])
            P.op("dve", lambda e, r0=r0: e.tensor_tensor(out=kropeT[:, r0:r0 + TB], in0=t1, in1=t2, op=ALU.add),
                 reads=["t1", "t2"], writes=["kropeT"])
        for blk in range(NOWN // TB):
            latent_norm(0, 3, 384, 0, qnT, "qnT", blk)
        P.op("dve", lambda e: e.memset(Vaug[:, :, 64:128], 1.0), writes=["Vaug"])
        P.op("pool", lambda e: e.memset(KhT[96:128, :], 0.0), writes=["KhT"])
        P.op("pool", lambda e: e.memset(QhT[96:128, :], 0.0), writes=["QhT"])

        def do_head(h):
            for blk in range(S // TB):
                r0 = blk * TB
                pb = P.bank(blk % 2)
                for c in range(2):
                    P.op("pe", lambda e, c=c, r0=r0, pb=pb: e.matmul(
                        pb[0:64, :], lhsT=wkn[:, c, h * 64:(h + 1) * 64], rhs=kvnT[:, c, r0:r0 + TB],
                        start=(c == 0), stop=(c == 1)),
                        reads=["wkn", "kvnT"], writes=[PSB[blk % 2]])
                P.op("act", lambda e, r0=r0, pb=pb: e.activation(out=KhT[0:64, r0:r0 + TB], in_=pb[0:64, :], func=AF.Copy),
                     reads=[PSB[blk % 2]], writes=["KhT"])
            P.op("dve", lambda e: e.tensor_copy(out=KhT[64:96, :], in_=kropeT), reads=["kropeT"], writes=["KhT"])
            for g8 in range(8):
                pb = P.bank(2 + g8 % 2)
                for i in range(8):
                    tt = g8 * 8 + i
                    for c in range(2):
                        P.op("pe", lambda e, c=c, tt=tt, i=i, pb=pb: e.matmul(
                            pb[:, i * 64:(i + 1) * 64], lhsT=kvnT[:, c, tt * 128:(tt + 1) * 128],
                            rhs=wv[:, c, h * 64:(h + 1) * 64], start=(c == 0), stop=(c == 1)),
                            reads=["kvnT", "wv"], writes=[PSB[2 + g8 % 2]])
                P.op("dve", lambda e, g8=g8, pb=pb: e.tensor_copy(
                    out=Vaug[:, g8 * 8:(g8 + 1) * 8, 0:64], in_=pb.rearrange("p (a d) -> p a d", a=8)),
                    reads=[PSB[2 + g8 % 2]], writes=["Vaug"])
            for blk in range(NOWN // TB):
                r0 = blk * TB
                pq = P.bank(4)
                pr = P.bank(5)
                pw = P.bank(6)
                for c in range(3):
                    P.op("pe", lambda e, c=c, r0=r0: e.matmul(
                        pq[0:64, :], lhsT=wq[:, c, h * 128:h * 128 + 64], rhs=qnT[:, c, r0:r0 + TB],
                        start=(c == 0), stop=(c == 2)),
                        reads=["wq", "qnT"], writes=[PSB[4]])
                for c in range(3):
                    P.op("pe", lambda e, c=c, r0=r0: e.matmul(
                        pr[0:32, :], lhsT=wq[:, c, h * 128 + 64:h * 128 + 96], rhs=qnT[:, c, r0:r0 + TB],
                        start=(c == 0), stop=(c == 2)),
                        reads=["wq", "qnT"], writes=[PSB[5]])
                for c in range(3):
                    P.op("pe", lambda e, c=c, r0=r0: e.matmul(
                        pw[0:32, :], lhsT=wq[:, c, h * 128 + 96:h * 128 + 128], rhs=qnT[:, c, r0:r0 + TB],
                        start=(c == 0), stop=(c == 2)),
                        reads=["wq", "qnT"], writes=[PSB[6]])
                P.op("act", lambda e, r0=r0: e.activation(out=QhT[0:64, r0:r0 + TB], in_=pq[0:64, :], func=AF.Copy, scale=SM_SCALE),
                     reads=[PSB[4]], writes=["QhT"])
                P.op("dve", lambda e, r0=r0: e.scalar_tensor_tensor(
                    out=t1, in0=pr[0:32, :], scalar=SM_SCALE, in1=cosT[:, r0:r0 + TB], op0=ALU.mult, op1=ALU.mult),
                    reads=[PSB[5], "cosT"], writes=["t1"])
                P.op("dve", lambda e, r0=r0: e.scalar_tensor_tensor(
                    out=t2, in0=pw[0:32, :], scalar=SM_SCALE, in1=ssT[:, r0:r0 + TB], op0=ALU.mult, op1=ALU.mult),
                    reads=[PSB[6], "ssT"], writes=["t2"])
                P.op("dve", lambda e, r0=r0: e.tensor_tensor(out=QhT[64:96, r0:r0 + TB], in0=t1, in1=t2, op=ALU.add),
                     reads=["t1", "t2"], writes=["QhT"])
            seq = [(qb, kp) for qb in range(NOWN // TB) for kp in range(32)]

            def emit_qk(n):
                qb, kp = seq[n]
                q0 = qb * TB
                b0 = 2 * (n % 3)
                ps = P.bank(b0, F32, 2)
                for i in range(2):
                    kt = kp * 2 + i
                    P.op("pe", lambda e, kt=kt, i=i, ps=ps, q0=q0: e.matmul(
                        ps[:, i * 512:(i + 1) * 512], lhsT=KhT[:, kt * 128:(kt + 1) * 128],
                        rhs=QhT[:, q0:q0 + TB], start=True, stop=True),
                        reads=["KhT", "QhT"], writes=[PSB[b0 + i]])

            def emit_rest(n):
                qb, kp = seq[n]
                q0 = qb * TB
                b0 = 2 * (n % 3)
                ps = P.bank(b0, F32, 2)
                ptb = PT[n % 3]
                ptn = "PT%d" % (n % 3)
                po = P.bank(6 + qb % 2)
                P.op("act", lambda e, ps=ps, ptb=ptb: e.activation(out=ptb, in_=ps, func=AF.Exp),
                     reads=[PSB[b0], PSB[b0 + 1]], writes=[ptn])
                for i in range(2):
                    kt = kp * 2 + i
                    P.op("pe", lambda e, kt=kt, i=i, ptb=ptb, po=po: e.matmul(
                        po, lhsT=Vaug[:, kt, :], rhs=ptb[:, i * 512:(i + 1) * 512],
                        start=(kt == 0), stop=(kt == 63)),
                        reads=["Vaug", ptn], writes=[PSB[6 + qb % 2]])
                if kp == 31:
                    P.op("dve", lambda e, po=po: e.reciprocal(out=rden, in_=po[64:128, :]),
                         reads=[PSB[6 + qb % 2]], writes=["rden"])
                    ob = oT[qb % 2]
                    on = "oT%d" % (qb % 2)
                    P.op("dve", lambda e, po=po, ob=ob: e.tensor_tensor(out=ob, in0=po[0:64, :], in1=rden, op=ALU.mult),
                         reads=[PSB[6 + qb % 2], "rden"], writes=[on])
                    P.op("sp", lambda e, ob=ob, q0=q0: e.dma_start(out=ot_d[h * 64:(h + 1) * 64, q0:q0 + TB], in_=ob),
                         reads=[on], dma=True)

            emit_qk(0)
            for n in range(len(seq)):
                if n + 1 < len(seq):
                    emit_qk(n + 1)
                emit_rest(n)

        for h in range(NH):
            do_head(h)

    def phase_fourier():
        fc = P.sb([128, 256], BF16)
        f128r = P.sb([128, 128], BF16)
        f128i = P.sb([128, 128], BF16)
        tcos = P.sb([64, 4096], BF16)
        tsin = P.sb([64, 4096], BF16)
        uT = P.sb([128, S], BF16)
        Z = P.sb([128, 64, 256], BF16)
        W = P.sb([64, 64, 2, 128], BF16)
        FT = P.sb([128, NOWN], BF16)
        for dst, src, nm in ((fc, fc_d, "fc"), (f128r, f128r_d, "f128r"), (f128i, f128i_d, "f128i"),
                             (tcos, tcos_d, "tcos"), (tsin, tsin_d, "tsin")):
            P.op("pool", lambda e, dst=dst, src=src: e.dma_start(out=dst, in_=src), writes=[nm], dma=True)
        uTv = uT.rearrange("p (j s) -> p s j", s=64)
        tcv = tcos.rearrange("p (k2 k1) -> p k1 k2", k1=64)
        tsv = tsin.rearrange("p (k2 k1) -> p k1 k2", k1=64)
        FTv = FT.rearrange("p (k2 k1) -> p k1 k2", k1=64)
        Wv = W.rearrange("p k r c -> p c r k")
        for g in range(4):
            P.op("sp", lambda e, g=g: e.dma_start(out=uT, in_=zu_d[g * 128:(g + 1) * 128, :]), writes=["uT"], dma=True)
            for sp_ in range(32):
                pb = P.bank(sp_ % 2)
                for i in range(2):
                    s2 = sp_ * 2 + i
                    P.op("pe", lambda e, s2=s2, i=i, pb=pb: e.matmul(
                        pb[:, i * 256:(i + 1) * 256], lhsT=uTv[:, s2, :], rhs=fc, start=True, stop=True),
                        reads=["uT", "fc"], writes=[PSB[sp_ % 2]])
                eng = "act" if sp_ % 2 == 0 else "dve"
                if eng == "act":
                    P.op("act", lambda e, sp_=sp_, pb=pb: e.activation(
                        out=Z[:, 2 * sp_:2 * sp_ + 2, :], in_=pb.rearrange("p (a c) -> p a c", a=2), func=AF.Copy),
                        reads=[PSB[sp_ % 2]], writes=["Z"])
                else:
                    P.op("dve", lambda e, sp_=sp_, pb=pb: e.tensor_copy(
                        out=Z[:, 2 * sp_:2 * sp_ + 2, :], in_=pb.rearrange("p (a c) -> p a c", a=2)),
                        reads=[PSB[sp_ % 2]], writes=["Z"])
            for c4 in range(32):
                pb = P.bank(2 + c4 % 2)
                for i in range(4):
                    cp = c4 * 4 + i
                    P.op("pe", lambda e, cp=cp, i=i, pb=pb: e.matmul(
                        pb[0:64, i * 128:(i + 1) * 128], lhsT=Z[:, :, cp], rhs=f128r, start=True, stop=False),
                        reads=["Z", "f128r"], writes=[PSB[2 + c4 % 2]])
                    P.op("pe", lambda e, cp=cp, i=i, pb=pb: e.matmul(
                        pb[0:64, i * 128:(i + 1) * 128], lhsT=Z[:, :, 128 + cp], rhs=f128i, start=False, stop=True),
                        reads=["Z", "f128i"], writes=[PSB[2 + c4 % 2]])
                src = pb[0:64, :].rearrange("p (c r k) -> p c r k", c=4, r=2)
                dstv = Wv[:, c4 * 4:(c4 + 1) * 4, :, :]
                if c4 % 2 == 0:
                    P.op("act", lambda e, src=src, dstv=dstv: e.activation(out=dstv, in_=src, func=AF.Copy),
                         reads=[PSB[2 + c4 % 2]], writes=["W"])
                else:
                    P.op("dve", lambda e, src=src, dstv=dstv: e.tensor_copy(out=dstv, in_=src),
                         reads=[PSB[2 + c4 % 2]], writes=["W"])
            for k8 in range(8):
                pb = P.bank(4 + k8 % 2)
                for i in range(8):
                    k1 = k8 * 8 + i
                    P.op("pe", lambda e, k1=k1, i=i, pb=pb: e.matmul(
                        pb[:, i * 64:(i + 1) * 64], lhsT=W[:, k1, 0, :], rhs=tcv[:, k1, :], start=True, stop=False),
                        reads=["W", "tcos"], writes=[PSB[4 + k8 % 2]])
                    P.op("pe", lambda e, k1=k1, i=i, pb=pb: e.matmul(
                        pb[:, i * 64:(i + 1) * 64], lhsT=W[:, k1, 1, :], rhs=tsv[:, k1, :], start=False, stop=True),
                        reads=["W", "tsin"], writes=[PSB[4 + k8 % 2]])
                P.op("dve", lambda e, k8=k8, pb=pb: e.tensor_copy(
                    out=FTv[:, k8 * 8:(k8 + 1) * 8, :], in_=pb.rearrange("p (a k) -> p a k", a=8)),
                    reads=[PSB[4 + k8 % 2]], writes=["FT"])
            P.op("sp", lambda e, g=g: e.dma_start(out=ft_d[g * 128:(g + 1) * 128, :], in_=FT), reads=["FT"], dma=True)

    def phase_c1():
        wG = P.sb([128, 8, 2048], BF16)
        wfo = P.sb([128, 4, D], BF16)
        wmo = P.sb([128, 4, D], BF16)
        wo = P.sb([128, 8, D], BF16)
        sets = []
        for i in range(2):
            sets.append({"xt": [P.sb([128, D], F32) for _ in range(4)], "xn": ["xt%d_%d" % (t, i) for t in range(4)],
                         "scr": norm_scratch(), "hT": P.sb([128, 8, TB], BF16), "hn": "hT_%d" % i,
                         "FTb": P.sb([128, 4, TB], BF16), "fn": "FTb%d" % i,
                         "OTb": P.sb([128, 4, TB], BF16), "on": "OTb%d" % i, "sfx": "_%d" % i})
        mT = P.sb([128, 8, TB], BF16)
        sga = [P.sb([128, TB], F32) for _ in range(2)]
        sgb = [P.sb([128, TB], F32) for _ in range(2)]
        u1 = P.sb([128, TB], F32)
        u2 = P.sb([128, TB], F32)
        tmp = P.sb([128, D], F32)
        g2b = P.sb([128, D], F32)
        load_bcast(g2b, "g2b", 40 * 128)
        for k in range(8):
            P.op("pool", lambda e, k=k: e.dma_start(out=wG[:, k, :], in_=winG_d[k * 128:(k + 1) * 128, :]), writes=["wG"], dma=True)
        P.op("pool", lambda e: e.dma_start(out=wfo, in_=wfo_d.rearrange("(k p) n -> p k n", p=128)), writes=["wfo"], dma=True)
        P.op("pool", lambda e: e.dma_start(out=wmo, in_=wmo_d.rearrange("(k p) n -> p k n", p=128)), writes=["wmo"], dma=True)
        P.op("pool", lambda e: e.dma_start(out=wo, in_=wout_d.rearrange("(k p) n -> p k n", p=128)), writes=["wo"], dma=True)
        def c1_load(blk):
            st = sets[blk % 2]
            r0 = blk * TB
            for t in range(4):
                P.op("sp", lambda e, t=t, r0=r0, st=st: e.dma_start(out=st["xt"][t], in_=x1_d[r0 + t * 128:r0 + (t + 1) * 128, :]),
                     writes=[st["xn"][t]], dma=True)
            P.op("sp", lambda e, r0=r0, st=st: e.dma_start(out=st["FTb"], in_=ft_d[:, r0:r0 + TB].rearrange("(c p) n -> p c n", p=128)),
                 writes=[st["fn"]], dma=True)
            P.op("sp", lambda e, r0=r0, st=st: e.dma_start(out=st["OTb"], in_=ot_d[:, r0:r0 + TB].rearrange("(c p) n -> p c n", p=128)),
                 writes=[st["on"]], dma=True)
            norm_to_hT(st["xt"], st["xn"], 1, st["hT"], st["hn"], st["scr"], st["sfx"])

        nbc = NOWN // TB
        c1_load(0)
        for blk in range(nbc):
            r0 = blk * TB
            if blk + 1 < nbc:
                c1_load(blk + 1)
            st = sets[blk % 2]
            xt, xn, hT, FTb, OTb = st["xt"], st["xn"], st["hT"], st["FTb"], st["OTb"]
            HN, FN, ON = st["hn"], st["fn"], st["on"]
            for c in range(8):
                par = c % 2
                pga = P.bank(par * 2)
                pgb = P.bank(par * 2 + 1)
                pya = P.bank(4 + par * 2)
                pyb = P.bank(5 + par * 2)
                for k in range(8):
                    P.op("pe", lambda e, c=c, k=k, pga=pga, hT=hT: e.matmul(
                        pga, lhsT=wG[:, k, c * 128:(c + 1) * 128], rhs=hT[:, k, :], start=(k == 0), stop=(k == 7)),
                        reads=["wG", HN], writes=[PSB[par * 2]])
                for k in range(8):
                    P.op("pe", lambda e, c=c, k=k, pgb=pgb, hT=hT: e.matmul(
                        pgb, lhsT=wG[:, k, 1024 + c * 128:1024 + (c + 1) * 128], rhs=hT[:, k, :], start=(k == 0), stop=(k == 7)),
                        reads=["wG", HN], writes=[PSB[par * 2 + 1]])
                for k in range(4):
                    P.op("pe", lambda e, c=c, k=k, pya=pya, FTb=FTb: e.matmul(
                        pya, lhsT=wfo[:, k, c * 128:(c + 1) * 128], rhs=FTb[:, k, :], start=(k == 0), stop=(k == 3)),
                        reads=["wfo", FN], writes=[PSB[4 + par * 2]])
                for k in range(4):
                    P.op("pe", lambda e, c=c, k=k, pyb=pyb, OTb=OTb: e.matmul(
                        pyb, lhsT=wmo[:, k, c * 128:(c + 1) * 128], rhs=OTb[:, k, :], start=(k == 0), stop=(k == 3)),
                        reads=["wmo", ON], writes=[PSB[5 + par * 2]])
                P.op("act", lambda e, par=par, pga=pga: e.activation(out=sga[par], in_=pga, func=AF.Sigmoid),
                     reads=[PSB[par * 2]], writes=["sga%d" % par])
                P.op("act", lambda e, par=par, pgb=pgb: e.activation(out=sgb[par], in_=pgb, func=AF.Sigmoid),
                     reads=[PSB[par * 2 + 1]], writes=["sgb%d" % par])
                P.op("dve", lambda e, par=par, pya=pya: e.tensor_tensor(out=u1, in0=pya, in1=sga[par], op=ALU.mult),
                     reads=[PSB[4 + par * 2], "sga%d" % par], writes=["u1"])
                P.op("dve", lambda e, par=par, pyb=pyb: e.tensor_tensor(out=u2, in0=pyb, in1=sgb[par], op=ALU.mult),
                     reads=[PSB[5 + par * 2], "sgb%d" % par], writes=["u2"])
                P.op("pool", lambda e, c=c: e.tensor_tensor(out=mT[:, c, :], in0=u1, in1=u2, op=ALU.add),
                     reads=["u1", "u2"], writes=["mT"])
            for t in range(4):
                b0 = 4 + 2 * (t % 2)
                pd = P.bank(b0, F32, 2)
                for c in range(8):
                    for hf in range(2):
                        P.op("pe", lambda e, t=t, c=c, hf=hf, pd=pd: e.matmul(
                            pd[:, hf * 512:(hf + 1) * 512], lhsT=mT[:, c, t * 128:(t + 1) * 128],
                            rhs=wo[:, c, hf * 512:(hf + 1) * 512], start=(c == 0), stop=(c == 7)),
                            reads=["mT", "wo"], writes=[PSB[b0 + hf]])
                P.op("dve", lambda e, pd=pd: e.tensor_tensor(out=tmp, in0=pd, in1=g2b, op=ALU.mult),
                     reads=[PSB[b0], PSB[b0 + 1], "g2b"], writes=["tmp"])
                P.op("pool", lambda e, t=t, xt=xt: e.tensor_tensor(out=xt[t], in0=xt[t], in1=tmp, op=ALU.add),
                     reads=[xn[t], "tmp"], writes=[xn[t]])
                P.op("sp", lambda e, t=t, r0=r0, xt=xt: e.dma_start(out=x2_d[r0 + t * 128:r0 + (t + 1) * 128, :], in_=xt[t]),
                     reads=[xn[t]], dma=True)

    phases.append(phase_a2)
    phases.append(phase_attn)
    phases.append(phase_fourier)
    phases.append(phase_c1)
    phases.append(lambda: ffn_phase(x2_d, out_d, NOWN // TB, f2g_d, f2u_d, f2d_d, 2, 64 * 128, True))

    for i, ph in enumerate(phases):
        if i > stop_after:
            break
        ph()
        P.barrier()
        P.release()
    P.emit()
    return nc


def _pp(v, n):
    return np.ascontiguousarray(v.reshape(n, 128).T).astype(np.float32)


def _consts(p):
    t = {}
    c = np.arange(128, dtype=np.float64)
    ang = 2 * np.pi * np.outer(c, c) / 128.0
    t["t_fc"] = np.concatenate([np.cos(ang), -np.sin(ang)], axis=1).astype(np.float32) / 1024.0
    j = np.arange(128)
    s1 = np.where(j < 64, 2 * j + p, 2 * (j - 64) + (1 - p)).astype(np.float64)
    k1 = (64 * p + np.arange(64)).astype(np.float64)
    a = 2 * np.pi * np.outer(s1, k1) / 128.0
    C, Sn = np.cos(a), np.sin(a)
    t["t_f128r"] = np.concatenate([C, -Sn], axis=1).astype(np.float32)
    t["t_f128i"] = np.concatenate([Sn, C], axis=1).astype(np.float32)
    s2 = np.arange(64, dtype=np.float64)
    k = (64 * p + np.arange(64)[None, :] + 128 * np.arange(64)[:, None]).reshape(-1).astype(np.float64)
    a = 2 * np.pi * np.outer(s2, k) / 8192.0
    t["t_cos"] = np.cos(a).astype(np.float32)
    t["t_sin"] = np.sin(a).astype(np.float32)
    half = 16
    inv = (1.0 / (np.float32(10000.0) ** (np.arange(half, dtype=np.float32) * np.float32(2.0) / np.float32(32)))).astype(np.float32)
    r = np.zeros((32, 4), np.float32)
    r[:, 0] = np.concatenate([inv, inv])
    r[:16, 1] = -2.0 * np.pi
    r[16:, 1] = 2.0 * np.pi
    t["t_rope"] = r
    return t


def _own_perm(p):
    k2 = np.arange(64)[:, None]
    own = (128 * k2 + 64 * p + np.arange(64)[None, :]).reshape(-1)
    oth = (128 * k2 + 64 * (1 - p) + np.arange(64)[None, :]).reshape(-1)
    return own, oth


def prep_inputs(x, c, positions, ada_w, ada_b, ffn1_norm, ffn1_w_gate, ffn1_w_up, ffn1_w_down,
                mix_norm, w_in, q_norm, w_q_up, kv_norm, w_kv_up, w_fourier_out, w_mla_out, w_out,
                ffn2_norm, ffn2_w_gate, ffn2_w_up, ffn2_w_down, final_norm, cores=range(8)):
    f = lambda a: np.ascontiguousarray(np.asarray(a))
    x, c, positions = f(x), f(c), f(positions)
    w_in0 = f(w_in)[0]
    swap = np.concatenate([np.arange(16, 32), np.arange(0, 16)])
    kr = w_in0[:, 1152:1184]
    winA = np.zeros((1024, 1280), np.float32)
    winA[:, 0:1184] = w_in0[:, 0:1184]
    winA[:, 1184:1216] = kr[:, swap]
    wq0 = f(w_q_up)[0].reshape(384, 8, 96)
    wq = np.zeros((384, 8, 128), np.float32)
    wq[:, :, 0:64] = wq0[:, :, 0:64]
    wq[:, :, 64:96] = wq0[:, :, 64:96]
    wq[:, :, 96:128] = wq0[:, :, 64:96][:, :, swap]
    wkv0 = f(w_kv_up)[0].reshape(256, 8, 128)
    shared = {
        "ada_w": f(ada_w)[0], "ada_b_pp": _pp(f(ada_b)[0], 72),
        "norms_pp": np.concatenate([_pp(f(ffn1_norm)[0], 8), _pp(f(mix_norm)[0], 8), _pp(f(ffn2_norm)[0], 8)], axis=1),
        "final_norm": f(final_norm).reshape(1, 1024),
        "f1_wg": f(ffn1_w_gate)[0], "f1_wu": f(ffn1_w_up)[0], "f1_wd": f(ffn1_w_down)[0],
        "f2_wg": f(ffn2_w_gate)[0], "f2_wu": f(ffn2_w_up)[0], "f2_wd": f(ffn2_w_down)[0],
        "w_inA": winA, "w_inG": np.ascontiguousarray(w_in0[:, 1184:3232]),
        "qkn_pp": np.concatenate([_pp(f(q_norm)[0], 3), _pp(f(kv_norm)[0], 2)], axis=1),
        "w_q": np.ascontiguousarray(wq.reshape(384, 1024)),
        "w_kn": np.ascontiguousarray(wkv0[:, :, 0:64].reshape(256, 512)),
        "w_v": np.ascontiguousarray(wkv0[:, :, 64:128].reshape(256, 512)),
        "w_fo": f(w_fourier_out)[0], "w_mo": f(w_mla_out)[0], "w_out": f(w_out)[0],
    }
    maps = []
    for core in cores:
        b, p = core // 2, core % 2
        own, oth = _own_perm(p)
        perm = np.concatenate([own, oth])
        m = dict(shared)
        m["x"] = np.ascontiguousarray(x[b][perm])
        m["pos"] = np.ascontiguousarray(positions[b][perm]).reshape(1, 8192).astype(np.int32)
        m["c_pp"] = _pp(c[b], 8)
        m.update(_consts(p))
        maps.append(m)
    return maps


_NC_CACHE = {}


def kernel(**inputs):
    if "nc" not in _NC_CACHE:
        _NC_CACHE["nc"] = build()
    nc = _NC_CACHE["nc"]
    maps = prep_inputs(**inputs)
    res = run_bass_kernel_spmd(nc, maps, core_ids=list(range(8)))
    out = np.zeros((4, 8192, 1024), np.float32)
    for core in range(8):
        b, p = core // 2, core % 2
        own, _ = _own_perm(p)
        out[b][own] = res.results[core]["out"]
    return out
```
